# Optimizing a Trainium2 kernel written in Bass

```python
import math
import jax, jax.numpy as jnp
from jax import lax
import numpy as np


D_MODEL = 1024
BATCH = 16
SEQ = 4096
DEPTH = 2
DEC_BATCH = 8
DEC_SEQ = 4096
PAST_LEN = 128

N_MIXERS = 2
N_RET_LAYERS = (DEPTH + 1) // 2
N_ATT_LAYERS = DEPTH // 2
GRID_W = 64
ROPE_THETA = 10000.0
EPS = 1e-6
RET_HEADS = 4
RET_DK = D_MODEL // RET_HEADS
RET_DV = 2 * D_MODEL // RET_HEADS
RET_CHUNK = 128
RET_IN = 2 * RET_HEADS * RET_DK + 2 * RET_HEADS * RET_DV
ATT_HD = 128
ATT_Q_HEADS = D_MODEL // ATT_HD
ATT_KV_HEADS = 2
ATT_GROUP = ATT_Q_HEADS // ATT_KV_HEADS
ATT_IN = (ATT_Q_HEADS + 2 * ATT_KV_HEADS) * ATT_HD
ATT_BLOCK = 128
D_FF = 4 * D_MODEL

kernel_name = 'hybrid_retention_gqa_adaln_encoder'


def rmsnorm(x, g):
    xf = x.astype(jnp.float32)
    y = xf * lax.rsqrt(jnp.mean(xf * xf, axis=-1, keepdims=True) + EPS)
    return (y * g.astype(jnp.float32)).astype(x.dtype)


def axial_rope_tables(n_tokens, head_dim):
    rows = n_tokens // GRID_W
    row = jnp.repeat(jnp.arange(rows, dtype=jnp.float32), GRID_W)
    col = jnp.tile(jnp.arange(GRID_W, dtype=jnp.float32), rows)
    nf = head_dim // 4
    inv = ROPE_THETA ** (-jnp.arange(nf, dtype=jnp.float32) / nf)
    ar = row[:, None] * inv[None, :]
    ac = col[:, None] * inv[None, :]
    return (jnp.cos(ar), jnp.sin(ar), jnp.cos(ac), jnp.sin(ac))


def _rot(seg, cos, sin):
    a, b = jnp.split(seg, 2, axis=-1)
    cos = cos[:, None, :]
    sin = sin[:, None, :]
    return jnp.concatenate([a * cos - b * sin, a * sin + b * cos], axis=-1)


def apply_axial_rope(x, tables):
    cos_r, sin_r, cos_c, sin_c = tables
    xf = x.astype(jnp.float32)
    half = xf.shape[-1] // 2
    return jnp.concatenate([_rot(xf[..., :half], cos_r, sin_r),
                            _rot(xf[..., half:], cos_c, sin_c)], axis=-1)


def adaln(c, w, b):
    mod = jax.nn.silu(c.astype(jnp.float32)) @ w.astype(jnp.float32) + b.astype(jnp.float32)
    return [m[:, None, :] for m in jnp.split(mod, 6, axis=-1)]


def retention(h, w_in, decay_logit, w_out, rope):
    b, n, _ = h.shape
    f32 = jnp.float32
    proj = h @ w_in
    q, k, v, g = jnp.split(proj, [RET_HEADS * RET_DK, 2 * RET_HEADS * RET_DK,
                                  2 * RET_HEADS * RET_DK + RET_HEADS * RET_DV], axis=-1)
    q = apply_axial_rope(q.reshape(b, n, RET_HEADS, RET_DK), rope) * (RET_DK ** -0.5)
    k = apply_axial_rope(k.reshape(b, n, RET_HEADS, RET_DK), rope)
    v = v.reshape(b, n, RET_HEADS, RET_DV).astype(f32)
    nc = n // RET_CHUNK

    def to_chunks(t):
        return t.reshape(b, nc, RET_CHUNK, RET_HEADS, t.shape[-1]).transpose(1, 0, 3, 2, 4)

    qc, kc, vc = to_chunks(q), to_chunks(k), to_chunks(v)
    log_g = -jax.nn.softplus(-decay_logit.astype(f32))
    lf, lb = log_g[0], log_g[1]
    idx = jnp.arange(RET_CHUNK, dtype=f32)
    diff = idx[:, None] - idx[None, :]
    d_fwd = jnp.where(diff >= 0, jnp.exp(lf[:, None, None] * jnp.maximum(diff, 0.0)), 0.0)
    d_bwd = jnp.where(diff < 0, jnp.exp(lb[:, None, None] * jnp.maximum(-diff, 0.0)), 0.0)
    scores = jnp.einsum('nbhjd,nbhld->nbhjl', qc, kc) * (d_fwd + d_bwd)
    inner = jnp.einsum('nbhjl,nbhle->nbhje', scores, vc)

    qdec_f = jnp.exp(lf[:, None] * (idx + 1.0))[:, :, None]
    kdec_f = jnp.exp(lf[:, None] * (RET_CHUNK - 1.0 - idx))[:, :, None]
    cdec_f = jnp.exp(lf * RET_CHUNK)[:, None, None]
    qdec_b = jnp.exp(lb[:, None] * (RET_CHUNK - idx))[:, :, None]
    kdec_b = jnp.exp(lb[:, None] * idx)[:, :, None]
    cdec_b = jnp.exp(lb * RET_CHUNK)[:, None, None]

    def fwd_step(state, xs):
        qi, ki, vi = xs
        out = jnp.einsum('bhjd,bhde->bhje', qi * qdec_f, state)
        state = state * cdec_f + jnp.einsum('bhld,bhle->bhde', ki * kdec_f, vi)
        return state, out

    def bwd_step(state, xs):
        qi, ki, vi = xs
        out = jnp.einsum('bhjd,bhde->bhje', qi * qdec_b, state)
        state = state * cdec_b + jnp.einsum('bhld,bhle->bhde', ki * kdec_b, vi)
        return state, out

    state0 = jnp.zeros((b, RET_HEADS, RET_DK, RET_DV), f32)
    _, cross_f = lax.scan(fwd_step, state0, (qc, kc, vc))
    _, cross_b = lax.scan(bwd_step, state0, (qc, kc, vc), reverse=True)
    y = (inner + cross_f + cross_b).transpose(1, 0, 3, 2, 4).reshape(b, n, RET_HEADS, RET_DV)
    mu = jnp.mean(y, axis=-1, keepdims=True)
    var = jnp.mean(jnp.square(y - mu), axis=-1, keepdims=True)
    y = (y - mu) * lax.rsqrt(var + EPS)
    y = y.reshape(b, n, RET_HEADS * RET_DV) * jax.nn.silu(g.astype(f32))
    return y.astype(h.dtype) @ w_out


def attention(h, w_in, q_gain, k_gain, w_out, rope):
    b, n, _ = h.shape
    proj = h @ w_in
    q, k, v = jnp.split(proj, [ATT_Q_HEADS * ATT_HD, (ATT_Q_HEADS + ATT_KV_HEADS) * ATT_HD], axis=-1)
    q = apply_axial_rope(rmsnorm(q.reshape(b, n, ATT_Q_HEADS, ATT_HD), q_gain), rope) * (ATT_HD ** -0.5)
    k = apply_axial_rope(rmsnorm(k.reshape(b, n, ATT_KV_HEADS, ATT_HD), k_gain), rope)
    v = v.reshape(b, n, ATT_KV_HEADS, ATT_HD)
    qb_all = q.reshape(b, n // ATT_BLOCK, ATT_BLOCK, ATT_KV_HEADS, ATT_GROUP, ATT_HD).swapaxes(0, 1)

    def block(qb):
        s = jnp.einsum('bqhgd,bkhd->bhgqk', qb, k)
        p = jax.nn.softmax(s.astype(jnp.float32), axis=-1)
        return jnp.einsum('bhgqk,bkhd->bqhgd', p.astype(v.dtype), v)

    o = lax.map(block, qb_all)
    o = o.swapaxes(0, 1).reshape(b, n, ATT_Q_HEADS * ATT_HD)
    return o.astype(h.dtype) @ w_out


def mlp(h, w1, w2):
    return jnp.square(jax.nn.relu(h @ w1)) @ w2


def trunk(x, c, mod_w, mod_b, norm1_g, norm2_g, ret_w_in, ret_decay, ret_w_out,
          att_w_in, att_q_gain, att_k_gain, att_w_out, mlp_w1, mlp_w2, final_g):
    n = x.shape[1]
    rope_ret = axial_rope_tables(n, RET_DK)
    rope_att = axial_rope_tables(n, ATT_HD)
    for i in range(DEPTH):
        sh1, sc1, g1, sh2, sc2, g2 = adaln(c, mod_w[i], mod_b[i])
        h = (rmsnorm(x, norm1_g[i]) * (1.0 + sc1) + sh1).astype(x.dtype)
        j = i // N_MIXERS
        if i % N_MIXERS == 0:
            m = retention(h, ret_w_in[j], ret_decay[j], ret_w_out[j], rope_ret)
        else:
            m = attention(h, att_w_in[j], att_q_gain[j], att_k_gain[j], att_w_out[j], rope_att)
        x = (x + g1 * m).astype(x.dtype)
        h = (rmsnorm(x, norm2_g[i]) * (1.0 + sc2) + sh2).astype(x.dtype)
        x = (x + g2 * mlp(h, mlp_w1[i], mlp_w2[i])).astype(x.dtype)
    return rmsnorm(x, final_g)


def setup_inputs(seed: int = 0) -> dict:
    key = jax.random.key(seed)
    ks = jax.random.split(key, 18)
    f32 = jnp.float32

    def nrm(k, shape, fan_in):
        return jax.random.normal(k, shape, f32) * (fan_in ** -0.5)

    gam_f = 1.0 - 2.0 ** (-5.0 - np.arange(RET_HEADS))
    gam = np.stack([gam_f, gam_f[::-1]])
    base = jnp.asarray(np.log(gam / (1.0 - gam)), f32)
    ret_decay = base[None] + 0.1 * jax.random.normal(ks[9], (N_RET_LAYERS, 2, RET_HEADS), f32)
    return {
        'x_prompt': jax.random.normal(ks[0], (BATCH, SEQ, D_MODEL), f32),
        'x_sample': jax.random.normal(ks[1], (DEC_BATCH, DEC_SEQ, D_MODEL), f32),
        'c_prompt': jax.random.normal(ks[2], (BATCH, D_MODEL), f32),
        'c_sample': jax.random.normal(ks[3], (DEC_BATCH, D_MODEL), f32),
        'mod_w': nrm(ks[4], (DEPTH, D_MODEL, 6 * D_MODEL), D_MODEL),
        'mod_b': 0.02 * jax.random.normal(ks[5], (DEPTH, 6 * D_MODEL), f32),
        'norm1_g': 1.0 + 0.05 * jax.random.normal(ks[6], (DEPTH, D_MODEL), f32),
        'norm2_g': 1.0 + 0.05 * jax.random.normal(ks[7], (DEPTH, D_MODEL), f32),
        'ret_w_in': nrm(ks[8], (N_RET_LAYERS, D_MODEL, RET_IN), D_MODEL),
        'ret_decay': ret_decay,
        'ret_w_out': nrm(ks[10], (N_RET_LAYERS, RET_HEADS * RET_DV, D_MODEL), RET_HEADS * RET_DV),
        'att_w_in': nrm(ks[11], (N_ATT_LAYERS, D_MODEL, ATT_IN), D_MODEL),
        'att_q_gain': 1.0 + 0.05 * jax.random.normal(ks[12], (N_ATT_LAYERS, ATT_HD), f32),
        'att_k_gain': 1.0 + 0.05 * jax.random.normal(ks[13], (N_ATT_LAYERS, ATT_HD), f32),
        'att_w_out': nrm(ks[14], (N_ATT_LAYERS, ATT_Q_HEADS * ATT_HD, D_MODEL), ATT_Q_HEADS * ATT_HD),
        'mlp_w1': nrm(ks[15], (DEPTH, D_MODEL, D_FF), D_MODEL),
        'mlp_w2': nrm(ks[16], (DEPTH, D_FF, D_MODEL), D_FF),
        'final_g': 1.0 + 0.05 * jax.random.normal(ks[17], (D_MODEL,), f32),
    }


def reference(x_prompt, x_sample, c_prompt, c_sample, mod_w, mod_b, norm1_g, norm2_g,
              ret_w_in, ret_decay, ret_w_out, att_w_in, att_q_gain, att_k_gain, att_w_out,
              mlp_w1, mlp_w2, final_g):
    y_prompt = trunk(x_prompt, c_prompt, mod_w, mod_b, norm1_g, norm2_g, ret_w_in, ret_decay,
                     ret_w_out, att_w_in, att_q_gain, att_k_gain, att_w_out, mlp_w1, mlp_w2, final_g)
    y_sample = trunk(x_sample, c_sample, mod_w, mod_b, norm1_g, norm2_g, ret_w_in, ret_decay,
                     ret_w_out, att_w_in, att_q_gain, att_k_gain, att_w_out, mlp_w1, mlp_w2, final_g)
    return (y_prompt, y_sample)
```

```python
import math
import numpy as np
import concourse.bass as bass
import concourse.mybir as mybir
from concourse.bass_utils import run_bass_kernel_spmd
from concourse.alu_op_type import AluOpType as ALU

AF = mybir.ActivationFunctionType
F32 = mybir.dt.float32
BF16 = mybir.dt.bfloat16
AX = mybir.AxisListType

D = 1024
DFF = 4096
EPS = 1e-6
RET_H = 4
RET_DK = 256
RET_DV = 512
RET_IN = 6144
ATT_HD = 128
ATT_QH = 8
ATT_KVH = 2
ATT_IN = 1536
GRID_W = 64
ROPE_THETA = 10000.0


class V:
    __slots__ = ("ap", "key")

    def __init__(self, ap, key):
        self.ap = ap
        self.key = key

    def __getitem__(self, idx):
        return V(self.ap[idx], self.key)

    def sub(self, s):
        return V(self.ap, (self.key, s))

    def re(self, pat, **kw):
        return V(self.ap.rearrange(pat, **kw), self.key)

    def bc(self, shape):
        return V(self.ap.broadcast_to(shape), self.key)


def _keys(*xs):
    out = []
    for x in xs:
        if isinstance(x, V) and x.key is not None:
            out.append(x.key)
    return out


def _a(x):
    return x.ap if isinstance(x, V) else x


class Op:
    __slots__ = ("eng", "fn", "reads", "writes", "dma", "signal", "sem", "count", "deps", "pre")

    def __init__(self, eng, fn, reads, writes, dma):
        self.eng = eng
        self.fn = fn
        self.reads = reads
        self.writes = writes
        self.dma = dma
        self.signal = dma
        self.sem = None
        self.count = 0
        self.deps = ()
        self.pre = None


ENGS = ("pe", "act", "dve", "pool", "sp")


class Kern:
    def __init__(self, nc, ndma=32):
        self.nc = nc
        self.ndma = ndma
        self.dma_sems = [nc.alloc_semaphore(name=f"dq{i}") for i in range(ndma)]
        self.dma_n = 0
        self.dma_cnt = [0] * ndma
        self.phase_i = 0
        self.waited = {e: {} for e in ENGS}
        self.n_inst = 0

    def eng(self, name):
        nc = self.nc
        return {"pe": nc.tensor, "act": nc.scalar, "dve": nc.vector, "pool": nc.gpsimd, "sp": nc.sync}[name]


class Phase:
    def __init__(self, K, name):
        self.K = K
        self.name = name
        self.ops = []
        self._uid = 0

    def uid(self):
        self._uid += 1
        return self._uid

    def add(self, eng, fn, reads=(), writes=(), dma=False):
        self.ops.append(Op(eng, fn, tuple(reads), tuple(writes), dma))

    def mm(self, out, lhsT, rhs, start=True, stop=True):
        self.add("pe", lambda e: e.matmul(out.ap, lhsT.ap, rhs.ap, start=start, stop=stop),
                 _keys(lhsT, rhs), _keys(out))

    def tr(self, out, in_, ident):
        self.add("pe", lambda e: e.transpose(out.ap, in_.ap, ident.ap), _keys(in_, ident), _keys(out))

    def act(self, out, in_, func, bias=None, scale=None, accum=None, eng="act"):
        kw = {}
        if bias is not None:
            kw["bias"] = _a(bias)
        if scale is not None:
            kw["scale"] = _a(scale)
        if accum is not None:
            kw["accum_out"] = accum.ap
        self.add(eng, lambda e: e.activation(out.ap, in_.ap, func, **kw),
                 _keys(in_, bias, scale), _keys(out, accum))

    def ts(self, out, in0, s1, s2, op0, op1=None, eng="dve", accum=None):
        kw = {}
        if op1 is not None:
            kw["op1"] = op1
        if accum is not None:
            kw["accum_out"] = accum.ap
        self.add(eng, lambda e: e.tensor_scalar(out.ap, in0.ap, _a(s1), _a(s2), op0, **kw),
                 _keys(in0, s1, s2), _keys(out, accum))

    def tt(self, out, in0, in1, op, eng="dve"):
        self.add(eng, lambda e: e.tensor_tensor(out.ap, in0.ap, in1.ap, op), _keys(in0, in1), _keys(out))

    def stt(self, out, in0, scalar, in1, op0, op1, eng="dve"):
        self.add(eng, lambda e: e.scalar_tensor_tensor(out.ap, in0.ap, _a(scalar), in1.ap, op0, op1),
                 _keys(in0, scalar, in1), _keys(out))

    def copy(self, out, in_, eng="dve"):
        if eng == "act":
            self.add(eng, lambda e: e.copy(out.ap, in_.ap), _keys(in_), _keys(out))
        else:
            self.add(eng, lambda e: e.tensor_copy(out.ap, in_.ap), _keys(in_), _keys(out))

    def memset(self, out, val, eng="pool"):
        self.add(eng, lambda e: e.memset(out.ap, val), (), _keys(out))

    def recip(self, out, in_, eng="dve"):
        self.add(eng, lambda e: e.reciprocal(out.ap, in_.ap), _keys(in_), _keys(out))

    def reduce(self, out, in_, op, axis=AX.X, eng="dve", absval=None):
        self.add(eng, lambda e: e.tensor_reduce(out.ap, in_.ap, axis, op, apply_absolute_value=absval),
                 _keys(in_), _keys(out))

    def bn_stats(self, out, in_):
        self.add("dve", lambda e: e.bn_stats(out.ap, in_.ap), _keys(in_), _keys(out))

    def bn_aggr(self, out, in_):
        self.add("dve", lambda e: e.bn_aggr(out.ap, in_.ap), _keys(in_), _keys(out))

    def iota(self, out, pattern, base, cm):
        self.add("pool", lambda e: e.iota(out.ap, pattern, base=base, channel_multiplier=cm,
                                          allow_small_or_imprecise_dtypes=True), (), _keys(out))

    def dma(self, out, in_, slow=False):
        kw = {"allow_slow_non_contiguous": True} if slow else {}
        self.add("sp", lambda e: e.dma_start(_a(out), _a(in_), **kw), _keys(in_), _keys(out), dma=True)

    def emit(self):
        K = self.K
        nc = K.nc
        ops = self.ops
        last_w = {}
        rd = {}
        for i, op in enumerate(ops):
            deps = {}

            def add(j, kind, op=op, deps=deps, i=i):
                if j is None or j == i:
                    return
                Pp = ops[j]
                if Pp.eng == op.eng and not Pp.dma and not op.dma:
                    if op.eng == "pe":
                        return
                deps[j] = True

            for k in op.reads:
                add(last_w.get(k), "RAW")
            for k in op.writes:
                add(last_w.get(k), "WAW")
                r = rd.get(k)
                if r:
                    for j in r.values():
                        add(j, "WAR")
            op.deps = tuple(deps)
            for j in deps:
                ops[j].signal = True
            for k in op.reads:
                r = rd.get(k)
                if r is None:
                    r = rd[k] = {}
                r[("d", i) if op.dma else op.eng] = i
            for k in op.writes:
                last_w[k] = i
                rd[k] = {}
        if self.name == "w":
            if not hasattr(K, "w_sems"):
                K.w_sems = {e: nc.alloc_semaphore(name=f"phw_{e}") for e in ("pe", "act", "dve", "pool")}
                K.w_cnt = {e: 0 for e in K.w_sems}
            esem, ecnt = K.w_sems, K.w_cnt
        else:
            esem = {e: nc.alloc_semaphore(name=f"ph{K.phase_i}_{e}") for e in ("pe", "act", "dve", "pool")}
            K.phase_i += 1
            ecnt = {e: 0 for e in esem}
        for op in ops:
            if op.dma:
                n = K.dma_n
                K.dma_n += 1
                s = n % K.ndma
                K.dma_cnt[s] += 16
                op.sem = K.dma_sems[s]
                op.count = K.dma_cnt[s]
                if op.count > 16:
                    op.pre = (op.sem, op.count - 16)
            elif op.signal:
                ecnt[op.eng] += 1
                op.sem = esem[op.eng]
                op.count = ecnt[op.eng]
        by_eng = {e: [] for e in ENGS}
        for op in ops:
            by_eng[op.eng].append(op)
        K.n_inst += len(ops)

        def run(e, name):
            waited = K.waited[name]
            for op in by_eng[name]:
                need = {}
                if op.pre is not None:
                    need[id(op.pre[0])] = op.pre
                for j in op.deps:
                    Pp = ops[j]
                    cur = need.get(id(Pp.sem))
                    if cur is None or cur[1] < Pp.count:
                        need[id(Pp.sem)] = (Pp.sem, Pp.count)
                for sid, (sem, val) in need.items():
                    if waited.get(sid, 0) < val:
                        e.wait_ge(sem, val)
                        waited[sid] = val
                inst = op.fn(e)
                if op.dma:
                    inst.then_inc(op.sem, 16)
                elif op.signal:
                    inst.then_inc(op.sem, 1)
            if name == "sp":
                for s in range(K.ndma):
                    if K.dma_cnt[s] > waited.get(id(K.dma_sems[s]), 0):
                        e.wait_ge(K.dma_sems[s], K.dma_cnt[s])
                        waited[id(K.dma_sems[s])] = K.dma_cnt[s]

        with nc.Block() as block:
            @block.tensor
            def _(e):
                run(e, "pe")

            @block.scalar
            def _(e):
                run(e, "act")

            @block.vector
            def _(e):
                run(e, "dve")

            @block.gpsimd
            def _(e):
                run(e, "pool")

            @block.sync
            def _(e):
                run(e, "sp")
        self.ops = []


class Alloc:
    _n = [0]

    def __init__(self, nc, stack):
        self.nc = nc
        self.stack = stack

    def sb(self, shape, dt, name=None):
        Alloc._n[0] += 1
        nm = f"{name or 't'}_{Alloc._n[0]}"
        t = self.stack.enter_context(self.nc.sbuf_tensor(nm, list(shape), dt))
        return V(t[:] if hasattr(t, "__getitem__") else t.ap(), nm)

    def ps(self, shape, dt, name=None):
        Alloc._n[0] += 1
        nm = f"{name or 'p'}_{Alloc._n[0]}"
        t = self.stack.enter_context(self.nc.psum_tensor(nm, list(shape), dt))
        return V(t[:] if hasattr(t, "__getitem__") else t.ap(), nm)


class Ctx:
    pass


def load_weight(C, dst, src, ncols, kcs):
    import contextlib
    nc = C.nc
    with contextlib.ExitStack() as st:
        A = Alloc(nc, st)
        CH = min(ncols, 2048)
        stage = [A.sb([128, CH], F32, "wst") for _ in range(3)]
        P = Phase(C.K, "w")
        i = 0
        for kc in range(kcs):
            for c0 in range(0, ncols, CH):
                sv = stage[i % 3]
                P.dma(sv, src[kc * 128:(kc + 1) * 128, c0:c0 + CH])
                P.copy(dst[:, kc, c0:c0 + CH], sv, eng=("dve" if i % 2 == 0 else "pool"))
                i += 1
        P.emit()


def load_consts(P, A, C):
    ident32 = A.sb([128, 128], F32, "id32")
    ident = A.sb([128, 128], BF16, "id")
    mhalf = A.sb([128, 1], F32, "mh")
    P.dma(ident32, C.ident)
    P.copy(ident, ident32)
    P.memset(mhalf, -0.5)
    return ident, mhalf


def load_mod(P, A, C, layer, which, want_pp=True, want_gate=True):
    off = 0 if which == 1 else 3 * D
    ng = C.norm1_g if which == 1 else C.norm2_g
    gpp = A.sb([128, 8], F32, "gpp")
    P.dma(gpp, ng[layer].rearrange("(kc p) -> p kc", p=128), slow=True)
    a_pp, b_pp, G = [], [], []
    for s in range(C.NSEQ):
        row = C.MOD[layer, s]
        if want_pp:
            sc = A.sb([128, 8], F32, "scpp")
            a = A.sb([128, 8], F32, "app")
            b = A.sb([128, 8], F32, "bpp")
            P.dma(b, row[off:off + D].rearrange("(kc p) -> p kc", p=128), slow=True)
            P.dma(sc, row[off + D:off + 2 * D].rearrange("(kc p) -> p kc", p=128), slow=True)
            P.stt(a, sc, 1.0, gpp, ALU.add, ALU.mult)
            a_pp.append(a)
            b_pp.append(b)
        if want_gate:
            g = A.sb([128, D], F32, "gate")
            P.dma(g, row[off + 2 * D:off + 3 * D].partition_broadcast(128))
            G.append(g)
    return a_pp, b_pp, G


def front1(P, xt, ss, junk, xn, mhalf, xn_eng="dve"):
    P.act(junk, xt, AF.Square, accum=ss[:, 0:1])
    P.ts(ss[:, 1:2], ss[:, 0:1], 1.0 / D, EPS, ALU.mult, ALU.add)
    P.tt(ss[:, 2:3], ss[:, 1:2], mhalf, ALU.pow, eng="pool")
    if xn_eng == "act":
        P.act(xn, xt, AF.Identity, scale=ss[:, 2:3])
    else:
        P.ts(xn, xt, ss[:, 2:3], None, ALU.mult)


def front2(P, xn, ident, pT, hT, a_pp, b_pp, ev_eng="dve"):
    for kc in range(8):
        P.tr(pT[:, kc, :], xn[:, kc * 128:(kc + 1) * 128], ident)
    for kc in range(8):
        if ev_eng == "act":
            P.act(hT[:, kc, :], pT[:, kc, :], AF.Identity, bias=b_pp[:, kc:kc + 1], scale=a_pp[:, kc:kc + 1])
        else:
            P.ts(hT[:, kc, :], pT[:, kc, :], a_pp[:, kc:kc + 1], b_pp[:, kc:kc + 1], ALU.mult, ALU.add)


def front(P, xt, ss, junk, xn, ident, mhalf, pT, hT, a_pp, b_pp, ev_eng="dve"):
    front1(P, xt, ss, junk, xn, mhalf)
    front2(P, xn, ident, pT, hT, a_pp, b_pp, ev_eng)


def phase_mod(C):
    import contextlib
    nc = C.nc
    NS = C.NSEQ
    with contextlib.ExitStack() as st:
        A = Alloc(nc, st)
        P = Phase(C.K, "mod")
        ct32 = A.sb([128, 8, NS], F32)
        sct = A.sb([128, 8, NS], BF16)
        P.dma(ct32, C.cT)
        P.act(sct, ct32, AF.Silu)
        stage = [A.sb([128, 8, 512], F32, "mst") for _ in range(2)]
        wb = [A.sb([128, 8, 512], BF16, "mwb") for _ in range(2)]
        bias = A.sb([NS, 2, 6 * D], F32)
        res = A.sb([NS, 2, 6 * D], F32)
        pp = [A.ps([128, 512], F32) for _ in range(2)]
        for l in range(2):
            for s in range(NS):
                P.dma(bias[s:s + 1, l, :], C.mod_b[l:l + 1, :])
        i = 0
        for l in range(2):
            for nt in range(12):
                sv, wv, pv = stage[i % 2], wb[i % 2], pp[i % 2]
                P.dma(sv, C.mod_w[l, :, nt * 512:(nt + 1) * 512].rearrange("(kc p) n -> p kc n", p=128))
                P.copy(wv, sv, eng=("dve" if i % 2 == 0 else "pool"))
                for kc in range(8):
                    P.mm(pv[0:NS, :], sct[:, kc, :], wv[:, kc, :], start=(kc == 0), stop=(kc == 7))
                P.tt(res[:, l, nt * 512:(nt + 1) * 512], pv[0:NS, :], bias[:, l, nt * 512:(nt + 1) * 512], ALU.add)
                i += 1
        for l in range(2):
            P.dma(C.MOD[l], res[:, l, :])
        P.emit()


def phase_mlp(C, layer, Xin, Xout, final):
    import contextlib
    nc = C.nc
    NS, NT = C.NSEQ, C.NT
    TS = 2
    with contextlib.ExitStack() as st0:
        A0 = Alloc(nc, st0)
        W1 = A0.sb([128, 8, DFF], BF16, "W1")
        W2 = A0.sb([128, 32, D], BF16, "W2")
        load_weight(C, W1, C.mlp_w1[layer], DFF, 8)
        load_weight(C, W2, C.mlp_w2[layer], D, 32)
        with contextlib.ExitStack() as st:
            A = Alloc(nc, st)
            P = Phase(C.K, "mlp")
            ident, mhalf = load_consts(P, A, C)
            a_pp, b_pp, G = load_mod(P, A, C, layer, 2)
            if final:
                FG = A.sb([128, D], F32, "FG")
                P.dma(FG, C.final_g.partition_broadcast(128))
            xt = [A.sb([128, D], F32, "xt") for _ in range(2 * TS)]
            xn = [A.sb([128, D], BF16, "xn") for _ in range(2)]
            junk = A.sb([128, D], BF16, "junk")
            ss = [A.sb([128, 4], F32, "ss") for _ in range(4)]
            hT = [A.sb([128, 8, TS * 128], BF16, "hT") for _ in range(2)]
            uT = A.sb([128, 32, TS * 128], BF16, "uT")
            r32 = [A.sb([128, TS * 128], F32, "r32") for _ in range(2)]
            tmp = [A.sb([128, 512], F32, "tmp") for _ in range(2)]
            pT = [A.ps([128, 8, 128], BF16, "pT") for _ in range(2)]
            pu = [A.ps([128, 512], F32, "pu") for _ in range(3)]
            po = [A.ps([128, 512], F32, "po") for _ in range(3)]
            items = [(s, sc) for s in range(NS) for sc in range(NT // TS)]
            cnt = [0]

            def xs_of(i):
                return [xt[(i % 2) * TS + j] for j in range(TS)]

            def do_load(i):
                s, sc = items[i]
                for j in range(TS):
                    c = sc * TS + j
                    P.dma(xs_of(i)[j], Xin[s, c * 128:(c + 1) * 128, :])

            def do_front(i):
                s, sc = items[i]
                for j in range(TS):
                    k = cnt[0]
                    cnt[0] += 1
                    front(P, xs_of(i)[j], ss[k % 4], junk, xn[k % 2], ident, mhalf, pT[k % 2],
                          hT[i % 2][:, :, j * 128:(j + 1) * 128], a_pp[s], b_pp[s],
                          ev_eng=("act" if k % 2 else "dve"))

            def do_w1(i):
                hv = hT[i % 2]
                for fc in range(32):
                    pv = pu[fc % 3]
                    for kc in range(8):
                        P.mm(pv[:, 0:TS * 128], W1[:, kc, fc * 128:(fc + 1) * 128], hv[:, kc, :],
                             start=(kc == 0), stop=(kc == 7))
                    rv = r32[fc % 2]
                    P.act(rv, pv[:, 0:TS * 128], AF.Relu)
                    P.tt(uT[:, fc, :], rv, rv, ALU.mult, eng=("dve" if fc % 2 == 0 else "pool"))

            def do_w2(i):
                s, sc = items[i]
                xs = xs_of(i)
                for j in range(TS):
                    for half in range(2):
                        pv = po[(j * 2 + half) % 3]
                        for fc in range(32):
                            P.mm(pv, uT[:, fc, j * 128:(j + 1) * 128], W2[:, fc, half * 512:(half + 1) * 512],
                                 start=(fc == 0), stop=(fc == 31))
                        tv = tmp[half]
                        P.tt(tv, pv, G[s][:, half * 512:(half + 1) * 512], ALU.mult)
                        P.tt(xs[j][:, half * 512:(half + 1) * 512], tv, xs[j][:, half * 512:(half + 1) * 512],
                             ALU.add, eng="pool")
                    c = sc * TS + j
                    if final:
                        k = cnt[0]
                        cnt[0] += 1
                        sv = ss[k % 4]
                        P.act(junk, xs[j], AF.Square, accum=sv[:, 0:1])
                        P.ts(sv[:, 1:2], sv[:, 0:1], 1.0 / D, EPS, ALU.mult, ALU.add)
                        P.tt(sv[:, 2:3], sv[:, 1:2], mhalf, ALU.pow, eng="pool")
                        P.stt(xs[j], xs[j], sv[:, 2:3], FG, ALU.mult, ALU.mult)
                    P.dma(Xout[s, c * 128:(c + 1) * 128, :], xs[j])

            n = len(items)
            do_load(0)
            do_front(0)
            for i in range(n):
                if i + 1 < n:
                    do_load(i + 1)
                do_w1(i)
                if i + 1 < n:
                    do_front(i + 1)
                do_w2(i)
            P.emit()


def decay_tables(P, A, C, want):
    T = {}
    dl = A.sb([128, 8], F32, "dl")
    lg = A.sb([128, 8], F32, "lg")
    P.dma(dl, C.ret_decay.partition_broadcast(128))
    P.act(lg, dl, AF.Exp, scale=-1.0)
    P.ts(lg, lg, 1.0, None, ALU.add)
    P.act(lg, lg, AF.Ln)
    P.ts(lg, lg, -1.0, None, ALU.mult)
    T["lg"] = lg
    pidx = A.sb([128, 1], F32, "pidx")
    P.iota(pidx, [[0, 1]], 0, 1)
    jidx = A.sb([128, 128], F32, "jidx")
    P.iota(jidx, [[1, 128]], 0, 0)
    scale = float(RET_DK) ** -0.5
    if "pp" in want:
        rp = A.sb([128, 1], F32, "rp")
        P.ts(rp, pidx, -1.0, 127.0, ALU.mult, ALU.add)
        kdf = A.sb([128, 4], F32, "kdf")
        kdb = A.sb([128, 4], F32, "kdb")
        for h in range(4):
            P.act(kdf[:, h:h + 1], rp, AF.Exp, scale=lg[:, h:h + 1])
            P.act(kdb[:, h:h + 1], pidx, AF.Exp, scale=lg[:, 4 + h:5 + h])
        T["kdf"], T["kdb"] = kdf, kdb
        j1 = A.sb([128, 128], F32, "j1")
        jr = A.sb([128, 128], F32, "jr")
        P.ts(j1, jidx, 1.0, None, ALU.add)
        P.ts(jr, jidx, -1.0, 128.0, ALU.mult, ALU.add)
        qdf = A.sb([128, 8, 128], F32, "qdf")
        qdb = A.sb([128, 8, 128], F32, "qdb")
        for h in range(4):
            for dc in range(2):
                P.act(qdf[:, 2 * h + dc, :], j1, AF.Exp, scale=lg[:, h:h + 1])
                P.act(qdb[:, 2 * h + dc, :], jr, AF.Exp, scale=lg[:, 4 + h:5 + h])
        P.ts(qdf, qdf, scale, None, ALU.mult)
        P.ts(qdb, qdb, scale, None, ALU.mult)
        T["qdf"], T["qdb"] = qdf, qdb
    if "mask" in want:
        diff = A.sb([128, 128], F32, "diff")
        P.ts(diff, jidx, pidx, None, ALU.subtract)
        dpos = A.sb([128, 128], F32, "dpos")
        dneg = A.sb([128, 128], F32, "dneg")
        mge = A.sb([128, 128], F32, "mge")
        P.ts(dpos, diff, 0.0, None, ALU.max)
        P.tt(dneg, dpos, diff, ALU.subtract)
        P.ts(mge, diff, 0.0, None, ALU.is_ge)
        DT = A.sb([128, 4, 128], F32, "DT")
        ea = A.sb([128, 128], F32, "ea")
        eb = A.sb([128, 128], F32, "eb")
        for h in range(4):
            P.act(ea, dpos, AF.Exp, scale=lg[:, h:h + 1])
            P.act(eb, dneg, AF.Exp, scale=lg[:, 4 + h:5 + h])
            P.tt(ea, ea, eb, ALU.subtract)
            P.tt(ea, ea, mge, ALU.mult)
            P.tt(ea, ea, eb, ALU.add)
            P.ts(DT[:, h, :], ea, scale, None, ALU.mult)
        T["DT"] = DT
        cd = A.sb([128, 8], F32, "cdec")
        P.act(cd, lg, AF.Exp, scale=128.0)
        T["cdec"] = cd
    return T


def phase_l0a(C):
    import contextlib
    nc = C.nc
    NS, NT = C.NSEQ, C.NT
    with contextlib.ExitStack() as st0:
        A0 = Alloc(nc, st0)
        W = A0.sb([128, 8, RET_IN], BF16, "Win")
        load_weight(C, W, C.ret_w_in, RET_IN, 8)
        with contextlib.ExitStack() as st:
            A = Alloc(nc, st)
            P = Phase(C.K, "l0a")
            ident, mhalf = load_consts(P, A, C)
            a_pp, b_pp, _G = load_mod(P, A, C, 0, 1, want_gate=False)
            T = decay_tables(P, A, C, ("pp",))
            xt = [A.sb([128, D], F32, "xt") for _ in range(3)]
            cs = [A.sb([128, 2, 2, 64], F32, "cs") for _ in range(3)]
            xn = [A.sb([128, D], BF16, "xn") for _ in range(2)]
            junk = A.sb([128, D], BF16, "junk")
            ss = [A.sb([128, 4], F32, "ss") for _ in range(4)]
            hT = [A.sb([128, 8, 128], BF16, "hT") for _ in range(2)]
            qk32 = [A.sb([128, 2048], F32, "qk32")] * 2
            t1 = A.sb([128, 8, 2, 64], F32, "t1")
            t2 = A.sb([128, 8, 2, 64], F32, "t2")
            t3, t4 = t1, t2
            qkr = [A.sb([128, 2048], BF16, "qkr") for _ in range(2)]
            kf = [A.sb([128, 4, 256], BF16, "kf") for _ in range(2)]
            kb = [A.sb([128, 4, 256], BF16, "kb") for _ in range(2)]
            vo = [A.sb([128, 2048], BF16, "vo") for _ in range(2)]
            sg = [A.sb([128, 2048], BF16, "sg") for _ in range(2)]
            qo = [A.sb([128, 3, 8, 128], BF16, "qo") for _ in range(2)]
            ko = [A.sb([128, 8, 128], BF16, "ko") for _ in range(2)]
            pT = [A.ps([128, 8, 128], BF16, "pT") for _ in range(2)]
            pp = [A.ps([128, 512], F32, "pp") for _ in range(4)]
            pq = A.ps([128, 16, 128], BF16, "pq")
            items = [(s, c) for s in range(NS) for c in range(NT)]

            def do_load(i):
                s, c = items[i]
                P.dma(xt[i % 3], C.x[s, c * 128:(c + 1) * 128, :])
                P.dma(cs[i % 3], C.rope_r[c])

            def do_A1(i):
                front1(P, xt[i % 3], ss[i % 4], junk, xn[i % 2], mhalf)

            def do_A2(i):
                s, c = items[i]
                front2(P, xn[i % 2], ident, pT[i % 2], hT[i % 2], a_pp[s], b_pp[s], ev_eng="dve")

            def do_B(i, cts):
                s, c = items[i]
                sl = i % 2
                hv = hT[sl]
                for ct in cts:
                    pv = pp[ct % 4]
                    for kc in range(8):
                        P.mm(pv, hv[:, kc, :], W[:, kc, ct * 512:(ct + 1) * 512], start=(kc == 0), stop=(kc == 7))
                    if ct < 4:
                        P.copy(qk32[sl][:, ct * 512:(ct + 1) * 512], pv, eng="act")
                    elif ct < 8:
                        P.copy(vo[sl][:, (ct - 4) * 512:(ct - 3) * 512], pv, eng="dve")
                    else:
                        P.act(sg[sl][:, (ct - 8) * 512:(ct - 7) * 512], pv, AF.Silu)

            def do_C(i):
                s, c = items[i]
                sl = i % 2
                v5 = qk32[sl].re("p (h f a d) -> p h f a d", h=8, f=2, a=2)
                o5 = qkr[sl].re("p (h f a d) -> p h f a d", h=8, f=2, a=2)
                a_, b_ = v5[:, :, :, 0, :], v5[:, :, :, 1, :]
                csv = cs[i % 3]
                cos = V(csv.ap[:, 0].unsqueeze(1).broadcast_to([128, 8, 2, 64]), csv.key)
                sin = V(csv.ap[:, 1].unsqueeze(1).broadcast_to([128, 8, 2, 64]), csv.key)
                P.tt(t1, a_, cos, ALU.mult, eng="dve")
                P.tt(t2, b_, sin, ALU.mult, eng="pool")
                P.tt(o5[:, :, :, 0, :], t1, t2, ALU.subtract, eng="dve")
                P.tt(t3, a_, sin, ALU.mult, eng="pool")
                P.tt(t4, b_, cos, ALU.mult, eng="dve")
                P.tt(o5[:, :, :, 1, :], t3, t4, ALU.add, eng="pool")
                kv = qkr[sl][:, 1024:2048].re("p (h d) -> p h d", h=4)
                P.tt(kf[sl], kv, V(T["kdf"].ap.unsqueeze(2).broadcast_to([128, 4, 256]), T["kdf"].key), ALU.mult, eng="pool")
                P.tt(kb[sl], kv, V(T["kdb"].ap.unsqueeze(2).broadcast_to([128, 4, 256]), T["kdb"].key), ALU.mult, eng="pool")

            def do_D(i):
                s, c = items[i]
                sl = i % 2
                for j in range(16):
                    P.tr(pq[:, j, :], qkr[sl][:, j * 128:(j + 1) * 128], ident)
                P.copy(qo[sl][:, 0], pq[:, 0:8, :], eng="dve")
                P.tt(qo[sl][:, 1], pq[:, 0:8, :], T["qdf"], ALU.mult, eng="dve")
                P.tt(qo[sl][:, 2], pq[:, 0:8, :], T["qdb"], ALU.mult, eng="dve")
                P.copy(ko[sl], pq[:, 8:16, :], eng="act")
                P.dma(C.QT[s, c].rearrange("v p h d t -> p v (h d) t"), qo[sl])
                P.dma(C.KT[s, c].rearrange("p h d t -> p (h d) t"), ko[sl])
                P.dma(C.KF[s, c], kf[sl])
                P.dma(C.KB[s, c], kb[sl])
                P.dma(C.VS[s, c].rearrange("p h d -> p (h d)"), vo[sl])
                P.dma(C.SG[s, c].rearrange("p h d -> p (h d)"), sg[sl])

            n = len(items)
            do_load(0)
            if n > 1:
                do_load(1)
            do_A1(0)
            do_A2(0)
            for i in range(n):
                if i + 2 < n:
                    do_load(i + 2)
                do_B(i, range(0, 4))
                if i + 1 < n:
                    do_A1(i + 1)
                do_C(i)
                do_B(i, range(4, 8))
                if i + 1 < n:
                    do_A2(i + 1)
                do_B(i, range(8, 12))
                do_D(i)
            P.emit()


def phase_l0b(C):
    import contextlib
    nc = C.nc
    NS, NT = C.NSEQ, C.NT
    GC = 4
    NG = NT // GC
    with contextlib.ExitStack() as st:
        A = Alloc(nc, st)
        P = Phase(C.K, "l0b")
        ident, mhalf = load_consts(P, A, C)
        T = decay_tables(P, A, C, ("mask",))
        DT, cdec = T["DT"], T["cdec"]
        epsv = A.sb([128, 1], F32, "epsv")
        P.memset(epsv, EPS)
        SbA = A.sb([128, NT, 2, 512], BF16, "SbA")
        SbK = [V(SbA.ap[:, c], ("SbA", c)) for c in range(NT)]
        S32 = [A.sb([128, 2, 512], F32, "S32") for _ in range(2)]
        sj = [0]
        NSF = 4
        Sf = [A.sb([128, 2, 512], BF16, "Sf") for _ in range(NSF)]
        bk = [A.sb([128, GC, 256], BF16, "bk") for _ in range(2)]
        bv = [A.sb([128, GC, 512], BF16, "bv") for _ in range(2)]
        fq = [A.sb([128, GC, 3, 2, 128], BF16, "fq") for _ in range(3)]
        fkT = [A.sb([128, GC, 2, 128], BF16, "fkT") for _ in range(3)]
        fkf = [A.sb([128, GC, 256], BF16, "fkf") for _ in range(3)]
        fv = [A.sb([128, GC, 512], BF16, "fv") for _ in range(3)]
        fsg = [A.sb([128, GC, 512], BF16, "fsg") for _ in range(3)]
        sTm = [A.sb([128, 128], BF16, "sTm") for _ in range(2)]
        st6 = [A.sb([128, 6], F32, "st6") for _ in range(2)]
        mv = [A.sb([128, 8], F32, "mv") for _ in range(2)]
        yn = [A.sb([128, 512], F32, "yn") for _ in range(2)]
        yb = [A.sb([128, 512], BF16, "yb") for _ in range(2)]
        yTa = [A.sb([128, 4, GC * 128], BF16, "yTa") for _ in range(3)]
        pss = A.ps([128, 512], F32, "pss")
        po = [A.ps([128, 512], F32, "po") for _ in range(2)]
        pk = [A.ps([128, 2, 512], F32, "pk") for _ in range(2)]
        pyT = A.ps([128, 8, 128], BF16, "pyT")
        groups = []
        for s in range(NS):
            for h in range(RET_H):
                for g in range(NG - 1, -1, -1):
                    groups.append(("b", s, h, g))
                for g in range(NG):
                    groups.append(("f", s, h, g))
        cntb = [0]
        cntf = [0]
        slot_of = {}

        def do_load(k):
            kind, s, h, g = groups[k]
            c0 = g * GC
            if kind == "b":
                sl = cntb[0] % 2
                cntb[0] += 1
                slot_of[k] = sl
                P.dma(bk[sl], C.KB[s, c0:c0 + GC, :, h, :].rearrange("c p d -> p c d"))
                P.dma(bv[sl], C.VS[s, c0:c0 + GC, :, h, :].rearrange("c p d -> p c d"))
            else:
                sl = cntf[0] % 3
                cntf[0] += 1
                slot_of[k] = sl
                P.dma(fq[sl], C.QT[s, c0:c0 + GC, :, :, h, :, :].rearrange("c v p d t -> p c v d t"))
                P.dma(fkT[sl], C.KT[s, c0:c0 + GC, :, h, :, :].rearrange("c p d t -> p c d t"))
                P.dma(fkf[sl], C.KF[s, c0:c0 + GC, :, h, :].rearrange("c p d -> p c d"))
                P.dma(fv[sl], C.VS[s, c0:c0 + GC, :, h, :].rearrange("c p d -> p c d"))
                P.dma(fsg[sl], C.SG[s, c0:c0 + GC, :, h, :].rearrange("c p d -> p c d"))

        kvi = [0]
        ci = [0]
        q_gn2 = []
        q_gate = []
        q_tail = []

        def flush():
            while q_gn2:
                q_gn2.pop(0)()
            while q_gate:
                q_gate.pop(0)()
            while q_tail:
                q_tail.pop(0)()

        do_load(0)
        for k, (kind, s, h, g) in enumerate(groups):
            if k + 1 < len(groups):
                do_load(k + 1)
            sl = slot_of[k]
            c0 = g * GC
            if kind == "b":
                if g == NG - 1:
                    flush()
                    P.memset(S32[sj[0] % 2], 0.0)
                    P.memset(SbK[NT - 1], 0.0)
                for cc in range(GC - 1, -1, -1):
                    c = c0 + cc
                    if c == 0:
                        continue
                    pkv = pk[kvi[0] % 2]
                    kvi[0] += 1
                    for dc in range(2):
                        P.mm(pkv[:, dc, :], bk[sl][:, cc, dc * 128:(dc + 1) * 128], bv[sl][:, cc, :])
                    sprev, scur = S32[sj[0] % 2], S32[(sj[0] + 1) % 2]
                    sj[0] += 1
                    P.stt(scur, sprev, cdec[:, 4 + h:5 + h], pkv, ALU.mult, ALU.add)
                    P.copy(SbK[c - 1], scur, eng="act")
            else:
                if g == 0:
                    P.memset(S32[sj[0] % 2], 0.0)
                    P.memset(Sf[ci[0] % NSF], 0.0)
                for cc in range(GC):
                    c = c0 + cc
                    k2 = ci[0] % 2
                    sfc, sfn = Sf[ci[0] % NSF], Sf[(ci[0] + 1) % NSF]
                    ci[0] += 1
                    if c < NT - 1:
                        pkv = pk[kvi[0] % 2]
                        kvi[0] += 1
                        for dc in range(2):
                            P.mm(pkv[:, dc, :], fkf[sl][:, cc, dc * 128:(dc + 1) * 128], fv[sl][:, cc, :])
                    for dc in range(2):
                        P.mm(pss[:, 0:128], fkT[sl][:, cc, dc, :], fq[sl][:, cc, 0, dc, :],
                             start=(dc == 0), stop=(dc == 1))
                    P.tt(sTm[k2], pss[:, 0:128], DT[:, h, :], ALU.mult)
                    if q_gn2:
                        q_gn2.pop(0)()
                    pov = po[k2]
                    P.mm(pov, sTm[k2], fv[sl][:, cc, :], start=True, stop=False)
                    for dc in range(2):
                        P.mm(pov, fq[sl][:, cc, 1, dc, :], sfc[:, dc, :], start=False, stop=False)
                    for dc in range(2):
                        P.mm(pov, fq[sl][:, cc, 2, dc, :], SbK[c][:, dc, :], start=False, stop=(dc == 1))
                    if q_tail:
                        q_tail.pop(0)()
                    if c < NT - 1:
                        sprev, scur = S32[sj[0] % 2], S32[(sj[0] + 1) % 2]
                        sj[0] += 1
                        P.stt(scur, sprev, cdec[:, h:h + 1], pkv, ALU.mult, ALU.add)
                        P.copy(sfn, scur, eng="act")
                    m = mv[k2]
                    P.bn_stats(st6[k2], pov)
                    P.bn_aggr(m[:, 0:2], st6[k2])
                    P.ts(m[:, 2:3], m[:, 1:2], epsv, None, ALU.add)
                    P.tt(m[:, 3:4], m[:, 2:3], mhalf, ALU.pow, eng="pool")
                    if q_gate:
                        q_gate.pop(0)()

                    def tail(k2=k2, sl=sl, cc=cc, s=s, h=h, c0=c0):
                        for ec in range(4):
                            P.tr(pyT[:, ec, :], yb[k2][:, ec * 128:(ec + 1) * 128], ident)
                        P.copy(yTa[sl][:, :, cc * 128:(cc + 1) * 128], pyT[:, 0:4, :], eng="act")
                        if cc == GC - 1:
                            P.dma(C.YT[s, :, h * 4:(h + 1) * 4, c0 * 128:(c0 + GC) * 128], yTa[sl])

                    def gate(k2=k2, sl=sl, cc=cc, tail=tail):
                        P.tt(yb[k2], yn[k2], fsg[sl][:, cc, :], ALU.mult, eng="pool")
                        q_tail.append(tail)

                    def gn2(k2=k2, m=m, pov=pov, gate=gate):
                        P.ts(m[:, 4:5], m[:, 0:1], m[:, 3:4], -1.0, ALU.mult, ALU.mult)
                        P.act(yn[k2], pov, AF.Identity, bias=m[:, 4:5], scale=m[:, 3:4])
                        q_gate.append(gate)
                    q_gn2.append(gn2)
        flush()
        P.emit()


def phase_l0c(C, Xin, Xout):
    import contextlib
    nc = C.nc
    NS, NT = C.NSEQ, C.NT
    TS = 4
    with contextlib.ExitStack() as st0:
        A0 = Alloc(nc, st0)
        W = A0.sb([128, 16, D], BF16, "Wro")
        load_weight(C, W, C.ret_w_out, D, 16)
        with contextlib.ExitStack() as st:
            A = Alloc(nc, st)
            P = Phase(C.K, "l0c")
            _a, _b, G = load_mod(P, A, C, 0, 1, want_pp=False)
            yT = [A.sb([128, 16, TS * 128], BF16, "yT") for _ in range(2)]
            xt = [A.sb([128, D], F32, "xt") for _ in range(2 * TS)]
            tmp = [A.sb([128, 512], F32, "tmp") for _ in range(2)]
            po = [A.ps([128, 512], F32, "po") for _ in range(4)]
            items = [(s, sc) for s in range(NS) for sc in range(NT // TS)]

            def do_load(i):
                s, sc = items[i]
                sl = i % 2
                P.dma(yT[sl], C.YT[s, :, :, sc * TS * 128:(sc + 1) * TS * 128])
                for j in range(TS):
                    c = sc * TS + j
                    P.dma(xt[sl * TS + j], Xin[s, c * 128:(c + 1) * 128, :])

            do_load(0)
            for i, (s, sc) in enumerate(items):
                sl = i % 2
                if i + 1 < len(items):
                    do_load(i + 1)
                for j in range(TS):
                    c = sc * TS + j
                    xv = xt[sl * TS + j]
                    for half in range(2):
                        pv = po[(j * 2 + half) % 4]
                        for ec in range(16):
                            P.mm(pv, yT[sl][:, ec, j * 128:(j + 1) * 128], W[:, ec, half * 512:(half + 1) * 512],
                                 start=(ec == 0), stop=(ec == 15))
                        P.tt(tmp[half], pv, G[s][:, half * 512:(half + 1) * 512], ALU.mult)
                        P.tt(xv[:, half * 512:(half + 1) * 512], tmp[half], xv[:, half * 512:(half + 1) * 512],
                             ALU.add, eng="pool")
                    P.dma(Xout[s, c * 128:(c + 1) * 128, :], xv)
            P.emit()


def phase_l1a(C, Xin):
    import contextlib
    nc = C.nc
    NS, NT = C.NSEQ, C.NT
    GC = 4
    with contextlib.ExitStack() as st0:
        A0 = Alloc(nc, st0)
        W = A0.sb([128, 8, ATT_IN], BF16, "Wai")
        load_weight(C, W, C.att_w_in, ATT_IN, 8)
        with contextlib.ExitStack() as st:
            A = Alloc(nc, st)
            P = Phase(C.K, "l1a")
            ident, mhalf = load_consts(P, A, C)
            a_pp, b_pp, _G = load_mod(P, A, C, 1, 1, want_gate=False)
            gains = A.sb([128, 10, 128], F32, "gains")
            for hh in range(10):
                src = C.att_q_gain if hh < 8 else C.att_k_gain
                P.dma(gains[:, hh, :], src.partition_broadcast(128))
            xt = [A.sb([128, D], F32, "xt") for _ in range(3)]
            cs = [A.sb([128, 2, 2, 32], F32, "cs") for _ in range(4)]
            xn = [A.sb([128, D], BF16, "xn") for _ in range(2)]
            junk = A.sb([128, D], BF16, "junk")
            ss = [A.sb([128, 4], F32, "ss") for _ in range(4)]
            hT = [A.sb([128, 8, 128], BF16, "hT") for _ in range(2)]
            qk32_ = [A.sb([128, 10, 128], F32, "qk32") for _ in range(2)]
            sq_ = [A.sb([128, 10, 128], F32, "sq") for _ in range(2)]
            s10 = [A.sb([128, 32], F32, "s10") for _ in range(2)]
            qkn_ = [A.sb([128, 10, 128], F32, "qkn") for _ in range(2)]
            t1_ = [A.sb([128, 10, 2, 32], F32, "t1") for _ in range(2)]
            t2_ = [A.sb([128, 10, 2, 32], F32, "t2") for _ in range(2)]
            t3_ = [A.sb([128, 10, 2, 32], F32, "t3") for _ in range(2)]
            t4_ = [A.sb([128, 10, 2, 32], F32, "t4") for _ in range(2)]
            qkr = [A.sb([128, 10 * 128], BF16, "qkr") for _ in range(2)]
            vb = [A.sb([128, 256], BF16, "vb") for _ in range(2)]
            acc = [A.sb([128, 10, GC * 128], BF16, "acc") for _ in range(2)]
            pT = [A.ps([128, 8, 128], BF16, "pT") for _ in range(2)]
            pp = [A.ps([128, 512], F32, "pp") for _ in range(3)]
            pq = A.ps([128, 16, 128], BF16, "pq")
            items = [(s, c) for s in range(NS) for c in range(NT)]

            def do_load(i):
                s, c = items[i]
                P.dma(xt[i % 3], Xin[s, c * 128:(c + 1) * 128, :])
                P.dma(cs[i % 4], C.rope_a[c])

            def do_A1(i):
                front1(P, xt[i % 3], ss[i % 4], junk, xn[i % 2], mhalf, xn_eng="act")

            def do_A2(i):
                s, c = items[i]
                front2(P, xn[i % 2], ident, pT[i % 2], hT[i % 2], a_pp[s], b_pp[s], ev_eng="act")

            def do_B(i):
                s, c = items[i]
                sl = i % 2
                g = (i // GC) % 2
                cc = c % GC
                hv = hT[sl]
                qk32, sq, qkn, t1, t2, t3, t4 = qk32_[sl], sq_[sl], qkn_[sl], t1_[sl], t2_[sl], t3_[sl], t4_[sl]
                qkf = qk32.re("p h d -> p (h d)")
                for ct in range(3):
                    pv = pp[ct]
                    for kc in range(8):
                        P.mm(pv, hv[:, kc, :], W[:, kc, ct * 512:(ct + 1) * 512], start=(kc == 0), stop=(kc == 7))
                    if ct < 2:
                        P.copy(qkf[:, ct * 512:(ct + 1) * 512], pv, eng="act")
                    else:
                        P.copy(qkf[:, 1024:1280], pv[:, 0:256], eng="act")
                        P.copy(vb[sl], pv[:, 256:512], eng="act")
                P.dma(C.VA[s, c], vb[sl])

            def do_C1(i):
                sl = i % 2
                qk32, sq = qk32_[sl], sq_[sl]
                P.act(sq, qk32, AF.Square)
                sv = s10[sl]
                P.reduce(sv[:, 0:10], sq, ALU.add)
                P.ts(sv[:, 10:20], sv[:, 0:10], 1.0 / ATT_HD, EPS, ALU.mult, ALU.add)
                P.tt(sv[:, 20:30], sv[:, 10:20], V(mhalf.ap.broadcast_to([128, 10]), mhalf.key), ALU.pow, eng="pool")

            def do_C(i):
                s, c = items[i]
                sl = i % 2
                qk32, sq, qkn, t1, t2, t3, t4 = qk32_[sl], sq_[sl], qkn_[sl], t1_[sl], t2_[sl], t3_[sl], t4_[sl]
                sv = s10[sl]
                P.tt(qkn, qk32, V(sv.ap[:, 20:30].unsqueeze(2).broadcast_to([128, 10, 128]), sv.key), ALU.mult, eng="dve")
                P.tt(qkn, qkn, gains, ALU.mult, eng="dve")
                v5 = qkn.re("p h (f a d) -> p h f a d", f=2, a=2)
                o5 = qkr[sl].re("p (h f a d) -> p h f a d", h=10, f=2, a=2)
                a_, b_ = v5[:, :, :, 0, :], v5[:, :, :, 1, :]
                csv = cs[i % 4]
                cos = V(csv.ap[:, 0].unsqueeze(1).broadcast_to([128, 10, 2, 32]), csv.key)
                sin = V(csv.ap[:, 1].unsqueeze(1).broadcast_to([128, 10, 2, 32]), csv.key)
                P.tt(t1, a_, cos, ALU.mult, eng="dve")
                P.tt(t3, a_, sin, ALU.mult, eng="pool")
                P.tt(t2, b_, sin, ALU.mult, eng="dve")
                P.tt(t4, b_, cos, ALU.mult, eng="pool")
                P.tt(o5[:, :, :, 0, :], t1, t2, ALU.subtract, eng="dve")
                P.tt(o5[:, :, :, 1, :], t3, t4, ALU.add, eng="pool")

            def do_D(i):
                s, c = items[i]
                sl = i % 2
                g = (i // GC) % 2
                cc = c % GC
                for j in range(10):
                    P.tr(pq[:, j, :], qkr[sl][:, j * 128:(j + 1) * 128], ident)
                P.copy(acc[g][:, :, cc * 128:(cc + 1) * 128], pq[:, 0:10, :], eng="act")
                if cc == GC - 1:
                    c0 = c - (GC - 1)
                    P.dma(C.QTA[s, :, :, c0 * 128:(c0 + GC) * 128], acc[g][:, 0:8, :])
                    P.dma(C.KTA[s, :, :, c0 * 128:(c0 + GC) * 128], acc[g][:, 8:10, :])

            n = len(items)
            do_load(0)
            if n > 1:
                do_load(1)
            do_A1(0)
            do_A2(0)
            for i in range(n + 2):
                if i + 2 < n:
                    do_load(i + 2)
                if 0 <= i - 1 < n:
                    do_C1(i - 1)
                if i + 1 < n:
                    do_A1(i + 1)
                if 0 <= i - 1 < n:
                    do_C(i - 1)
                if 0 <= i - 2 < n:
                    do_D(i - 2)
                if i + 1 < n:
                    do_A2(i + 1)
                if i < n:
                    do_B(i)
            P.emit()


def phase_l1b(C, Xin, Xout):
    import contextlib
    nc = C.nc
    NS, NT, N = C.NSEQ, C.NT, C.N
    QB = 512
    NQB = N // QB
    NP = NT // 2
    with contextlib.ExitStack() as st0:
        A0 = Alloc(nc, st0)
        W = A0.sb([128, 8, D], BF16, "Wao")
        load_weight(C, W, C.att_w_out, D, 8)
        with contextlib.ExitStack() as st:
            A = Alloc(nc, st)
            P = Phase(C.K, "l1b")
            _a, _b, G = load_mod(P, A, C, 1, 1, want_pp=False)
            gq = A.sb([128, 128], F32, "gq")
            gk = A.sb([128, 128], F32, "gk")
            mm_ = A.sb([128, 4], F32, "mm")
            P.dma(gq, C.att_q_gain.partition_broadcast(128))
            P.dma(gk, C.att_k_gain.partition_broadcast(128))
            P.reduce(mm_[:, 0:1], gq, ALU.max, absval=True)
            P.reduce(mm_[:, 1:2], gk, ALU.max, absval=True)
            P.tt(mm_[:, 2:3], mm_[:, 0:1], mm_[:, 1:2], ALU.mult)
            P.ts(mm_[:, 3:4], mm_[:, 2:3], -math.sqrt(ATT_HD), None, ALU.mult)
            negm = mm_[:, 3:4]
            ones32 = A.sb([128, 128], F32, "ones32")
            P.memset(ones32, 1.0)
            onesb = A.sb([128, 128], BF16, "onesb")
            P.memset(onesb, 1.0)
            pe_us = [u for u in range(NP) if u % 4 == 3] if NP >= 8 else []
            kt = [A.sb([128, 2, N], BF16, "kt") for _ in range(2)]
            va = [A.sb([128, NT, 256], BF16, "va") for _ in range(2)]
            qt = [A.sb([128, 8, QB], BF16, "qt") for _ in range(2)]
            PT = [A.sb([128, 2, QB], BF16, "PT") for _ in range(3)]
            acc = [A.sb([128, 2, QB], F32, "acc") for _ in range(2)]
            accp = [A.sb([128, 2, QB], F32, "accp") for _ in range(2)]
            POOL_SHARE = False
            pool_us = [u for u in range(NP) if u % 4 == 3] if (NP >= 8 and POOL_SHARE) else []
            OT = [A.sb([128, 8, QB], BF16, "OT") for _ in range(2)]
            rinv = [A.sb([128, QB], F32, "rinv") for _ in range(2)]
            xt = [A.sb([128, D], F32, "xt") for _ in range(8)]
            tmp = [A.sb([128, 512], F32, "tmp") for _ in range(2)]
            ps_s = [A.ps([128, 2, 512], F32, "ps_s") for _ in range(2)]
            po = [A.ps([128, 512], F32, "po") for _ in range(2)]
            prs = A.ps([128, 512], F32, "prs")
            pw = A.ps([128, 512], F32, "pw")
            scale = float(ATT_HD) ** -0.5
            blocks = [(s, qb) for s in range(NS) for qb in range(NQB)]
            units = [(bi, h, u) for bi in range(len(blocks)) for h in range(ATT_QH) for u in range(NP)]

            def load_seq(s):
                P.dma(kt[s % 2], C.KTA[s])
                P.dma(va[s % 2], C.VA[s].rearrange("c p d -> p c d"))

            def load_block(bi):
                s, qb = blocks[bi]
                sl = bi % 2
                P.dma(qt[sl], C.QTA[s, :, :, qb * QB:(qb + 1) * QB])
                for j in range(4):
                    c = qb * 4 + j
                    P.dma(xt[sl * 4 + j], Xin[s, c * 128:(c + 1) * 128, :])

            def qk(i):
                bi, h, u = units[i]
                s, qb = blocks[bi]
                for t in range(2):
                    kc = 2 * u + t
                    P.mm(ps_s[i % 2][:, t, :], kt[s % 2][:, h // 4, kc * 128:(kc + 1) * 128], qt[bi % 2][:, h, :])

            deferred = []
            hfq = []

            def head_final(bi, h):
                def f():
                    a = acc[h % 2]
                    if pool_us:
                        P.tt(a, a, accp[h % 2], ALU.add)
                    for t in range(2):
                        P.mm(prs, ones32, a[:, t, :], start=(t == 0 and not pe_us), stop=(t == 1))
                    P.recip(rinv[h % 2], prs)
                    P.tt(OT[bi % 2][:, h, :], po[h % 2], rinv[h % 2], ALU.mult)
                return f

            def out_group(bi, j, half):
                def f():
                    s, qb = blocks[bi]
                    c = qb * 4 + j
                    xv = xt[(bi % 2) * 4 + j]
                    for h in range(8):
                        P.mm(pw, OT[bi % 2][:, h, j * 128:(j + 1) * 128], W[:, h, half * 512:(half + 1) * 512],
                             start=(h == 0), stop=(h == 7))
                    P.tt(tmp[half], pw, G[s][:, half * 512:(half + 1) * 512], ALU.mult)
                    P.tt(xv[:, half * 512:(half + 1) * 512], tmp[half], xv[:, half * 512:(half + 1) * 512],
                         ALU.add, eng="pool")
                    if half == 1:
                        P.dma(Xout[s, c * 128:(c + 1) * 128, :], xv)
                return f

            def pv_(i):
                bi, h, u = units[i]
                s, qb = blocks[bi]
                pt = PT[i % 3]
                import os
                if os.environ.get("DBG_EXP1"):
                    for t in range(2):
                        P.act(pt[:, t, :], ps_s[i % 2][:, t, :], AF.Exp, bias=negm, scale=scale)
                else:
                    P.act(pt, ps_s[i % 2], AF.Exp, bias=negm, scale=scale)
                for t in range(2):
                    kc = 2 * u + t
                    P.mm(po[h % 2], va[s % 2][:, kc, (h // 4) * 128:(h // 4 + 1) * 128], pt[:, t, :],
                         start=(kc == 0), stop=(kc == NT - 1))
                if u in pe_us:
                    for t in range(2):
                        P.mm(prs, onesb, pt[:, t, :], start=(u == pe_us[0] and t == 0), stop=False)
                elif u in pool_us:
                    if u == pool_us[0]:
                        P.copy(accp[h % 2], pt, eng="pool")
                    else:
                        P.tt(accp[h % 2], accp[h % 2], pt, ALU.add, eng="pool")
                elif u == 0:
                    P.copy(acc[h % 2], pt, eng="dve")
                else:
                    P.tt(acc[h % 2], acc[h % 2], pt, ALU.add)
                if u == NP - 1:
                    hfq.append(head_final(bi, h))
                    if h == ATT_QH - 1:
                        for j in range(4):
                            for half in range(2):
                                deferred.append(out_group(bi, j, half))

            nu = len(units)
            load_seq(0)
            load_block(0)
            loaded = 0
            qk(0)
            if nu > 1:
                qk(1)
            for i in range(nu):
                bi, h, u = units[i]
                if h == 0 and u == 0:
                    s, qb = blocks[bi]
                    if qb == 0 and s + 1 < NS:
                        load_seq(s + 1)
                pv_(i)
                if i + 2 < nu:
                    nb = units[i + 2][0]
                    if nb > loaded:
                        while hfq:
                            hfq.pop(0)()
                        while deferred:
                            deferred.pop(0)()
                        load_block(nb)
                        loaded = nb
                    qk(i + 2)
                if hfq and u == 0:
                    hfq.pop(0)()
                elif deferred and (u % 2 == 1) and not hfq:
                    deferred.pop(0)()
                if not deferred and loaded < bi + 1 and bi + 1 < len(blocks) and not (h == ATT_QH - 1 and u == NP - 1):
                    load_block(bi + 1)
                    loaded = bi + 1
            while hfq:
                hfq.pop(0)()
            while deferred:
                deferred.pop(0)()
            P.emit()


def rope_table(n_tokens, nf):
    t = np.arange(n_tokens)
    row = (t // GRID_W).astype(np.float32)
    col = (t % GRID_W).astype(np.float32)
    inv = (np.float32(ROPE_THETA) ** (-(np.arange(nf, dtype=np.float32)) / np.float32(nf))).astype(np.float32)
    ar = (row[:, None] * inv[None, :]).astype(np.float32)
    ac = (col[:, None] * inv[None, :]).astype(np.float32)
    tab = np.stack([np.stack([np.cos(ar), np.cos(ac)], 1), np.stack([np.sin(ar), np.sin(ac)], 1)], 1)
    return np.ascontiguousarray(tab.reshape(n_tokens // 128, 128, 2, 2, nf).astype(np.float32))


def build(NSEQ, N, phases=None):
    nc = bass.Bass("TRN2", target_bir_lowering=False)
    C = Ctx()
    C.nc = nc
    C.NSEQ = NSEQ
    C.N = N
    C.NT = N // 128
    NT = C.NT

    def inp(name, shape, dt=F32):
        return nc.dram_tensor(name, list(shape), dt, kind="ExternalInput").ap()

    C.x = inp("x", [NSEQ, N, D])
    C.cT = inp("cT", [128, 8, NSEQ])
    C.mod_w = inp("mod_w", [2, D, 6 * D])
    C.mod_b = inp("mod_b", [2, 6 * D])
    C.norm1_g = inp("norm1_g", [2, D])
    C.norm2_g = inp("norm2_g", [2, D])
    C.ret_w_in = inp("ret_w_in", [D, RET_IN])
    C.ret_decay = inp("ret_decay", [8])
    C.ret_w_out = inp("ret_w_out", [2 * D, D])
    C.att_w_in = inp("att_w_in", [D, ATT_IN])
    C.att_q_gain = inp("att_q_gain", [ATT_HD])
    C.att_k_gain = inp("att_k_gain", [ATT_HD])
    C.att_w_out = inp("att_w_out", [D, D])
    C.mlp_w1 = inp("mlp_w1", [2, D, DFF])
    C.mlp_w2 = inp("mlp_w2", [2, DFF, D])
    C.final_g = inp("final_g", [D])
    C.ident = inp("ident", [128, 128])
    C.rope_r = inp("rope_r", [NT, 128, 2, 2, 64])
    C.rope_a = inp("rope_a", [NT, 128, 2, 2, 32])
    C.y = nc.dram_tensor("y", [NSEQ, N, D], F32, kind="ExternalOutput").ap()

    def scr(name, shape, dt):
        return nc.dram_tensor(name, list(shape), dt).ap()

    C.MOD = scr("MOD", [2, NSEQ, 6 * D], F32)
    C.XA = scr("XA", [NSEQ, N, D], F32)
    C.XB = scr("XB", [NSEQ, N, D], F32)
    C.QT = scr("QT", [NSEQ, NT, 3, 128, 4, 2, 128], BF16)
    C.KT = scr("KT", [NSEQ, NT, 128, 4, 2, 128], BF16)
    C.KF = scr("KF", [NSEQ, NT, 128, 4, 256], BF16)
    C.KB = scr("KB", [NSEQ, NT, 128, 4, 256], BF16)
    C.VS = scr("VS", [NSEQ, NT, 128, 4, 512], BF16)
    C.SG = scr("SG", [NSEQ, NT, 128, 4, 512], BF16)
    C.YT = scr("YT", [NSEQ, 128, 16, N], BF16)
    C.QTA = scr("QTA", [NSEQ, 128, 8, N], BF16)
    C.KTA = scr("KTA", [NSEQ, 128, 2, N], BF16)
    C.VA = scr("VA", [NSEQ, NT, 128, 256], BF16)
    C.K = Kern(nc)
    if phases is None:
        phases = ["mod", "l0a", "l0b", "l0c", "mlp0", "l1a", "l1b", "mlp1"]
    C.phases = phases
    for ph in phases:
        if ph == "mod":
            phase_mod(C)
        elif ph == "l0a":
            phase_l0a(C)
        elif ph == "l0b":
            phase_l0b(C)
        elif ph == "l0c":
            phase_l0c(C, C.x, C.XA)
        elif ph == "mlp0":
            phase_mlp(C, 0, C.XA, C.XB, False)
        elif ph == "l1a":
            phase_l1a(C, C.XB)
        elif ph == "l1b":
            phase_l1b(C, C.XB, C.XA)
        elif ph == "mlp1":
            phase_mlp(C, 1, C.XA, C.y, True)
        elif ph == "l1a_x":
            phase_l1a(C, C.x)
        elif ph == "l1b_xy":
            phase_l1b(C, C.x, C.y)
        elif ph == "l0c_y":
            phase_l0c(C, C.x, C.y)
        elif ph == "mlp_only":
            phase_mlp(C, 0, C.x, C.y, True)
        else:
            raise ValueError(ph)
    return nc, C


def make_in_maps(inputs, NSEQ_P=2, NSEQ_S=1, ncores=8):
    f = lambda a: np.ascontiguousarray(np.asarray(a, dtype=np.float32))
    xp, xs_ = f(inputs["x_prompt"]), f(inputs["x_sample"])
    cp, cs = f(inputs["c_prompt"]), f(inputs["c_sample"])
    N = xp.shape[1]
    shared = {
        "mod_w": f(inputs["mod_w"]), "mod_b": f(inputs["mod_b"]),
        "norm1_g": f(inputs["norm1_g"]), "norm2_g": f(inputs["norm2_g"]),
        "ret_w_in": f(inputs["ret_w_in"])[0], "ret_decay": f(inputs["ret_decay"])[0].reshape(8),
        "ret_w_out": f(inputs["ret_w_out"])[0], "att_w_in": f(inputs["att_w_in"])[0],
        "att_q_gain": f(inputs["att_q_gain"])[0], "att_k_gain": f(inputs["att_k_gain"])[0],
        "att_w_out": f(inputs["att_w_out"])[0], "mlp_w1": f(inputs["mlp_w1"]), "mlp_w2": f(inputs["mlp_w2"]),
        "final_g": f(inputs["final_g"]),
        "ident": np.eye(128, dtype=np.float32),
        "rope_r": rope_table(N, 64), "rope_a": rope_table(N, 32),
    }
    maps = []
    for i in range(ncores):
        x = np.concatenate([xp[NSEQ_P * i:NSEQ_P * (i + 1)], xs_[NSEQ_S * i:NSEQ_S * (i + 1)]], 0)
        c = np.concatenate([cp[NSEQ_P * i:NSEQ_P * (i + 1)], cs[NSEQ_S * i:NSEQ_S * (i + 1)]], 0)
        cT = np.ascontiguousarray(c.reshape(c.shape[0], 8, 128).transpose(2, 1, 0))
        m = dict(shared)
        m["x"] = np.ascontiguousarray(x)
        m["cT"] = cT
        maps.append(m)
    return maps


def kernel(**inputs):
    maps = make_in_maps(inputs)
    N = maps[0]["x"].shape[1]
    nc, C = build(3, N)
    res = run_bass_kernel_spmd(nc, maps, core_ids=list(range(8)))
    ys = [np.asarray(r["y"]) for r in res.results]
    y_prompt = np.concatenate([y[0:2] for y in ys], 0).astype(np.float32)
    y_sample = np.concatenate([y[2:3] for y in ys], 0).astype(np.float32)
    return (y_prompt, y_sample)
```

```python
import math
import numpy as np
import concourse.bass as bass
import concourse.mybir as mybir
from concourse.bass_utils import run_bass_kernel_spmd
from concourse.alu_op_type import AluOpType as ALU

AF = mybir.ActivationFunctionType
F32 = mybir.dt.float32
BF16 = mybir.dt.bfloat16
AX = mybir.AxisListType

D = 1024
DFF = 4096
EPS = 1e-6
RET_H = 4
RET_DK = 256
RET_DV = 512
RET_IN = 6144
ATT_HD = 128
ATT_QH = 8
ATT_KVH = 2
ATT_IN = 1536
GRID_W = 64
ROPE_THETA = 10000.0


class V:
    __slots__ = ("ap", "key")

    def __init__(self, ap, key):
        self.ap = ap
        self.key = key

    def __getitem__(self, idx):
        return V(self.ap[idx], self.key)

    def sub(self, s):
        return V(self.ap, (self.key, s))

    def re(self, pat, **kw):
        return V(self.ap.rearrange(pat, **kw), self.key)

    def bc(self, shape):
        return V(self.ap.broadcast_to(shape), self.key)


def _keys(*xs):
    out = []
    for x in xs:
        if isinstance(x, V) and x.key is not None:
            out.append(x.key)
    return out


def _a(x):
    return x.ap if isinstance(x, V) else x


class Op:
    __slots__ = ("eng", "fn", "reads", "writes", "dma", "signal", "sem", "count", "deps", "pre")

    def __init__(self, eng, fn, reads, writes, dma):
        self.eng = eng
        self.fn = fn
        self.reads = reads
        self.writes = writes
        self.dma = dma
        self.signal = dma
        self.sem = None
        self.count = 0
        self.deps = ()
        self.pre = None


ENGS = ("pe", "act", "dve", "pool", "sp")


class Kern:
    def __init__(self, nc, ndma=32):
        self.nc = nc
        self.ndma = ndma
        self.dma_sems = [nc.alloc_semaphore(name=f"dq{i}") for i in range(ndma)]
        self.dma_n = 0
        self.dma_cnt = [0] * ndma
        self.phase_i = 0
        self.waited = {e: {} for e in ENGS}
        self.n_inst = 0

    def eng(self, name):
        nc = self.nc
        return {"pe": nc.tensor, "act": nc.scalar, "dve": nc.vector, "pool": nc.gpsimd, "sp": nc.sync}[name]


class Phase:
    def __init__(self, K, name):
        self.K = K
        self.name = name
        self.ops = []
        self._uid = 0

    def uid(self):
        self._uid += 1
        return self._uid

    def add(self, eng, fn, reads=(), writes=(), dma=False):
        self.ops.append(Op(eng, fn, tuple(reads), tuple(writes), dma))

    def mm(self, out, lhsT, rhs, start=True, stop=True):
        self.add("pe", lambda e: e.matmul(out.ap, lhsT.ap, rhs.ap, start=start, stop=stop),
                 _keys(lhsT, rhs), _keys(out))

    def tr(self, out, in_, ident):
        self.add("pe", lambda e: e.transpose(out.ap, in_.ap, ident.ap), _keys(in_, ident), _keys(out))

    def act(self, out, in_, func, bias=None, scale=None, accum=None, eng="act"):
        kw = {}
        if bias is not None:
            kw["bias"] = _a(bias)
        if scale is not None:
            kw["scale"] = _a(scale)
        if accum is not None:
            kw["accum_out"] = accum.ap
        self.add(eng, lambda e: e.activation(out.ap, in_.ap, func, **kw),
                 _keys(in_, bias, scale), _keys(out, accum))

    def ts(self, out, in0, s1, s2, op0, op1=None, eng="dve", accum=None):
        kw = {}
        if op1 is not None:
            kw["op1"] = op1
        if accum is not None:
            kw["accum_out"] = accum.ap
        self.add(eng, lambda e: e.tensor_scalar(out.ap, in0.ap, _a(s1), _a(s2), op0, **kw),
                 _keys(in0, s1, s2), _keys(out, accum))

    def tt(self, out, in0, in1, op, eng="dve"):
        self.add(eng, lambda e: e.tensor_tensor(out.ap, in0.ap, in1.ap, op), _keys(in0, in1), _keys(out))

    def stt(self, out, in0, scalar, in1, op0, op1, eng="dve"):
        self.add(eng, lambda e: e.scalar_tensor_tensor(out.ap, in0.ap, _a(scalar), in1.ap, op0, op1),
                 _keys(in0, scalar, in1), _keys(out))

    def copy(self, out, in_, eng="dve"):
        if eng == "act":
            self.add(eng, lambda e: e.copy(out.ap, in_.ap), _keys(in_), _keys(out))
        else:
            self.add(eng, lambda e: e.tensor_copy(out.ap, in_.ap), _keys(in_), _keys(out))

    def memset(self, out, val, eng="pool"):
        self.add(eng, lambda e: e.memset(out.ap, val), (), _keys(out))

    def recip(self, out, in_, eng="dve"):
        self.add(eng, lambda e: e.reciprocal(out.ap, in_.ap), _keys(in_), _keys(out))

    def reduce(self, out, in_, op, axis=AX.X, eng="dve", absval=None):
        self.add(eng, lambda e: e.tensor_reduce(out.ap, in_.ap, axis, op, apply_absolute_value=absval),
                 _keys(in_), _keys(out))

    def bn_stats(self, out, in_):
        self.add("dve", lambda e: e.bn_stats(out.ap, in_.ap), _keys(in_), _keys(out))

    def bn_aggr(self, out, in_):
        self.add("dve", lambda e: e.bn_aggr(out.ap, in_.ap), _keys(in_), _keys(out))

    def iota(self, out, pattern, base, cm):
        self.add("pool", lambda e: e.iota(out.ap, pattern, base=base, channel_multiplier=cm,
                                          allow_small_or_imprecise_dtypes=True), (), _keys(out))

    def dma(self, out, in_, slow=False):
        kw = {"allow_slow_non_contiguous": True} if slow else {}
        self.add("sp", lambda e: e.dma_start(_a(out), _a(in_), **kw), _keys(in_), _keys(out), dma=True)

    def emit(self):
        K = self.K
        nc = K.nc
        ops = self.ops
        last_w = {}
        rd = {}
        for i, op in enumerate(ops):
            deps = {}

            def add(j, kind, op=op, deps=deps, i=i):
                if j is None or j == i:
                    return
                Pp = ops[j]
                if Pp.eng == op.eng and not Pp.dma and not op.dma:
                    if op.eng == "pe":
                        return
                deps[j] = True

            for k in op.reads:
                add(last_w.get(k), "RAW")
            for k in op.writes:
                add(last_w.get(k), "WAW")
                r = rd.get(k)
                if r:
                    for j in r.values():
                        add(j, "WAR")
            op.deps = tuple(deps)
            for j in deps:
                ops[j].signal = True
            for k in op.reads:
                r = rd.get(k)
                if r is None:
                    r = rd[k] = {}
                r[("d", i) if op.dma else op.eng] = i
            for k in op.writes:
                last_w[k] = i
                rd[k] = {}
        if self.name == "w":
            if not hasattr(K, "w_sems"):
                K.w_sems = {e: nc.alloc_semaphore(name=f"phw_{e}") for e in ("pe", "act", "dve", "pool")}
                K.w_cnt = {e: 0 for e in K.w_sems}
            esem, ecnt = K.w_sems, K.w_cnt
        else:
            esem = {e: nc.alloc_semaphore(name=f"ph{K.phase_i}_{e}") for e in ("pe", "act", "dve", "pool")}
            K.phase_i += 1
            ecnt = {e: 0 for e in esem}
        for op in ops:
            if op.dma:
                n = K.dma_n
                K.dma_n += 1
                s = n % K.ndma
                K.dma_cnt[s] += 16
                op.sem = K.dma_sems[s]
                op.count = K.dma_cnt[s]
                if op.count > 16:
                    op.pre = (op.sem, op.count - 16)
            elif op.signal:
                ecnt[op.eng] += 1
                op.sem = esem[op.eng]
                op.count = ecnt[op.eng]
        by_eng = {e: [] for e in ENGS}
        for op in ops:
            by_eng[op.eng].append(op)
        K.n_inst += len(ops)

        def run(e, name):
            waited = K.waited[name]
            for op in by_eng[name]:
                need = {}
                if op.pre is not None:
                    need[id(op.pre[0])] = op.pre
                for j in op.deps:
                    Pp = ops[j]
                    cur = need.get(id(Pp.sem))
                    if cur is None or cur[1] < Pp.count:
                        need[id(Pp.sem)] = (Pp.sem, Pp.count)
                for sid, (sem, val) in need.items():
                    if waited.get(sid, 0) < val:
                        e.wait_ge(sem, val)
                        waited[sid] = val
                inst = op.fn(e)
                if op.dma:
                    inst.then_inc(op.sem, 16)
                elif op.signal:
                    inst.then_inc(op.sem, 1)
            if name == "sp":
                for s in range(K.ndma):
                    if K.dma_cnt[s] > waited.get(id(K.dma_sems[s]), 0):
                        e.wait_ge(K.dma_sems[s], K.dma_cnt[s])
                        waited[id(K.dma_sems[s])] = K.dma_cnt[s]

        with nc.Block() as block:
            @block.tensor
            def _(e):
                run(e, "pe")

            @block.scalar
            def _(e):
                run(e, "act")

            @block.vector
            def _(e):
                run(e, "dve")

            @block.gpsimd
            def _(e):
                run(e, "pool")

            @block.sync
            def _(e):
                run(e, "sp")
        self.ops = []


class Alloc:
    _n = [0]

    def __init__(self, nc, stack):
        self.nc = nc
        self.stack = stack

    def sb(self, shape, dt, name=None):
        Alloc._n[0] += 1
        nm = f"{name or 't'}_{Alloc._n[0]}"
        t = self.stack.enter_context(self.nc.sbuf_tensor(nm, list(shape), dt))
        return V(t[:] if hasattr(t, "__getitem__") else t.ap(), nm)

    def ps(self, shape, dt, name=None):
        Alloc._n[0] += 1
        nm = f"{name or 'p'}_{Alloc._n[0]}"
        t = self.stack.enter_context(self.nc.psum_tensor(nm, list(shape), dt))
        return V(t[:] if hasattr(t, "__getitem__") else t.ap(), nm)


class Ctx:
    pass


def load_weight(C, dst, src, ncols, kcs):
    import contextlib
    nc = C.nc
    with contextlib.ExitStack() as st:
        A = Alloc(nc, st)
        CH = min(ncols, 2048)
        stage = [A.sb([128, CH], F32, "wst") for _ in range(3)]
        P = Phase(C.K, "w")
        i = 0
        for kc in range(kcs):
            for c0 in range(0, ncols, CH):
                sv = stage[i % 3]
                P.dma(sv, src[kc * 128:(kc + 1) * 128, c0:c0 + CH])
                P.copy(dst[:, kc, c0:c0 + CH], sv, eng=("dve" if i % 2 == 0 else "pool"))
                i += 1
        P.emit()


def load_consts(P, A, C):
    ident32 = A.sb([128, 128], F32, "id32")
    ident = A.sb([128, 128], BF16, "id")
    mhalf = A.sb([128, 1], F32, "mh")
    P.dma(ident32, C.ident)
    P.copy(ident, ident32)
    P.memset(mhalf, -0.5)
    return ident, mhalf


def load_mod(P, A, C, layer, which, want_pp=True, want_gate=True):
    off = 0 if which == 1 else 3 * D
    ng = C.norm1_g if which == 1 else C.norm2_g
    gpp = A.sb([128, 8], F32, "gpp")
    P.dma(gpp, ng[layer].rearrange("(kc p) -> p kc", p=128), slow=True)
    a_pp, b_pp, G = [], [], []
    for s in range(C.NSEQ):
        row = C.MOD[layer, s]
        if want_pp:
            sc = A.sb([128, 8], F32, "scpp")
            a = A.sb([128, 8], F32, "app")
            b = A.sb([128, 8], F32, "bpp")
            P.dma(b, row[off:off + D].rearrange("(kc p) -> p kc", p=128), slow=True)
            P.dma(sc, row[off + D:off + 2 * D].rearrange("(kc p) -> p kc", p=128), slow=True)
            P.stt(a, sc, 1.0, gpp, ALU.add, ALU.mult)
            a_pp.append(a)
            b_pp.append(b)
        if want_gate:
            g = A.sb([128, D], F32, "gate")
            P.dma(g, row[off + 2 * D:off + 3 * D].partition_broadcast(128))
            G.append(g)
    return a_pp, b_pp, G


def front1(P, xt, ss, junk, xn, mhalf, xn_eng="dve"):
    P.act(junk, xt, AF.Square, accum=ss[:, 0:1])
    P.ts(ss[:, 1:2], ss[:, 0:1], 1.0 / D, EPS, ALU.mult, ALU.add)
    P.tt(ss[:, 2:3], ss[:, 1:2], mhalf, ALU.pow, eng="pool")
    if xn_eng == "act":
        P.act(xn, xt, AF.Identity, scale=ss[:, 2:3])
    else:
        P.ts(xn, xt, ss[:, 2:3], None, ALU.mult)


def front2(P, xn, ident, pT, hT, a_pp, b_pp, ev_eng="dve"):
    for kc in range(8):
        P.tr(pT[:, kc, :], xn[:, kc * 128:(kc + 1) * 128], ident)
    for kc in range(8):
        if ev_eng == "act":
            P.act(hT[:, kc, :], pT[:, kc, :], AF.Identity, bias=b_pp[:, kc:kc + 1], scale=a_pp[:, kc:kc + 1])
        else:
            P.ts(hT[:, kc, :], pT[:, kc, :], a_pp[:, kc:kc + 1], b_pp[:, kc:kc + 1], ALU.mult, ALU.add)


def front(P, xt, ss, junk, xn, ident, mhalf, pT, hT, a_pp, b_pp, ev_eng="dve"):
    front1(P, xt, ss, junk, xn, mhalf)
    front2(P, xn, ident, pT, hT, a_pp, b_pp, ev_eng)


def phase_mod(C):
    import contextlib
    nc = C.nc
    NS = C.NSEQ
    with contextlib.ExitStack() as st:
        A = Alloc(nc, st)
        P = Phase(C.K, "mod")
        ct32 = A.sb([128, 8, NS], F32)
        sct = A.sb([128, 8, NS], BF16)
        P.dma(ct32, C.cT)
        P.act(sct, ct32, AF.Silu)
        stage = [A.sb([128, 8, 512], F32, "mst") for _ in range(2)]
        wb = [A.sb([128, 8, 512], BF16, "mwb") for _ in range(2)]
        bias = A.sb([NS, 2, 6 * D], F32)
        res = A.sb([NS, 2, 6 * D], F32)
        pp = [A.ps([128, 512], F32) for _ in range(2)]
        for l in range(2):
            for s in range(NS):
                P.dma(bias[s:s + 1, l, :], C.mod_b[l:l + 1, :])
        i = 0
        for l in range(2):
            for nt in range(12):
                sv, wv, pv = stage[i % 2], wb[i % 2], pp[i % 2]
                P.dma(sv, C.mod_w[l, :, nt * 512:(nt + 1) * 512].rearrange("(kc p) n -> p kc n", p=128))
                P.copy(wv, sv, eng=("dve" if i % 2 == 0 else "pool"))
                for kc in range(8):
                    P.mm(pv[0:NS, :], sct[:, kc, :], wv[:, kc, :], start=(kc == 0), stop=(kc == 7))
                P.tt(res[:, l, nt * 512:(nt + 1) * 512], pv[0:NS, :], bias[:, l, nt * 512:(nt + 1) * 512], ALU.add)
                i += 1
        for l in range(2):
            P.dma(C.MOD[l], res[:, l, :])
        P.emit()


def phase_mlp(C, layer, Xin, Xout, final):
    import contextlib
    nc = C.nc
    NS, NT = C.NSEQ, C.NT
    TS = 2
    with contextlib.ExitStack() as st0:
        A0 = Alloc(nc, st0)
        W1 = A0.sb([128, 8, DFF], BF16, "W1")
        W2 = A0.sb([128, 32, D], BF16, "W2")
        load_weight(C, W1, C.mlp_w1[layer], DFF, 8)
        load_weight(C, W2, C.mlp_w2[layer], D, 32)
        with contextlib.ExitStack() as st:
            A = Alloc(nc, st)
            P = Phase(C.K, "mlp")
            ident, mhalf = load_consts(P, A, C)
            a_pp, b_pp, G = load_mod(P, A, C, layer, 2)
            if final:
                FG = A.sb([128, D], F32, "FG")
                P.dma(FG, C.final_g.partition_broadcast(128))
            xt = [A.sb([128, D], F32, "xt") for _ in range(2 * TS)]
            xn = [A.sb([128, D], BF16, "xn") for _ in range(2)]
            junk = A.sb([128, D], BF16, "junk")
            ss = [A.sb([128, 4], F32, "ss") for _ in range(4)]
            hT = [A.sb([128, 8, TS * 128], BF16, "hT") for _ in range(2)]
            uT = A.sb([128, 32, TS * 128], BF16, "uT")
            r32 = [A.sb([128, TS * 128], F32, "r32") for _ in range(2)]
            tmp = [A.sb([128, 512], F32, "tmp") for _ in range(2)]
            pT = [A.ps([128, 8, 128], BF16, "pT") for _ in range(2)]
            pu = [A.ps([128, 512], F32, "pu") for _ in range(3)]
            po = [A.ps([128, 512], F32, "po") for _ in range(3)]
            items = [(s, sc) for s in range(NS) for sc in range(NT // TS)]
            cnt = [0]

            def xs_of(i):
                return [xt[(i % 2) * TS + j] for j in range(TS)]

            def do_load(i):
                s, sc = items[i]
                for j in range(TS):
                    c = sc * TS + j
                    P.dma(xs_of(i)[j], Xin[s, c * 128:(c + 1) * 128, :])

            fk = {}

            def do_front1(i):
                ks = []
                for j in range(TS):
                    k = cnt[0]
                    cnt[0] += 1
                    ks.append(k)
                    front1(P, xs_of(i)[j], ss[k % 4], junk, xn[k % 2], mhalf)
                fk[i] = ks

            def do_front2(i):
                s, sc = items[i]
                for j in range(TS):
                    k = fk[i][j]
                    front2(P, xn[k % 2], ident, pT[k % 2], hT[i % 2][:, :, j * 128:(j + 1) * 128],
                           a_pp[s], b_pp[s], ev_eng=("act" if k % 2 else "dve"))

            def do_w1(i):
                hv = hT[i % 2]
                for fc in range(32):
                    pv = pu[fc % 3]
                    for kc in range(8):
                        P.mm(pv[:, 0:TS * 128], W1[:, kc, fc * 128:(fc + 1) * 128], hv[:, kc, :],
                             start=(kc == 0), stop=(kc == 7))
                    rv = r32[fc % 2]
                    P.act(rv, pv[:, 0:TS * 128], AF.Relu)
                    P.tt(uT[:, fc, :], rv, rv, ALU.mult, eng=("dve" if fc % 2 == 0 else "pool"))

            def do_w2(i):
                s, sc = items[i]
                xs = xs_of(i)
                for j in range(TS):
                    for half in range(2):
                        pv = po[(j * 2 + half) % 3]
                        for fc in range(32):
                            P.mm(pv, uT[:, fc, j * 128:(j + 1) * 128], W2[:, fc, half * 512:(half + 1) * 512],
                                 start=(fc == 0), stop=(fc == 31))
                        tv = tmp[half]
                        P.tt(tv, pv, G[s][:, half * 512:(half + 1) * 512], ALU.mult)
                        P.tt(xs[j][:, half * 512:(half + 1) * 512], tv, xs[j][:, half * 512:(half + 1) * 512],
                             ALU.add, eng="pool")
                    c = sc * TS + j
                    if final:
                        k = cnt[0]
                        cnt[0] += 1
                        sv = ss[k % 4]
                        P.act(junk, xs[j], AF.Square, accum=sv[:, 0:1])
                        P.ts(sv[:, 1:2], sv[:, 0:1], 1.0 / D, EPS, ALU.mult, ALU.add)
                        P.tt(sv[:, 2:3], sv[:, 1:2], mhalf, ALU.pow, eng="pool")
                        P.stt(xs[j], xs[j], sv[:, 2:3], FG, ALU.mult, ALU.mult)
                    P.dma(Xout[s, c * 128:(c + 1) * 128, :], xs[j])

            n = len(items)
            do_load(0)
            do_front1(0)
            do_front2(0)
            for i in range(n):
                if i + 1 < n:
                    do_load(i + 1)
                    do_front1(i + 1)
                do_w1(i)
                if i + 1 < n:
                    do_front2(i + 1)
                do_w2(i)
            P.emit()


def decay_tables(P, A, C, want):
    T = {}
    dl = A.sb([128, 8], F32, "dl")
    lg = A.sb([128, 8], F32, "lg")
    P.dma(dl, C.ret_decay.partition_broadcast(128))
    P.act(lg, dl, AF.Exp, scale=-1.0)
    P.ts(lg, lg, 1.0, None, ALU.add)
    P.act(lg, lg, AF.Ln)
    P.ts(lg, lg, -1.0, None, ALU.mult)
    T["lg"] = lg
    pidx = A.sb([128, 1], F32, "pidx")
    P.iota(pidx, [[0, 1]], 0, 1)
    jidx = A.sb([128, 128], F32, "jidx")
    P.iota(jidx, [[1, 128]], 0, 0)
    scale = float(RET_DK) ** -0.5
    if "pp" in want:
        rp = A.sb([128, 1], F32, "rp")
        P.ts(rp, pidx, -1.0, 127.0, ALU.mult, ALU.add)
        kdf = A.sb([128, 4], F32, "kdf")
        kdb = A.sb([128, 4], F32, "kdb")
        for h in range(4):
            P.act(kdf[:, h:h + 1], rp, AF.Exp, scale=lg[:, h:h + 1])
            P.act(kdb[:, h:h + 1], pidx, AF.Exp, scale=lg[:, 4 + h:5 + h])
        T["kdf"], T["kdb"] = kdf, kdb
        j1 = A.sb([128, 128], F32, "j1")
        jr = A.sb([128, 128], F32, "jr")
        P.ts(j1, jidx, 1.0, None, ALU.add)
        P.ts(jr, jidx, -1.0, 128.0, ALU.mult, ALU.add)
        qdf = A.sb([128, 8, 128], F32, "qdf")
        qdb = A.sb([128, 8, 128], F32, "qdb")
        for h in range(4):
            for dc in range(2):
                P.act(qdf[:, 2 * h + dc, :], j1, AF.Exp, scale=lg[:, h:h + 1])
                P.act(qdb[:, 2 * h + dc, :], jr, AF.Exp, scale=lg[:, 4 + h:5 + h])
        P.ts(qdf, qdf, scale, None, ALU.mult)
        P.ts(qdb, qdb, scale, None, ALU.mult)
        T["qdf"], T["qdb"] = qdf, qdb
    if "mask" in want:
        diff = A.sb([128, 128], F32, "diff")
        P.ts(diff, jidx, pidx, None, ALU.subtract)
        dpos = A.sb([128, 128], F32, "dpos")
        dneg = A.sb([128, 128], F32, "dneg")
        mge = A.sb([128, 128], F32, "mge")
        P.ts(dpos, diff, 0.0, None, ALU.max)
        P.tt(dneg, dpos, diff, ALU.subtract)
        P.ts(mge, diff, 0.0, None, ALU.is_ge)
        DT = A.sb([128, 4, 128], F32, "DT")
        ea = A.sb([128, 128], F32, "ea")
        eb = A.sb([128, 128], F32, "eb")
        for h in range(4):
            P.act(ea, dpos, AF.Exp, scale=lg[:, h:h + 1])
            P.act(eb, dneg, AF.Exp, scale=lg[:, 4 + h:5 + h])
            P.tt(ea, ea, eb, ALU.subtract)
            P.tt(ea, ea, mge, ALU.mult)
            P.tt(ea, ea, eb, ALU.add)
            P.ts(DT[:, h, :], ea, scale, None, ALU.mult)
        T["DT"] = DT
        cd = A.sb([128, 8], F32, "cdec")
        P.act(cd, lg, AF.Exp, scale=128.0)
        T["cdec"] = cd
    return T


def phase_l0a(C):
    import contextlib
    nc = C.nc
    NS, NT = C.NSEQ, C.NT
    with contextlib.ExitStack() as st0:
        A0 = Alloc(nc, st0)
        W = A0.sb([128, 8, RET_IN], BF16, "Win")
        load_weight(C, W, C.ret_w_in, RET_IN, 8)
        with contextlib.ExitStack() as st:
            A = Alloc(nc, st)
            P = Phase(C.K, "l0a")
            ident, mhalf = load_consts(P, A, C)
            a_pp, b_pp, _G = load_mod(P, A, C, 0, 1, want_gate=False)
            T = decay_tables(P, A, C, ("pp",))
            xt = [A.sb([128, D], F32, "xt") for _ in range(3)]
            cs = [A.sb([128, 2, 2, 64], F32, "cs") for _ in range(3)]
            xn = [A.sb([128, D], BF16, "xn") for _ in range(2)]
            junk = A.sb([128, D], BF16, "junk")
            ss = [A.sb([128, 4], F32, "ss") for _ in range(4)]
            hT = [A.sb([128, 8, 128], BF16, "hT") for _ in range(2)]
            qk32 = [A.sb([128, 2048], F32, "qk32")] * 2
            t1 = A.sb([128, 8, 2, 64], F32, "t1")
            t2 = A.sb([128, 8, 2, 64], F32, "t2")
            t3, t4 = t1, t2
            qkr = [A.sb([128, 2048], BF16, "qkr") for _ in range(2)]
            kf = [A.sb([128, 4, 256], BF16, "kf") for _ in range(2)]
            kb = [A.sb([128, 4, 256], BF16, "kb") for _ in range(2)]
            vo = [A.sb([128, 2048], BF16, "vo") for _ in range(2)]
            sg = [A.sb([128, 2048], BF16, "sg") for _ in range(2)]
            qo = [A.sb([128, 3, 8, 128], BF16, "qo") for _ in range(2)]
            ko = [A.sb([128, 8, 128], BF16, "ko") for _ in range(2)]
            pT = [A.ps([128, 8, 128], BF16, "pT") for _ in range(2)]
            pp = [A.ps([128, 512], F32, "pp") for _ in range(4)]
            pq = A.ps([128, 16, 128], BF16, "pq")
            items = [(s, c) for s in range(NS) for c in range(NT)]

            def do_load(i):
                s, c = items[i]
                P.dma(xt[i % 3], C.x[s, c * 128:(c + 1) * 128, :])
                P.dma(cs[i % 3], C.rope_r[c])

            def do_A1(i):
                front1(P, xt[i % 3], ss[i % 4], junk, xn[i % 2], mhalf)

            def do_A2(i):
                s, c = items[i]
                front2(P, xn[i % 2], ident, pT[i % 2], hT[i % 2], a_pp[s], b_pp[s], ev_eng="dve")

            def do_B(i, cts):
                s, c = items[i]
                sl = i % 2
                hv = hT[sl]
                for ct in cts:
                    pv = pp[ct % 4]
                    for kc in range(8):
                        P.mm(pv, hv[:, kc, :], W[:, kc, ct * 512:(ct + 1) * 512], start=(kc == 0), stop=(kc == 7))
                    if ct < 4:
                        P.copy(qk32[sl][:, ct * 512:(ct + 1) * 512], pv, eng="act")
                    elif ct < 8:
                        P.copy(vo[sl][:, (ct - 4) * 512:(ct - 3) * 512], pv, eng="dve")
                    else:
                        P.act(sg[sl][:, (ct - 8) * 512:(ct - 7) * 512], pv, AF.Silu)

            def do_C(i):
                s, c = items[i]
                sl = i % 2
                v5 = qk32[sl].re("p (h f a d) -> p h f a d", h=8, f=2, a=2)
                o5 = qkr[sl].re("p (h f a d) -> p h f a d", h=8, f=2, a=2)
                a_, b_ = v5[:, :, :, 0, :], v5[:, :, :, 1, :]
                csv = cs[i % 3]
                cos = V(csv.ap[:, 0].unsqueeze(1).broadcast_to([128, 8, 2, 64]), csv.key)
                sin = V(csv.ap[:, 1].unsqueeze(1).broadcast_to([128, 8, 2, 64]), csv.key)
                P.tt(t1, a_, cos, ALU.mult, eng="dve")
                P.tt(t2, b_, sin, ALU.mult, eng="pool")
                P.tt(o5[:, :, :, 0, :], t1, t2, ALU.subtract, eng="dve")
                P.tt(t3, a_, sin, ALU.mult, eng="pool")
                P.tt(t4, b_, cos, ALU.mult, eng="dve")
                P.tt(o5[:, :, :, 1, :], t3, t4, ALU.add, eng="pool")
                kv = qkr[sl][:, 1024:2048].re("p (h d) -> p h d", h=4)
                P.tt(kf[sl], kv, V(T["kdf"].ap.unsqueeze(2).broadcast_to([128, 4, 256]), T["kdf"].key), ALU.mult, eng="pool")
                P.tt(kb[sl], kv, V(T["kdb"].ap.unsqueeze(2).broadcast_to([128, 4, 256]), T["kdb"].key), ALU.mult, eng="pool")

            def do_D(i):
                s, c = items[i]
                sl = i % 2
                for j in range(16):
                    P.tr(pq[:, j, :], qkr[sl][:, j * 128:(j + 1) * 128], ident)
                P.copy(qo[sl][:, 0], pq[:, 0:8, :], eng="dve")
                P.tt(qo[sl][:, 1], pq[:, 0:8, :], T["qdf"], ALU.mult, eng="dve")
                P.tt(qo[sl][:, 2], pq[:, 0:8, :], T["qdb"], ALU.mult, eng="dve")
                P.copy(ko[sl], pq[:, 8:16, :], eng="act")
                P.dma(C.QT[s, c].rearrange("v p h d t -> p v (h d) t"), qo[sl])
                P.dma(C.KT[s, c].rearrange("p h d t -> p (h d) t"), ko[sl])
                P.dma(C.KF[s, c], kf[sl])
                P.dma(C.KB[s, c], kb[sl])
                P.dma(C.VS[s, c].rearrange("p h d -> p (h d)"), vo[sl])
                P.dma(C.SG[s, c].rearrange("p h d -> p (h d)"), sg[sl])

            n = len(items)
            do_load(0)
            if n > 1:
                do_load(1)
            do_A1(0)
            do_A2(0)
            for i in range(n):
                if i + 2 < n:
                    do_load(i + 2)
                do_B(i, range(0, 4))
                if i + 1 < n:
                    do_A1(i + 1)
                do_C(i)
                do_B(i, range(4, 8))
                if i + 1 < n:
                    do_A2(i + 1)
                do_B(i, range(8, 12))
                do_D(i)
            P.emit()


def phase_l0b(C):
    import contextlib
    nc = C.nc
    NS, NT = C.NSEQ, C.NT
    GC = 4
    NG = NT // GC
    with contextlib.ExitStack() as st:
        A = Alloc(nc, st)
        P = Phase(C.K, "l0b")
        ident, mhalf = load_consts(P, A, C)
        T = decay_tables(P, A, C, ("mask",))
        DT, cdec = T["DT"], T["cdec"]
        epsv = A.sb([128, 1], F32, "epsv")
        P.memset(epsv, EPS)
        SbA = A.sb([128, NT, 2, 512], BF16, "SbA")
        SbK = [V(SbA.ap[:, c], ("SbA", c)) for c in range(NT)]
        S32 = [A.sb([128, 2, 512], F32, "S32") for _ in range(2)]
        sj = [0]
        NSF = 4
        Sf = [A.sb([128, 2, 512], BF16, "Sf") for _ in range(NSF)]
        bk = [A.sb([128, GC, 256], BF16, "bk") for _ in range(2)]
        bv = [A.sb([128, GC, 512], BF16, "bv") for _ in range(2)]
        fq = [A.sb([128, GC, 3, 2, 128], BF16, "fq") for _ in range(3)]
        fkT = [A.sb([128, GC, 2, 128], BF16, "fkT") for _ in range(3)]
        fkf = [A.sb([128, GC, 256], BF16, "fkf") for _ in range(3)]
        fv = [A.sb([128, GC, 512], BF16, "fv") for _ in range(3)]
        fsg = [A.sb([128, GC, 512], BF16, "fsg") for _ in range(3)]
        sTm = [A.sb([128, 128], BF16, "sTm") for _ in range(2)]
        st6 = [A.sb([128, 6], F32, "st6") for _ in range(2)]
        mv = [A.sb([128, 8], F32, "mv") for _ in range(2)]
        yn = [A.sb([128, 512], F32, "yn") for _ in range(2)]
        yb = [A.sb([128, 512], BF16, "yb") for _ in range(2)]
        yTa = [A.sb([128, 4, GC * 128], BF16, "yTa") for _ in range(3)]
        pss = A.ps([128, 512], F32, "pss")
        po = [A.ps([128, 512], F32, "po") for _ in range(2)]
        pk = [A.ps([128, 2, 512], F32, "pk") for _ in range(2)]
        pyT = A.ps([128, 8, 128], BF16, "pyT")
        groups = []
        for s in range(NS):
            for h in range(RET_H):
                for g in range(NG - 1, -1, -1):
                    groups.append(("b", s, h, g))
                for g in range(NG):
                    groups.append(("f", s, h, g))
        cntb = [0]
        cntf = [0]
        slot_of = {}

        def do_load(k):
            kind, s, h, g = groups[k]
            c0 = g * GC
            if kind == "b":
                sl = cntb[0] % 2
                cntb[0] += 1
                slot_of[k] = sl
                P.dma(bk[sl], C.KB[s, c0:c0 + GC, :, h, :].rearrange("c p d -> p c d"))
                P.dma(bv[sl], C.VS[s, c0:c0 + GC, :, h, :].rearrange("c p d -> p c d"))
            else:
                sl = cntf[0] % 3
                cntf[0] += 1
                slot_of[k] = sl
                P.dma(fq[sl], C.QT[s, c0:c0 + GC, :, :, h, :, :].rearrange("c v p d t -> p c v d t"))
                P.dma(fkT[sl], C.KT[s, c0:c0 + GC, :, h, :, :].rearrange("c p d t -> p c d t"))
                P.dma(fkf[sl], C.KF[s, c0:c0 + GC, :, h, :].rearrange("c p d -> p c d"))
                P.dma(fv[sl], C.VS[s, c0:c0 + GC, :, h, :].rearrange("c p d -> p c d"))
                P.dma(fsg[sl], C.SG[s, c0:c0 + GC, :, h, :].rearrange("c p d -> p c d"))

        kvi = [0]
        ci = [0]
        q_gn2 = []
        q_gate = []
        q_tail = []

        def flush():
            while q_gn2:
                q_gn2.pop(0)()
            while q_gate:
                q_gate.pop(0)()
            while q_tail:
                q_tail.pop(0)()

        do_load(0)
        for k, (kind, s, h, g) in enumerate(groups):
            if k + 1 < len(groups):
                do_load(k + 1)
            sl = slot_of[k]
            c0 = g * GC
            if kind == "b":
                if g == NG - 1:
                    flush()
                    P.memset(S32[sj[0] % 2], 0.0)
                    P.memset(SbK[NT - 1], 0.0)
                for cc in range(GC - 1, -1, -1):
                    c = c0 + cc
                    if c == 0:
                        continue
                    pkv = pk[kvi[0] % 2]
                    kvi[0] += 1
                    for dc in range(2):
                        P.mm(pkv[:, dc, :], bk[sl][:, cc, dc * 128:(dc + 1) * 128], bv[sl][:, cc, :])
                    sprev, scur = S32[sj[0] % 2], S32[(sj[0] + 1) % 2]
                    sj[0] += 1
                    P.stt(scur, sprev, cdec[:, 4 + h:5 + h], pkv, ALU.mult, ALU.add)
                    P.copy(SbK[c - 1], scur, eng="act")
            else:
                if g == 0:
                    P.memset(S32[sj[0] % 2], 0.0)
                    P.memset(Sf[ci[0] % NSF], 0.0)
                for cc in range(GC):
                    c = c0 + cc
                    k2 = ci[0] % 2
                    sfc, sfn = Sf[ci[0] % NSF], Sf[(ci[0] + 1) % NSF]
                    ci[0] += 1
                    if c < NT - 1:
                        pkv = pk[kvi[0] % 2]
                        kvi[0] += 1
                        for dc in range(2):
                            P.mm(pkv[:, dc, :], fkf[sl][:, cc, dc * 128:(dc + 1) * 128], fv[sl][:, cc, :])
                    for dc in range(2):
                        P.mm(pss[:, 0:128], fkT[sl][:, cc, dc, :], fq[sl][:, cc, 0, dc, :],
                             start=(dc == 0), stop=(dc == 1))
                    P.tt(sTm[k2], pss[:, 0:128], DT[:, h, :], ALU.mult)
                    if q_gn2:
                        q_gn2.pop(0)()
                    pov = po[k2]
                    P.mm(pov, sTm[k2], fv[sl][:, cc, :], start=True, stop=False)
                    for dc in range(2):
                        P.mm(pov, fq[sl][:, cc, 1, dc, :], sfc[:, dc, :], start=False, stop=False)
                    for dc in range(2):
                        P.mm(pov, fq[sl][:, cc, 2, dc, :], SbK[c][:, dc, :], start=False, stop=(dc == 1))
                    if q_tail:
                        q_tail.pop(0)()
                    if c < NT - 1:
                        sprev, scur = S32[sj[0] % 2], S32[(sj[0] + 1) % 2]
                        sj[0] += 1
                        P.stt(scur, sprev, cdec[:, h:h + 1], pkv, ALU.mult, ALU.add)
                        P.copy(sfn, scur, eng="act")
                    m = mv[k2]
                    P.bn_stats(st6[k2], pov)
                    P.bn_aggr(m[:, 0:2], st6[k2])
                    P.ts(m[:, 2:3], m[:, 1:2], epsv, None, ALU.add)
                    P.tt(m[:, 3:4], m[:, 2:3], mhalf, ALU.pow, eng="pool")
                    if q_gate:
                        q_gate.pop(0)()

                    def tail(k2=k2, sl=sl, cc=cc, s=s, h=h, c0=c0):
                        for ec in range(4):
                            P.tr(pyT[:, ec, :], yb[k2][:, ec * 128:(ec + 1) * 128], ident)
                        P.copy(yTa[sl][:, :, cc * 128:(cc + 1) * 128], pyT[:, 0:4, :], eng="act")
                        if cc == GC - 1:
                            P.dma(C.YT[s, :, h * 4:(h + 1) * 4, c0 * 128:(c0 + GC) * 128], yTa[sl])

                    def gate(k2=k2, sl=sl, cc=cc, tail=tail):
                        P.tt(yb[k2], yn[k2], fsg[sl][:, cc, :], ALU.mult, eng="pool")
                        q_tail.append(tail)

                    def gn2(k2=k2, m=m, pov=pov, gate=gate):
                        P.ts(m[:, 4:5], m[:, 0:1], m[:, 3:4], -1.0, ALU.mult, ALU.mult)
                        P.act(yn[k2], pov, AF.Identity, bias=m[:, 4:5], scale=m[:, 3:4])
                        q_gate.append(gate)
                    q_gn2.append(gn2)
        flush()
        P.emit()


def phase_l0c(C, Xin, Xout):
    import contextlib
    nc = C.nc
    NS, NT = C.NSEQ, C.NT
    TS = 4
    with contextlib.ExitStack() as st0:
        A0 = Alloc(nc, st0)
        W = A0.sb([128, 16, D], BF16, "Wro")
        load_weight(C, W, C.ret_w_out, D, 16)
        with contextlib.ExitStack() as st:
            A = Alloc(nc, st)
            P = Phase(C.K, "l0c")
            _a, _b, G = load_mod(P, A, C, 0, 1, want_pp=False)
            yT = [A.sb([128, 16, TS * 128], BF16, "yT") for _ in range(2)]
            xt = [A.sb([128, D], F32, "xt") for _ in range(2 * TS)]
            tmp = [A.sb([128, 512], F32, "tmp") for _ in range(2)]
            po = [A.ps([128, 512], F32, "po") for _ in range(4)]
            items = [(s, sc) for s in range(NS) for sc in range(NT // TS)]

            def do_load(i):
                s, sc = items[i]
                sl = i % 2
                P.dma(yT[sl], C.YT[s, :, :, sc * TS * 128:(sc + 1) * TS * 128])
                for j in range(TS):
                    c = sc * TS + j
                    P.dma(xt[sl * TS + j], Xin[s, c * 128:(c + 1) * 128, :])

            do_load(0)
            for i, (s, sc) in enumerate(items):
                sl = i % 2
                if i + 1 < len(items):
                    do_load(i + 1)
                for j in range(TS):
                    c = sc * TS + j
                    xv = xt[sl * TS + j]
                    for half in range(2):
                        pv = po[(j * 2 + half) % 4]
                        for ec in range(16):
                            P.mm(pv, yT[sl][:, ec, j * 128:(j + 1) * 128], W[:, ec, half * 512:(half + 1) * 512],
                                 start=(ec == 0), stop=(ec == 15))
                        P.tt(tmp[half], pv, G[s][:, half * 512:(half + 1) * 512], ALU.mult)
                        P.tt(xv[:, half * 512:(half + 1) * 512], tmp[half], xv[:, half * 512:(half + 1) * 512],
                             ALU.add, eng="pool")
                    P.dma(Xout[s, c * 128:(c + 1) * 128, :], xv)
            P.emit()


def phase_l1a(C, Xin):
    import contextlib
    nc = C.nc
    NS, NT = C.NSEQ, C.NT
    GC = 4
    with contextlib.ExitStack() as st0:
        A0 = Alloc(nc, st0)
        W = A0.sb([128, 8, ATT_IN], BF16, "Wai")
        load_weight(C, W, C.att_w_in, ATT_IN, 8)
        with contextlib.ExitStack() as st:
            A = Alloc(nc, st)
            P = Phase(C.K, "l1a")
            ident, mhalf = load_consts(P, A, C)
            a_pp, b_pp, _G = load_mod(P, A, C, 1, 1, want_gate=False)
            gains = A.sb([128, 10, 128], F32, "gains")
            for hh in range(10):
                src = C.att_q_gain if hh < 8 else C.att_k_gain
                P.dma(gains[:, hh, :], src.partition_broadcast(128))
            xt = [A.sb([128, D], F32, "xt") for _ in range(3)]
            cs = [A.sb([128, 2, 2, 32], F32, "cs") for _ in range(4)]
            xn = [A.sb([128, D], BF16, "xn") for _ in range(2)]
            junk = A.sb([128, D], BF16, "junk")
            ss = [A.sb([128, 4], F32, "ss") for _ in range(4)]
            hT = [A.sb([128, 8, 128], BF16, "hT") for _ in range(2)]
            qk32_ = [A.sb([128, 10, 128], F32, "qk32") for _ in range(2)]
            sq_ = [A.sb([128, 10, 128], F32, "sq") for _ in range(2)]
            s10 = [A.sb([128, 32], F32, "s10") for _ in range(2)]
            qkn_ = [A.sb([128, 10, 128], F32, "qkn") for _ in range(2)]
            t1_ = [A.sb([128, 10, 2, 32], F32, "t1") for _ in range(2)]
            t2_ = [A.sb([128, 10, 2, 32], F32, "t2") for _ in range(2)]
            t3_ = [A.sb([128, 10, 2, 32], F32, "t3") for _ in range(2)]
            t4_ = [A.sb([128, 10, 2, 32], F32, "t4") for _ in range(2)]
            qkr = [A.sb([128, 10 * 128], BF16, "qkr") for _ in range(2)]
            vb = [A.sb([128, 256], BF16, "vb") for _ in range(2)]
            acc = [A.sb([128, 10, GC * 128], BF16, "acc") for _ in range(2)]
            pT = [A.ps([128, 8, 128], BF16, "pT") for _ in range(2)]
            pp = [A.ps([128, 512], F32, "pp") for _ in range(3)]
            pq = A.ps([128, 16, 128], BF16, "pq")
            items = [(s, c) for s in range(NS) for c in range(NT)]

            def do_load(i):
                s, c = items[i]
                P.dma(xt[i % 3], Xin[s, c * 128:(c + 1) * 128, :])
                P.dma(cs[i % 4], C.rope_a[c])

            def do_A1(i):
                front1(P, xt[i % 3], ss[i % 4], junk, xn[i % 2], mhalf, xn_eng="act")

            def do_A2(i):
                s, c = items[i]
                front2(P, xn[i % 2], ident, pT[i % 2], hT[i % 2], a_pp[s], b_pp[s], ev_eng="act")

            def do_B(i):
                s, c = items[i]
                sl = i % 2
                g = (i // GC) % 2
                cc = c % GC
                hv = hT[sl]
                qk32, sq, qkn, t1, t2, t3, t4 = qk32_[sl], sq_[sl], qkn_[sl], t1_[sl], t2_[sl], t3_[sl], t4_[sl]
                qkf = qk32.re("p h d -> p (h d)")
                for ct in range(3):
                    pv = pp[ct]
                    for kc in range(8):
                        P.mm(pv, hv[:, kc, :], W[:, kc, ct * 512:(ct + 1) * 512], start=(kc == 0), stop=(kc == 7))
                    if ct < 2:
                        P.copy(qkf[:, ct * 512:(ct + 1) * 512], pv, eng="act")
                    else:
                        P.copy(qkf[:, 1024:1280], pv[:, 0:256], eng="act")
                        P.copy(vb[sl], pv[:, 256:512], eng="act")
                P.dma(C.VA[s, c], vb[sl])

            def do_C(i):
                s, c = items[i]
                sl = i % 2
                qk32, sq, qkn, t1, t2, t3, t4 = qk32_[sl], sq_[sl], qkn_[sl], t1_[sl], t2_[sl], t3_[sl], t4_[sl]
                P.act(sq, qk32, AF.Square)
                sv = s10[sl]
                P.reduce(sv[:, 0:10], sq, ALU.add)
                P.ts(sv[:, 10:20], sv[:, 0:10], 1.0 / ATT_HD, EPS, ALU.mult, ALU.add)
                P.tt(sv[:, 20:30], sv[:, 10:20], V(mhalf.ap.broadcast_to([128, 10]), mhalf.key), ALU.pow, eng="pool")
                P.tt(qkn, qk32, V(sv.ap[:, 20:30].unsqueeze(2).broadcast_to([128, 10, 128]), sv.key), ALU.mult, eng="dve")
                P.tt(qkn, qkn, gains, ALU.mult, eng="dve")
                v5 = qkn.re("p h (f a d) -> p h f a d", f=2, a=2)
                o5 = qkr[sl].re("p (h f a d) -> p h f a d", h=10, f=2, a=2)
                a_, b_ = v5[:, :, :, 0, :], v5[:, :, :, 1, :]
                csv = cs[i % 4]
                cos = V(csv.ap[:, 0].unsqueeze(1).broadcast_to([128, 10, 2, 32]), csv.key)
                sin = V(csv.ap[:, 1].unsqueeze(1).broadcast_to([128, 10, 2, 32]), csv.key)
                P.tt(t1, a_, cos, ALU.mult, eng="dve")
                P.tt(t2, b_, sin, ALU.mult, eng="pool")
                P.tt(t3, a_, sin, ALU.mult, eng="pool")
                P.tt(t4, b_, cos, ALU.mult, eng="dve")
                P.tt(o5[:, :, :, 0, :], t1, t2, ALU.subtract, eng="dve")
                P.tt(o5[:, :, :, 1, :], t3, t4, ALU.add, eng="pool")

            def do_D(i):
                s, c = items[i]
                sl = i % 2
                g = (i // GC) % 2
                cc = c % GC
                for j in range(10):
                    P.tr(pq[:, j, :], qkr[sl][:, j * 128:(j + 1) * 128], ident)
                P.copy(acc[g][:, :, cc * 128:(cc + 1) * 128], pq[:, 0:10, :], eng="act")
                if cc == GC - 1:
                    c0 = c - (GC - 1)
                    P.dma(C.QTA[s, :, :, c0 * 128:(c0 + GC) * 128], acc[g][:, 0:8, :])
                    P.dma(C.KTA[s, :, :, c0 * 128:(c0 + GC) * 128], acc[g][:, 8:10, :])

            n = len(items)
            do_load(0)
            if n > 1:
                do_load(1)
            do_A1(0)
            do_A2(0)
            for i in range(n + 2):
                if i + 2 < n:
                    do_load(i + 2)
                if i + 1 < n:
                    do_A1(i + 1)
                if 0 <= i - 1 < n:
                    do_C(i - 1)
                if 0 <= i - 2 < n:
                    do_D(i - 2)
                if i + 1 < n:
                    do_A2(i + 1)
                if i < n:
                    do_B(i)
            P.emit()


def phase_l1b(C, Xin, Xout):
    import contextlib
    nc = C.nc
    NS, NT, N = C.NSEQ, C.NT, C.N
    QB = 512
    NQB = N // QB
    NP = NT // 2
    with contextlib.ExitStack() as st0:
        A0 = Alloc(nc, st0)
        W = A0.sb([128, 8, D], BF16, "Wao")
        load_weight(C, W, C.att_w_out, D, 8)
        with contextlib.ExitStack() as st:
            A = Alloc(nc, st)
            P = Phase(C.K, "l1b")
            _a, _b, G = load_mod(P, A, C, 1, 1, want_pp=False)
            gq = A.sb([128, 128], F32, "gq")
            gk = A.sb([128, 128], F32, "gk")
            mm_ = A.sb([128, 4], F32, "mm")
            P.dma(gq, C.att_q_gain.partition_broadcast(128))
            P.dma(gk, C.att_k_gain.partition_broadcast(128))
            P.reduce(mm_[:, 0:1], gq, ALU.max, absval=True)
            P.reduce(mm_[:, 1:2], gk, ALU.max, absval=True)
            P.tt(mm_[:, 2:3], mm_[:, 0:1], mm_[:, 1:2], ALU.mult)
            P.ts(mm_[:, 3:4], mm_[:, 2:3], -math.sqrt(ATT_HD), None, ALU.mult)
            negm = mm_[:, 3:4]
            ones32 = A.sb([128, 128], F32, "ones32")
            P.memset(ones32, 1.0)
            onesb = A.sb([128, 128], BF16, "onesb")
            P.memset(onesb, 1.0)
            pe_us = [u for u in range(NP) if u % 4 == 3] if NP >= 8 else []
            kt = [A.sb([128, 2, N], BF16, "kt") for _ in range(2)]
            va = [A.sb([128, NT, 256], BF16, "va") for _ in range(2)]
            qt = [A.sb([128, 8, QB], BF16, "qt") for _ in range(2)]
            PT = [A.sb([128, 2, QB], BF16, "PT") for _ in range(3)]
            acc = [A.sb([128, 2, QB], F32, "acc") for _ in range(2)]
            accp = [A.sb([128, 2, QB], F32, "accp") for _ in range(2)]
            POOL_SHARE = False
            pool_us = [u for u in range(NP) if u % 4 == 3] if (NP >= 8 and POOL_SHARE) else []
            OT = [A.sb([128, 8, QB], BF16, "OT") for _ in range(2)]
            rinv = [A.sb([128, QB], F32, "rinv") for _ in range(2)]
            xt = [A.sb([128, D], F32, "xt") for _ in range(8)]
            tmp = [A.sb([128, 512], F32, "tmp") for _ in range(2)]
            ps_s = [A.ps([128, 2, 512], F32, "ps_s") for _ in range(2)]
            po = [A.ps([128, 512], F32, "po") for _ in range(2)]
            prs = A.ps([128, 512], F32, "prs")
            pw = A.ps([128, 512], F32, "pw")
            scale = float(ATT_HD) ** -0.5
            blocks = [(s, qb) for s in range(NS) for qb in range(NQB)]
            units = [(bi, h, u) for bi in range(len(blocks)) for h in range(ATT_QH) for u in range(NP)]

            def load_seq(s):
                P.dma(kt[s % 2], C.KTA[s])
                P.dma(va[s % 2], C.VA[s].rearrange("c p d -> p c d"))

            def load_block(bi):
                s, qb = blocks[bi]
                sl = bi % 2
                P.dma(qt[sl], C.QTA[s, :, :, qb * QB:(qb + 1) * QB])
                for j in range(4):
                    c = qb * 4 + j
                    P.dma(xt[sl * 4 + j], Xin[s, c * 128:(c + 1) * 128, :])

            def qk(i):
                bi, h, u = units[i]
                s, qb = blocks[bi]
                for t in range(2):
                    kc = 2 * u + t
                    P.mm(ps_s[i % 2][:, t, :], kt[s % 2][:, h // 4, kc * 128:(kc + 1) * 128], qt[bi % 2][:, h, :])

            deferred = []
            hfq = []

            def head_final(bi, h):
                def f():
                    a = acc[h % 2]
                    if pool_us:
                        P.tt(a, a, accp[h % 2], ALU.add)
                    for t in range(2):
                        P.mm(prs, ones32, a[:, t, :], start=(t == 0 and not pe_us), stop=(t == 1))
                    P.recip(rinv[h % 2], prs)
                    P.tt(OT[bi % 2][:, h, :], po[h % 2], rinv[h % 2], ALU.mult)
                return f

            def out_group(bi, j, half):
                def f():
                    s, qb = blocks[bi]
                    c = qb * 4 + j
                    xv = xt[(bi % 2) * 4 + j]
                    for h in range(8):
                        P.mm(pw, OT[bi % 2][:, h, j * 128:(j + 1) * 128], W[:, h, half * 512:(half + 1) * 512],
                             start=(h == 0), stop=(h == 7))
                    P.tt(tmp[half], pw, G[s][:, half * 512:(half + 1) * 512], ALU.mult)
                    P.tt(xv[:, half * 512:(half + 1) * 512], tmp[half], xv[:, half * 512:(half + 1) * 512],
                         ALU.add, eng="pool")
                    if half == 1:
                        P.dma(Xout[s, c * 128:(c + 1) * 128, :], xv)
                return f

            def pv_(i):
                bi, h, u = units[i]
                s, qb = blocks[bi]
                pt = PT[i % 3]
                import os
                if os.environ.get("DBG_EXP1"):
                    for t in range(2):
                        P.act(pt[:, t, :], ps_s[i % 2][:, t, :], AF.Exp, bias=negm, scale=scale)
                else:
                    P.act(pt, ps_s[i % 2], AF.Exp, bias=negm, scale=scale)
                for t in range(2):
                    kc = 2 * u + t
                    P.mm(po[h % 2], va[s % 2][:, kc, (h // 4) * 128:(h // 4 + 1) * 128], pt[:, t, :],
                         start=(kc == 0), stop=(kc == NT - 1))
                if u in pe_us:
                    for t in range(2):
                        P.mm(prs, onesb, pt[:, t, :], start=(u == pe_us[0] and t == 0), stop=False)
                elif u in pool_us:
                    if u == pool_us[0]:
                        P.copy(accp[h % 2], pt, eng="pool")
                    else:
                        P.tt(accp[h % 2], accp[h % 2], pt, ALU.add, eng="pool")
                elif u == 0:
                    P.copy(acc[h % 2], pt, eng="dve")
                else:
                    P.tt(acc[h % 2], acc[h % 2], pt, ALU.add)
                if u == NP - 1:
                    hfq.append(head_final(bi, h))
                    if h == ATT_QH - 1:
                        for j in range(4):
                            for half in range(2):
                                deferred.append(out_group(bi, j, half))

            nu = len(units)
            load_seq(0)
            load_block(0)
            loaded = 0
            qk(0)
            if nu > 1:
                qk(1)
            for i in range(nu):
                bi, h, u = units[i]
                if h == 0 and u == 0:
                    s, qb = blocks[bi]
                    if qb == 0 and s + 1 < NS:
                        load_seq(s + 1)
                pv_(i)
                if i + 2 < nu:
                    nb = units[i + 2][0]
                    if nb > loaded:
                        while hfq:
                            hfq.pop(0)()
                        while deferred:
                            deferred.pop(0)()
                        load_block(nb)
                        loaded = nb
                    qk(i + 2)
                if hfq and u == 0:
                    hfq.pop(0)()
                elif deferred and (u % 2 == 1) and not hfq:
                    deferred.pop(0)()
                if not deferred and loaded < bi + 1 and bi + 1 < len(blocks) and not (h == ATT_QH - 1 and u == NP - 1):
                    load_block(bi + 1)
                    loaded = bi + 1
            while hfq:
                hfq.pop(0)()
            while deferred:
                deferred.pop(0)()
            P.emit()


def rope_table(n_tokens, nf):
    t = np.arange(n_tokens)
    row = (t // GRID_W).astype(np.float32)
    col = (t % GRID_W).astype(np.float32)
    inv = (np.float32(ROPE_THETA) ** (-(np.arange(nf, dtype=np.float32)) / np.float32(nf))).astype(np.float32)
    ar = (row[:, None] * inv[None, :]).astype(np.float32)
    ac = (col[:, None] * inv[None, :]).astype(np.float32)
    tab = np.stack([np.stack([np.cos(ar), np.cos(ac)], 1), np.stack([np.sin(ar), np.sin(ac)], 1)], 1)
    return np.ascontiguousarray(tab.reshape(n_tokens // 128, 128, 2, 2, nf).astype(np.float32))


def build(NSEQ, N, phases=None):
    nc = bass.Bass("TRN2", target_bir_lowering=False)
    C = Ctx()
    C.nc = nc
    C.NSEQ = NSEQ
    C.N = N
    C.NT = N // 128
    NT = C.NT

    def inp(name, shape, dt=F32):
        return nc.dram_tensor(name, list(shape), dt, kind="ExternalInput").ap()

    C.x = inp("x", [NSEQ, N, D])
    C.cT = inp("cT", [128, 8, NSEQ])
    C.mod_w = inp("mod_w", [2, D, 6 * D])
    C.mod_b = inp("mod_b", [2, 6 * D])
    C.norm1_g = inp("norm1_g", [2, D])
    C.norm2_g = inp("norm2_g", [2, D])
    C.ret_w_in = inp("ret_w_in", [D, RET_IN])
    C.ret_decay = inp("ret_decay", [8])
    C.ret_w_out = inp("ret_w_out", [2 * D, D])
    C.att_w_in = inp("att_w_in", [D, ATT_IN])
    C.att_q_gain = inp("att_q_gain", [ATT_HD])
    C.att_k_gain = inp("att_k_gain", [ATT_HD])
    C.att_w_out = inp("att_w_out", [D, D])
    C.mlp_w1 = inp("mlp_w1", [2, D, DFF])
    C.mlp_w2 = inp("mlp_w2", [2, DFF, D])
    C.final_g = inp("final_g", [D])
    C.ident = inp("ident", [128, 128])
    C.rope_r = inp("rope_r", [NT, 128, 2, 2, 64])
    C.rope_a = inp("rope_a", [NT, 128, 2, 2, 32])
    C.y = nc.dram_tensor("y", [NSEQ, N, D], F32, kind="ExternalOutput").ap()

    def scr(name, shape, dt):
        return nc.dram_tensor(name, list(shape), dt).ap()

    C.MOD = scr("MOD", [2, NSEQ, 6 * D], F32)
    C.XA = scr("XA", [NSEQ, N, D], F32)
    C.XB = scr("XB", [NSEQ, N, D], F32)
    C.QT = scr("QT", [NSEQ, NT, 3, 128, 4, 2, 128], BF16)
    C.KT = scr("KT", [NSEQ, NT, 128, 4, 2, 128], BF16)
    C.KF = scr("KF", [NSEQ, NT, 128, 4, 256], BF16)
    C.KB = scr("KB", [NSEQ, NT, 128, 4, 256], BF16)
    C.VS = scr("VS", [NSEQ, NT, 128, 4, 512], BF16)
    C.SG = scr("SG", [NSEQ, NT, 128, 4, 512], BF16)
    C.YT = scr("YT", [NSEQ, 128, 16, N], BF16)
    C.QTA = scr("QTA", [NSEQ, 128, 8, N], BF16)
    C.KTA = scr("KTA", [NSEQ, 128, 2, N], BF16)
    C.VA = scr("VA", [NSEQ, NT, 128, 256], BF16)
    C.K = Kern(nc)
    if phases is None:
        phases = ["mod", "l0a", "l0b", "l0c", "mlp0", "l1a", "l1b", "mlp1"]
    C.phases = phases
    for ph in phases:
        if ph == "mod":
            phase_mod(C)
        elif ph == "l0a":
            phase_l0a(C)
        elif ph == "l0b":
            phase_l0b(C)
        elif ph == "l0c":
            phase_l0c(C, C.x, C.XA)
        elif ph == "mlp0":
            phase_mlp(C, 0, C.XA, C.XB, False)
        elif ph == "l1a":
            phase_l1a(C, C.XB)
        elif ph == "l1b":
            phase_l1b(C, C.XB, C.XA)
        elif ph == "mlp1":
            phase_mlp(C, 1, C.XA, C.y, True)
        elif ph == "l1a_x":
            phase_l1a(C, C.x)
        elif ph == "l1b_xy":
            phase_l1b(C, C.x, C.y)
        elif ph == "l0c_y":
            phase_l0c(C, C.x, C.y)
        elif ph == "mlp_only":
            phase_mlp(C, 0, C.x, C.y, True)
        else:
            raise ValueError(ph)
    return nc, C


def make_in_maps(inputs, NSEQ_P=2, NSEQ_S=1, ncores=8):
    f = lambda a: np.ascontiguousarray(np.asarray(a, dtype=np.float32))
    xp, xs_ = f(inputs["x_prompt"]), f(inputs["x_sample"])
    cp, cs = f(inputs["c_prompt"]), f(inputs["c_sample"])
    N = xp.shape[1]
    shared = {
        "mod_w": f(inputs["mod_w"]), "mod_b": f(inputs["mod_b"]),
        "norm1_g": f(inputs["norm1_g"]), "norm2_g": f(inputs["norm2_g"]),
        "ret_w_in": f(inputs["ret_w_in"])[0], "ret_decay": f(inputs["ret_decay"])[0].reshape(8),
        "ret_w_out": f(inputs["ret_w_out"])[0], "att_w_in": f(inputs["att_w_in"])[0],
        "att_q_gain": f(inputs["att_q_gain"])[0], "att_k_gain": f(inputs["att_k_gain"])[0],
        "att_w_out": f(inputs["att_w_out"])[0], "mlp_w1": f(inputs["mlp_w1"]), "mlp_w2": f(inputs["mlp_w2"]),
        "final_g": f(inputs["final_g"]),
        "ident": np.eye(128, dtype=np.float32),
        "rope_r": rope_table(N, 64), "rope_a": rope_table(N, 32),
    }
    maps = []
    for i in range(ncores):
        x = np.concatenate([xp[NSEQ_P * i:NSEQ_P * (i + 1)], xs_[NSEQ_S * i:NSEQ_S * (i + 1)]], 0)
        c = np.concatenate([cp[NSEQ_P * i:NSEQ_P * (i + 1)], cs[NSEQ_S * i:NSEQ_S * (i + 1)]], 0)
        cT = np.ascontiguousarray(c.reshape(c.shape[0], 8, 128).transpose(2, 1, 0))
        m = dict(shared)
        m["x"] = np.ascontiguousarray(x)
        m["cT"] = cT
        maps.append(m)
    return maps


def kernel(**inputs):
    maps = make_in_maps(inputs)
    N = maps[0]["x"].shape[1]
    nc, C = build(3, N)
    res = run_bass_kernel_spmd(nc, maps, core_ids=list(range(8)))
    ys = [np.asarray(r["y"]) for r in res.results]
    y_prompt = np.concatenate([y[0:2] for y in ys], 0).astype(np.float32)
    y_sample = np.concatenate([y[2:3] for y in ys], 0).astype(np.float32)
    return (y_prompt, y_sample)
```

```python
import math
import numpy as np
import concourse.bass as bass
import concourse.mybir as mybir
from concourse.bass_utils import run_bass_kernel_spmd
from concourse.alu_op_type import AluOpType as ALU

AF = mybir.ActivationFunctionType
F32 = mybir.dt.float32
BF16 = mybir.dt.bfloat16
AX = mybir.AxisListType

D = 1024
DFF = 4096
EPS = 1e-6
RET_H = 4
RET_DK = 256
RET_DV = 512
RET_IN = 6144
ATT_HD = 128
ATT_QH = 8
ATT_KVH = 2
ATT_IN = 1536
GRID_W = 64
ROPE_THETA = 10000.0


class V:
    __slots__ = ("ap", "key")

    def __init__(self, ap, key):
        self.ap = ap
        self.key = key

    def __getitem__(self, idx):
        return V(self.ap[idx], self.key)

    def sub(self, s):
        return V(self.ap, (self.key, s))

    def re(self, pat, **kw):
        return V(self.ap.rearrange(pat, **kw), self.key)

    def bc(self, shape):
        return V(self.ap.broadcast_to(shape), self.key)


def _keys(*xs):
    out = []
    for x in xs:
        if isinstance(x, V) and x.key is not None:
            out.append(x.key)
    return out


def _a(x):
    return x.ap if isinstance(x, V) else x


class Op:
    __slots__ = ("eng", "fn", "reads", "writes", "dma", "signal", "sem", "count", "deps", "pre")

    def __init__(self, eng, fn, reads, writes, dma):
        self.eng = eng
        self.fn = fn
        self.reads = reads
        self.writes = writes
        self.dma = dma
        self.signal = dma
        self.sem = None
        self.count = 0
        self.deps = ()
        self.pre = None


ENGS = ("pe", "act", "dve", "pool", "sp")


class Kern:
    def __init__(self, nc, ndma=32):
        self.nc = nc
        self.ndma = ndma
        self.dma_sems = [nc.alloc_semaphore(name=f"dq{i}") for i in range(ndma)]
        self.dma_n = 0
        self.dma_cnt = [0] * ndma
        self.phase_i = 0
        self.waited = {e: {} for e in ENGS}
        self.n_inst = 0

    def eng(self, name):
        nc = self.nc
        return {"pe": nc.tensor, "act": nc.scalar, "dve": nc.vector, "pool": nc.gpsimd, "sp": nc.sync}[name]


class Phase:
    def __init__(self, K, name):
        self.K = K
        self.name = name
        self.ops = []
        self._uid = 0

    def uid(self):
        self._uid += 1
        return self._uid

    def add(self, eng, fn, reads=(), writes=(), dma=False):
        self.ops.append(Op(eng, fn, tuple(reads), tuple(writes), dma))

    def mm(self, out, lhsT, rhs, start=True, stop=True):
        self.add("pe", lambda e: e.matmul(out.ap, lhsT.ap, rhs.ap, start=start, stop=stop),
                 _keys(lhsT, rhs), _keys(out))

    def tr(self, out, in_, ident):
        self.add("pe", lambda e: e.transpose(out.ap, in_.ap, ident.ap), _keys(in_, ident), _keys(out))

    def act(self, out, in_, func, bias=None, scale=None, accum=None, eng="act"):
        kw = {}
        if bias is not None:
            kw["bias"] = _a(bias)
        if scale is not None:
            kw["scale"] = _a(scale)
        if accum is not None:
            kw["accum_out"] = accum.ap
        self.add(eng, lambda e: e.activation(out.ap, in_.ap, func, **kw),
                 _keys(in_, bias, scale), _keys(out, accum))

    def ts(self, out, in0, s1, s2, op0, op1=None, eng="dve", accum=None):
        kw = {}
        if op1 is not None:
            kw["op1"] = op1
        if accum is not None:
            kw["accum_out"] = accum.ap
        self.add(eng, lambda e: e.tensor_scalar(out.ap, in0.ap, _a(s1), _a(s2), op0, **kw),
                 _keys(in0, s1, s2), _keys(out, accum))

    def tt(self, out, in0, in1, op, eng="dve"):
        self.add(eng, lambda e: e.tensor_tensor(out.ap, in0.ap, in1.ap, op), _keys(in0, in1), _keys(out))

    def stt(self, out, in0, scalar, in1, op0, op1, eng="dve"):
        self.add(eng, lambda e: e.scalar_tensor_tensor(out.ap, in0.ap, _a(scalar), in1.ap, op0, op1),
                 _keys(in0, scalar, in1), _keys(out))

    def copy(self, out, in_, eng="dve"):
        if eng == "act":
            self.add(eng, lambda e: e.copy(out.ap, in_.ap), _keys(in_), _keys(out))
        else:
            self.add(eng, lambda e: e.tensor_copy(out.ap, in_.ap), _keys(in_), _keys(out))

    def memset(self, out, val, eng="pool"):
        self.add(eng, lambda e: e.memset(out.ap, val), (), _keys(out))

    def recip(self, out, in_, eng="dve"):
        self.add(eng, lambda e: e.reciprocal(out.ap, in_.ap), _keys(in_), _keys(out))

    def reduce(self, out, in_, op, axis=AX.X, eng="dve", absval=None):
        self.add(eng, lambda e: e.tensor_reduce(out.ap, in_.ap, axis, op, apply_absolute_value=absval),
                 _keys(in_), _keys(out))

    def bn_stats(self, out, in_):
        self.add("dve", lambda e: e.bn_stats(out.ap, in_.ap), _keys(in_), _keys(out))

    def bn_aggr(self, out, in_):
        self.add("dve", lambda e: e.bn_aggr(out.ap, in_.ap), _keys(in_), _keys(out))

    def iota(self, out, pattern, base, cm):
        self.add("pool", lambda e: e.iota(out.ap, pattern, base=base, channel_multiplier=cm,
                                          allow_small_or_imprecise_dtypes=True), (), _keys(out))

    def dma(self, out, in_, slow=False):
        kw = {"allow_slow_non_contiguous": True} if slow else {}
        self.add("sp", lambda e: e.dma_start(_a(out), _a(in_), **kw), _keys(in_), _keys(out), dma=True)

    def emit(self):
        K = self.K
        nc = K.nc
        ops = self.ops
        last_w = {}
        rd = {}
        for i, op in enumerate(ops):
            deps = {}

            def add(j, kind, op=op, deps=deps, i=i):
                if j is None or j == i:
                    return
                Pp = ops[j]
                if Pp.eng == op.eng and not Pp.dma and not op.dma:
                    if op.eng == "pe":
                        return
                deps[j] = True

            for k in op.reads:
                add(last_w.get(k), "RAW")
            for k in op.writes:
                add(last_w.get(k), "WAW")
                r = rd.get(k)
                if r:
                    for j in r.values():
                        add(j, "WAR")
            op.deps = tuple(deps)
            for j in deps:
                ops[j].signal = True
            for k in op.reads:
                r = rd.get(k)
                if r is None:
                    r = rd[k] = {}
                r[("d", i) if op.dma else op.eng] = i
            for k in op.writes:
                last_w[k] = i
                rd[k] = {}
        if self.name == "w":
            if not hasattr(K, "w_sems"):
                K.w_sems = {e: nc.alloc_semaphore(name=f"phw_{e}") for e in ("pe", "act", "dve", "pool")}
                K.w_cnt = {e: 0 for e in K.w_sems}
            esem, ecnt = K.w_sems, K.w_cnt
        else:
            esem = {e: nc.alloc_semaphore(name=f"ph{K.phase_i}_{e}") for e in ("pe", "act", "dve", "pool")}
            K.phase_i += 1
            ecnt = {e: 0 for e in esem}
        for op in ops:
            if op.dma:
                n = K.dma_n
                K.dma_n += 1
                s = n % K.ndma
                K.dma_cnt[s] += 16
                op.sem = K.dma_sems[s]
                op.count = K.dma_cnt[s]
                if op.count > 16:
                    op.pre = (op.sem, op.count - 16)
            elif op.signal:
                ecnt[op.eng] += 1
                op.sem = esem[op.eng]
                op.count = ecnt[op.eng]
        by_eng = {e: [] for e in ENGS}
        for op in ops:
            by_eng[op.eng].append(op)
        K.n_inst += len(ops)

        def run(e, name):
            waited = K.waited[name]
            for op in by_eng[name]:
                need = {}
                if op.pre is not None:
                    need[id(op.pre[0])] = op.pre
                for j in op.deps:
                    Pp = ops[j]
                    cur = need.get(id(Pp.sem))
                    if cur is None or cur[1] < Pp.count:
                        need[id(Pp.sem)] = (Pp.sem, Pp.count)
                for sid, (sem, val) in need.items():
                    if waited.get(sid, 0) < val:
                        e.wait_ge(sem, val)
                        waited[sid] = val
                inst = op.fn(e)
                if op.dma:
                    inst.then_inc(op.sem, 16)
                elif op.signal:
                    inst.then_inc(op.sem, 1)
            if name == "sp":
                for s in range(K.ndma):
                    if K.dma_cnt[s] > waited.get(id(K.dma_sems[s]), 0):
                        e.wait_ge(K.dma_sems[s], K.dma_cnt[s])
                        waited[id(K.dma_sems[s])] = K.dma_cnt[s]

        with nc.Block() as block:
            @block.tensor
            def _(e):
                run(e, "pe")

            @block.scalar
            def _(e):
                run(e, "act")

            @block.vector
            def _(e):
                run(e, "dve")

            @block.gpsimd
            def _(e):
                run(e, "pool")

            @block.sync
            def _(e):
                run(e, "sp")
        self.ops = []


class Alloc:
    _n = [0]

    def __init__(self, nc, stack):
        self.nc = nc
        self.stack = stack

    def sb(self, shape, dt, name=None):
        Alloc._n[0] += 1
        nm = f"{name or 't'}_{Alloc._n[0]}"
        t = self.stack.enter_context(self.nc.sbuf_tensor(nm, list(shape), dt))
        return V(t[:] if hasattr(t, "__getitem__") else t.ap(), nm)

    def ps(self, shape, dt, name=None):
        Alloc._n[0] += 1
        nm = f"{name or 'p'}_{Alloc._n[0]}"
        t = self.stack.enter_context(self.nc.psum_tensor(nm, list(shape), dt))
        return V(t[:] if hasattr(t, "__getitem__") else t.ap(), nm)


class Ctx:
    pass


def load_weight(C, dst, src, ncols, kcs):
    import contextlib
    nc = C.nc
    with contextlib.ExitStack() as st:
        A = Alloc(nc, st)
        CH = min(ncols, 2048)
        stage = [A.sb([128, CH], F32, "wst") for _ in range(3)]
        P = Phase(C.K, "w")
        i = 0
        for kc in range(kcs):
            for c0 in range(0, ncols, CH):
                sv = stage[i % 3]
                P.dma(sv, src[kc * 128:(kc + 1) * 128, c0:c0 + CH])
                P.copy(dst[:, kc, c0:c0 + CH], sv, eng=("dve" if i % 2 == 0 else "pool"))
                i += 1
        P.emit()


def load_consts(P, A, C):
    ident32 = A.sb([128, 128], F32, "id32")
    ident = A.sb([128, 128], BF16, "id")
    mhalf = A.sb([128, 1], F32, "mh")
    P.dma(ident32, C.ident)
    P.copy(ident, ident32)
    P.memset(mhalf, -0.5)
    return ident, mhalf


def load_mod(P, A, C, layer, which, want_pp=True, want_gate=True):
    off = 0 if which == 1 else 3 * D
    ng = C.norm1_g if which == 1 else C.norm2_g
    gpp = A.sb([128, 8], F32, "gpp")
    P.dma(gpp, ng[layer].rearrange("(kc p) -> p kc", p=128), slow=True)
    a_pp, b_pp, G = [], [], []
    for s in range(C.NSEQ):
        row = C.MOD[layer, s]
        if want_pp:
            sc = A.sb([128, 8], F32, "scpp")
            a = A.sb([128, 8], F32, "app")
            b = A.sb([128, 8], F32, "bpp")
            P.dma(b, row[off:off + D].rearrange("(kc p) -> p kc", p=128), slow=True)
            P.dma(sc, row[off + D:off + 2 * D].rearrange("(kc p) -> p kc", p=128), slow=True)
            P.stt(a, sc, 1.0, gpp, ALU.add, ALU.mult)
            a_pp.append(a)
            b_pp.append(b)
        if want_gate:
            g = A.sb([128, D], F32, "gate")
            P.dma(g, row[off + 2 * D:off + 3 * D].partition_broadcast(128))
            G.append(g)
    return a_pp, b_pp, G


def front1(P, xt, ss, junk, xn, mhalf, xn_eng="dve"):
    P.act(junk, xt, AF.Square, accum=ss[:, 0:1])
    P.ts(ss[:, 1:2], ss[:, 0:1], 1.0 / D, EPS, ALU.mult, ALU.add)
    P.tt(ss[:, 2:3], ss[:, 1:2], mhalf, ALU.pow, eng="pool")
    if xn_eng == "act":
        P.act(xn, xt, AF.Identity, scale=ss[:, 2:3])
    else:
        P.ts(xn, xt, ss[:, 2:3], None, ALU.mult)


def front2(P, xn, ident, pT, hT, a_pp, b_pp, ev_eng="dve"):
    for kc in range(8):
        P.tr(pT[:, kc, :], xn[:, kc * 128:(kc + 1) * 128], ident)
    for kc in range(8):
        if ev_eng == "act":
            P.act(hT[:, kc, :], pT[:, kc, :], AF.Identity, bias=b_pp[:, kc:kc + 1], scale=a_pp[:, kc:kc + 1])
        else:
            P.ts(hT[:, kc, :], pT[:, kc, :], a_pp[:, kc:kc + 1], b_pp[:, kc:kc + 1], ALU.mult, ALU.add)


def front(P, xt, ss, junk, xn, ident, mhalf, pT, hT, a_pp, b_pp, ev_eng="dve"):
    front1(P, xt, ss, junk, xn, mhalf)
    front2(P, xn, ident, pT, hT, a_pp, b_pp, ev_eng)


def phase_mod(C):
    import contextlib
    nc = C.nc
    NS = C.NSEQ
    with contextlib.ExitStack() as st:
        A = Alloc(nc, st)
        P = Phase(C.K, "mod")
        ct32 = A.sb([128, 8, NS], F32)
        sct = A.sb([128, 8, NS], BF16)
        P.dma(ct32, C.cT)
        P.act(sct, ct32, AF.Silu)
        stage = [A.sb([128, 8, 512], F32, "mst") for _ in range(2)]
        wb = [A.sb([128, 8, 512], BF16, "mwb") for _ in range(2)]
        bias = A.sb([NS, 2, 6 * D], F32)
        res = A.sb([NS, 2, 6 * D], F32)
        pp = [A.ps([128, 512], F32) for _ in range(2)]
        for l in range(2):
            for s in range(NS):
                P.dma(bias[s:s + 1, l, :], C.mod_b[l:l + 1, :])
        i = 0
        for l in range(2):
            for nt in range(12):
                sv, wv, pv = stage[i % 2], wb[i % 2], pp[i % 2]
                P.dma(sv, C.mod_w[l, :, nt * 512:(nt + 1) * 512].rearrange("(kc p) n -> p kc n", p=128))
                P.copy(wv, sv, eng=("dve" if i % 2 == 0 else "pool"))
                for kc in range(8):
                    P.mm(pv[0:NS, :], sct[:, kc, :], wv[:, kc, :], start=(kc == 0), stop=(kc == 7))
                P.tt(res[:, l, nt * 512:(nt + 1) * 512], pv[0:NS, :], bias[:, l, nt * 512:(nt + 1) * 512], ALU.add)
                i += 1
        for l in range(2):
            P.dma(C.MOD[l], res[:, l, :])
        P.emit()


def phase_mlp(C, layer, Xin, Xout, final):
    import contextlib
    nc = C.nc
    NS, NT = C.NSEQ, C.NT
    TS = 2
    with contextlib.ExitStack() as st0:
        A0 = Alloc(nc, st0)
        W1 = A0.sb([128, 8, DFF], BF16, "W1")
        W2 = A0.sb([128, 32, D], BF16, "W2")
        load_weight(C, W1, C.mlp_w1[layer], DFF, 8)
        load_weight(C, W2, C.mlp_w2[layer], D, 32)
        with contextlib.ExitStack() as st:
            A = Alloc(nc, st)
            P = Phase(C.K, "mlp")
            ident, mhalf = load_consts(P, A, C)
            a_pp, b_pp, G = load_mod(P, A, C, layer, 2)
            if final:
                FG = A.sb([128, D], F32, "FG")
                P.dma(FG, C.final_g.partition_broadcast(128))
            xt = [A.sb([128, D], F32, "xt") for _ in range(3 * TS)]
            xn = [A.sb([128, D], BF16, "xn") for _ in range(2)]
            junk = A.sb([128, D], BF16, "junk")
            ss = [A.sb([128, 4], F32, "ss") for _ in range(4)]
            hT = [A.sb([128, 8, TS * 128], BF16, "hT") for _ in range(2)]
            uT = A.sb([128, 32, TS * 128], BF16, "uT")
            r32 = [A.sb([128, TS * 128], F32, "r32") for _ in range(2)]
            tmp = [A.sb([128, 512], F32, "tmp") for _ in range(2)]
            pT = [A.ps([128, 8, 128], BF16, "pT") for _ in range(2)]
            pu = [A.ps([128, 512], F32, "pu") for _ in range(3)]
            po = [A.ps([128, 512], F32, "po") for _ in range(3)]
            items = [(s, sc) for s in range(NS) for sc in range(NT // TS)]
            cnt = [0]

            def xs_of(i):
                return [xt[(i % 3) * TS + j] for j in range(TS)]

            def do_load(i):
                s, sc = items[i]
                for j in range(TS):
                    c = sc * TS + j
                    P.dma(xs_of(i)[j], Xin[s, c * 128:(c + 1) * 128, :])

            fk = {}

            def do_front1(i):
                ks = []
                for j in range(TS):
                    k = cnt[0]
                    cnt[0] += 1
                    ks.append(k)
                    front1(P, xs_of(i)[j], ss[k % 4], junk, xn[k % 2], mhalf)
                fk[i] = ks

            def do_front2(i):
                s, sc = items[i]
                for j in range(TS):
                    k = fk[i][j]
                    front2(P, xn[k % 2], ident, pT[k % 2], hT[i % 2][:, :, j * 128:(j + 1) * 128],
                           a_pp[s], b_pp[s], ev_eng=("act" if k % 2 else "dve"))

            def do_w1(i):
                hv = hT[i % 2]
                for fc in range(32):
                    pv = pu[fc % 3]
                    for kc in range(8):
                        P.mm(pv[:, 0:TS * 128], W1[:, kc, fc * 128:(fc + 1) * 128], hv[:, kc, :],
                             start=(kc == 0), stop=(kc == 7))
                    rv = r32[fc % 2]
                    P.act(rv, pv[:, 0:TS * 128], AF.Relu)
                    P.tt(uT[:, fc, :], rv, rv, ALU.mult, eng=("dve" if fc % 2 == 0 else "pool"))

            def do_w2(i):
                s, sc = items[i]
                xs = xs_of(i)
                for j in range(TS):
                    for half in range(2):
                        pv = po[(j * 2 + half) % 3]
                        for fc in range(32):
                            P.mm(pv, uT[:, fc, j * 128:(j + 1) * 128], W2[:, fc, half * 512:(half + 1) * 512],
                                 start=(fc == 0), stop=(fc == 31))
                        tv = tmp[half]
                        P.tt(tv, pv, G[s][:, half * 512:(half + 1) * 512], ALU.mult)
                        P.tt(xs[j][:, half * 512:(half + 1) * 512], tv, xs[j][:, half * 512:(half + 1) * 512],
                             ALU.add, eng="pool")
                    c = sc * TS + j
                    if final:
                        k = cnt[0]
                        cnt[0] += 1
                        sv = ss[k % 4]
                        P.act(junk, xs[j], AF.Square, accum=sv[:, 0:1])
                        P.ts(sv[:, 1:2], sv[:, 0:1], 1.0 / D, EPS, ALU.mult, ALU.add)
                        P.tt(sv[:, 2:3], sv[:, 1:2], mhalf, ALU.pow, eng="pool")
                        P.stt(xs[j], xs[j], sv[:, 2:3], FG, ALU.mult, ALU.mult)
                    P.dma(Xout[s, c * 128:(c + 1) * 128, :], xs[j])

            n = len(items)
            do_load(0)
            if n > 1:
                do_load(1)
            do_front1(0)
            do_front2(0)
            for i in range(n):
                if i + 2 < n:
                    do_load(i + 2)
                if i + 1 < n:
                    do_front1(i + 1)
                do_w1(i)
                if i + 1 < n:
                    do_front2(i + 1)
                do_w2(i)
            P.emit()


def decay_tables(P, A, C, want):
    T = {}
    dl = A.sb([128, 8], F32, "dl")
    lg = A.sb([128, 8], F32, "lg")
    P.dma(dl, C.ret_decay.partition_broadcast(128))
    P.act(lg, dl, AF.Exp, scale=-1.0)
    P.ts(lg, lg, 1.0, None, ALU.add)
    P.act(lg, lg, AF.Ln)
    P.ts(lg, lg, -1.0, None, ALU.mult)
    T["lg"] = lg
    pidx = A.sb([128, 1], F32, "pidx")
    P.iota(pidx, [[0, 1]], 0, 1)
    jidx = A.sb([128, 128], F32, "jidx")
    P.iota(jidx, [[1, 128]], 0, 0)
    scale = float(RET_DK) ** -0.5
    if "pp" in want:
        rp = A.sb([128, 1], F32, "rp")
        P.ts(rp, pidx, -1.0, 127.0, ALU.mult, ALU.add)
        kdf = A.sb([128, 4], F32, "kdf")
        kdb = A.sb([128, 4], F32, "kdb")
        for h in range(4):
            P.act(kdf[:, h:h + 1], rp, AF.Exp, scale=lg[:, h:h + 1])
            P.act(kdb[:, h:h + 1], pidx, AF.Exp, scale=lg[:, 4 + h:5 + h])
        T["kdf"], T["kdb"] = kdf, kdb
        j1 = A.sb([128, 128], F32, "j1")
        jr = A.sb([128, 128], F32, "jr")
        P.ts(j1, jidx, 1.0, None, ALU.add)
        P.ts(jr, jidx, -1.0, 128.0, ALU.mult, ALU.add)
        qdf = A.sb([128, 8, 128], F32, "qdf")
        qdb = A.sb([128, 8, 128], F32, "qdb")
        for h in range(4):
            for dc in range(2):
                P.act(qdf[:, 2 * h + dc, :], j1, AF.Exp, scale=lg[:, h:h + 1])
                P.act(qdb[:, 2 * h + dc, :], jr, AF.Exp, scale=lg[:, 4 + h:5 + h])
        P.ts(qdf, qdf, scale, None, ALU.mult)
        P.ts(qdb, qdb, scale, None, ALU.mult)
        T["qdf"], T["qdb"] = qdf, qdb
    if "mask" in want:
        diff = A.sb([128, 128], F32, "diff")
        P.ts(diff, jidx, pidx, None, ALU.subtract)
        dpos = A.sb([128, 128], F32, "dpos")
        dneg = A.sb([128, 128], F32, "dneg")
        mge = A.sb([128, 128], F32, "mge")
        P.ts(dpos, diff, 0.0, None, ALU.max)
        P.tt(dneg, dpos, diff, ALU.subtract)
        P.ts(mge, diff, 0.0, None, ALU.is_ge)
        DT = A.sb([128, 4, 128], F32, "DT")
        ea = A.sb([128, 128], F32, "ea")
        eb = A.sb([128, 128], F32, "eb")
        for h in range(4):
            P.act(ea, dpos, AF.Exp, scale=lg[:, h:h + 1])
            P.act(eb, dneg, AF.Exp, scale=lg[:, 4 + h:5 + h])
            P.tt(ea, ea, eb, ALU.subtract)
            P.tt(ea, ea, mge, ALU.mult)
            P.tt(ea, ea, eb, ALU.add)
            P.ts(DT[:, h, :], ea, scale, None, ALU.mult)
        T["DT"] = DT
        cd = A.sb([128, 8], F32, "cdec")
        P.act(cd, lg, AF.Exp, scale=128.0)
        T["cdec"] = cd
    return T


def phase_l0a(C):
    import contextlib
    nc = C.nc
    NS, NT = C.NSEQ, C.NT
    with contextlib.ExitStack() as st0:
        A0 = Alloc(nc, st0)
        W = A0.sb([128, 8, RET_IN], BF16, "Win")
        load_weight(C, W, C.ret_w_in, RET_IN, 8)
        with contextlib.ExitStack() as st:
            A = Alloc(nc, st)
            P = Phase(C.K, "l0a")
            ident, mhalf = load_consts(P, A, C)
            a_pp, b_pp, _G = load_mod(P, A, C, 0, 1, want_gate=False)
            T = decay_tables(P, A, C, ("pp",))
            xt = [A.sb([128, D], F32, "xt") for _ in range(3)]
            cs = [A.sb([128, 2, 2, 64], F32, "cs") for _ in range(3)]
            xn = [A.sb([128, D], BF16, "xn") for _ in range(2)]
            junk = A.sb([128, D], BF16, "junk")
            ss = [A.sb([128, 4], F32, "ss") for _ in range(4)]
            hT = [A.sb([128, 8, 128], BF16, "hT") for _ in range(2)]
            qk32 = [A.sb([128, 2048], F32, "qk32")] * 2
            t1 = A.sb([128, 8, 2, 64], F32, "t1")
            t2 = A.sb([128, 8, 2, 64], F32, "t2")
            t3, t4 = t1, t2
            qkr = [A.sb([128, 2048], BF16, "qkr") for _ in range(2)]
            kf = [A.sb([128, 4, 256], BF16, "kf") for _ in range(2)]
            kb = [A.sb([128, 4, 256], BF16, "kb") for _ in range(2)]
            vo = [A.sb([128, 2048], BF16, "vo") for _ in range(2)]
            sg = [A.sb([128, 2048], BF16, "sg") for _ in range(2)]
            qo = [A.sb([128, 3, 8, 128], BF16, "qo") for _ in range(2)]
            ko = [A.sb([128, 8, 128], BF16, "ko") for _ in range(2)]
            pT = [A.ps([128, 8, 128], BF16, "pT") for _ in range(2)]
            pp = [A.ps([128, 512], F32, "pp") for _ in range(4)]
            pq = A.ps([128, 16, 128], BF16, "pq")
            items = [(s, c) for s in range(NS) for c in range(NT)]

            def do_load(i):
                s, c = items[i]
                P.dma(xt[i % 3], C.x[s, c * 128:(c + 1) * 128, :])
                P.dma(cs[i % 3], C.rope_r[c])

            def do_A1(i):
                front1(P, xt[i % 3], ss[i % 4], junk, xn[i % 2], mhalf)

            def do_A2(i):
                s, c = items[i]
                front2(P, xn[i % 2], ident, pT[i % 2], hT[i % 2], a_pp[s], b_pp[s], ev_eng="dve")

            def do_B(i, cts):
                s, c = items[i]
                sl = i % 2
                hv = hT[sl]
                for ct in cts:
                    pv = pp[ct % 4]
                    for kc in range(8):
                        P.mm(pv, hv[:, kc, :], W[:, kc, ct * 512:(ct + 1) * 512], start=(kc == 0), stop=(kc == 7))
                    if ct < 4:
                        P.copy(qk32[sl][:, ct * 512:(ct + 1) * 512], pv, eng="act")
                    elif ct < 8:
                        P.copy(vo[sl][:, (ct - 4) * 512:(ct - 3) * 512], pv, eng="dve")
                    else:
                        P.act(sg[sl][:, (ct - 8) * 512:(ct - 7) * 512], pv, AF.Silu)

            def do_C(i):
                s, c = items[i]
                sl = i % 2
                v5 = qk32[sl].re("p (h f a d) -> p h f a d", h=8, f=2, a=2)
                o5 = qkr[sl].re("p (h f a d) -> p h f a d", h=8, f=2, a=2)
                a_, b_ = v5[:, :, :, 0, :], v5[:, :, :, 1, :]
                csv = cs[i % 3]
                cos = V(csv.ap[:, 0].unsqueeze(1).broadcast_to([128, 8, 2, 64]), csv.key)
                sin = V(csv.ap[:, 1].unsqueeze(1).broadcast_to([128, 8, 2, 64]), csv.key)
                P.tt(t1, a_, cos, ALU.mult, eng="dve")
                P.tt(t2, b_, sin, ALU.mult, eng="pool")
                P.tt(o5[:, :, :, 0, :], t1, t2, ALU.subtract, eng="dve")
                P.tt(t3, a_, sin, ALU.mult, eng="pool")
                P.tt(t4, b_, cos, ALU.mult, eng="dve")
                P.tt(o5[:, :, :, 1, :], t3, t4, ALU.add, eng="pool")
                kv = qkr[sl][:, 1024:2048].re("p (h d) -> p h d", h=4)
                P.tt(kf[sl], kv, V(T["kdf"].ap.unsqueeze(2).broadcast_to([128, 4, 256]), T["kdf"].key), ALU.mult, eng="pool")
                P.tt(kb[sl], kv, V(T["kdb"].ap.unsqueeze(2).broadcast_to([128, 4, 256]), T["kdb"].key), ALU.mult, eng="pool")

            def do_D(i):
                s, c = items[i]
                sl = i % 2
                for j in range(16):
                    P.tr(pq[:, j, :], qkr[sl][:, j * 128:(j + 1) * 128], ident)
                P.copy(qo[sl][:, 0], pq[:, 0:8, :], eng="dve")
                P.tt(qo[sl][:, 1], pq[:, 0:8, :], T["qdf"], ALU.mult, eng="dve")
                P.tt(qo[sl][:, 2], pq[:, 0:8, :], T["qdb"], ALU.mult, eng="dve")
                P.copy(ko[sl], pq[:, 8:16, :], eng="act")
                P.dma(C.QT[s, c].rearrange("v p h d t -> p v (h d) t"), qo[sl])
                P.dma(C.KT[s, c].rearrange("p h d t -> p (h d) t"), ko[sl])
                P.dma(C.KF[s, c], kf[sl])
                P.dma(C.KB[s, c], kb[sl])
                P.dma(C.VS[s, c].rearrange("p h d -> p (h d)"), vo[sl])
                P.dma(C.SG[s, c].rearrange("p h d -> p (h d)"), sg[sl])

            n = len(items)
            do_load(0)
            if n > 1:
                do_load(1)
            do_A1(0)
            do_A2(0)
            for i in range(n):
                if i + 2 < n:
                    do_load(i + 2)
                do_B(i, range(0, 4))
                if i + 1 < n:
                    do_A1(i + 1)
                do_C(i)
                do_B(i, range(4, 8))
                if i + 1 < n:
                    do_A2(i + 1)
                do_B(i, range(8, 12))
                do_D(i)
            P.emit()


def phase_l0b(C):
    import contextlib
    nc = C.nc
    NS, NT = C.NSEQ, C.NT
    GC = 4
    NG = NT // GC
    with contextlib.ExitStack() as st:
        A = Alloc(nc, st)
        P = Phase(C.K, "l0b")
        ident, mhalf = load_consts(P, A, C)
        T = decay_tables(P, A, C, ("mask",))
        DT, cdec = T["DT"], T["cdec"]
        epsv = A.sb([128, 1], F32, "epsv")
        P.memset(epsv, EPS)
        SbA = A.sb([128, NT, 2, 512], BF16, "SbA")
        SbK = [V(SbA.ap[:, c], ("SbA", c)) for c in range(NT)]
        S32 = [A.sb([128, 2, 512], F32, "S32") for _ in range(2)]
        sj = [0]
        NSF = 4
        Sf = [A.sb([128, 2, 512], BF16, "Sf") for _ in range(NSF)]
        bk = [A.sb([128, GC, 256], BF16, "bk") for _ in range(2)]
        bv = [A.sb([128, GC, 512], BF16, "bv") for _ in range(2)]
        fq = [A.sb([128, GC, 3, 2, 128], BF16, "fq") for _ in range(3)]
        fkT = [A.sb([128, GC, 2, 128], BF16, "fkT") for _ in range(3)]
        fkf = [A.sb([128, GC, 256], BF16, "fkf") for _ in range(3)]
        fv = [A.sb([128, GC, 512], BF16, "fv") for _ in range(3)]
        fsg = [A.sb([128, GC, 512], BF16, "fsg") for _ in range(3)]
        sTm = [A.sb([128, 128], BF16, "sTm") for _ in range(2)]
        st6 = [A.sb([128, 6], F32, "st6") for _ in range(2)]
        mv = [A.sb([128, 8], F32, "mv") for _ in range(2)]
        yn = [A.sb([128, 512], F32, "yn") for _ in range(2)]
        yb = [A.sb([128, 512], BF16, "yb") for _ in range(2)]
        yTa = [A.sb([128, 4, GC * 128], BF16, "yTa") for _ in range(3)]
        pss = A.ps([128, 512], F32, "pss")
        po = [A.ps([128, 512], F32, "po") for _ in range(2)]
        pk = [A.ps([128, 2, 512], F32, "pk") for _ in range(2)]
        pyT = A.ps([128, 8, 128], BF16, "pyT")
        groups = []
        for s in range(NS):
            for h in range(RET_H):
                for g in range(NG - 1, -1, -1):
                    groups.append(("b", s, h, g))
                for g in range(NG):
                    groups.append(("f", s, h, g))
        cntb = [0]
        cntf = [0]
        slot_of = {}

        def do_load(k):
            kind, s, h, g = groups[k]
            c0 = g * GC
            if kind == "b":
                sl = cntb[0] % 2
                cntb[0] += 1
                slot_of[k] = sl
                P.dma(bk[sl], C.KB[s, c0:c0 + GC, :, h, :].rearrange("c p d -> p c d"))
                P.dma(bv[sl], C.VS[s, c0:c0 + GC, :, h, :].rearrange("c p d -> p c d"))
            else:
                sl = cntf[0] % 3
                cntf[0] += 1
                slot_of[k] = sl
                P.dma(fq[sl], C.QT[s, c0:c0 + GC, :, :, h, :, :].rearrange("c v p d t -> p c v d t"))
                P.dma(fkT[sl], C.KT[s, c0:c0 + GC, :, h, :, :].rearrange("c p d t -> p c d t"))
                P.dma(fkf[sl], C.KF[s, c0:c0 + GC, :, h, :].rearrange("c p d -> p c d"))
                P.dma(fv[sl], C.VS[s, c0:c0 + GC, :, h, :].rearrange("c p d -> p c d"))
                P.dma(fsg[sl], C.SG[s, c0:c0 + GC, :, h, :].rearrange("c p d -> p c d"))

        kvi = [0]
        ci = [0]
        q_gn2 = []
        q_gate = []
        q_tail = []

        def flush():
            while q_gn2:
                q_gn2.pop(0)()
            while q_gate:
                q_gate.pop(0)()
            while q_tail:
                q_tail.pop(0)()

        do_load(0)
        for k, (kind, s, h, g) in enumerate(groups):
            if k + 1 < len(groups):
                do_load(k + 1)
            sl = slot_of[k]
            c0 = g * GC
            if kind == "b":
                if g == NG - 1:
                    flush()
                    P.memset(S32[sj[0] % 2], 0.0)
                    P.memset(SbK[NT - 1], 0.0)
                for cc in range(GC - 1, -1, -1):
                    c = c0 + cc
                    if c == 0:
                        continue
                    pkv = pk[kvi[0] % 2]
                    kvi[0] += 1
                    for dc in range(2):
                        P.mm(pkv[:, dc, :], bk[sl][:, cc, dc * 128:(dc + 1) * 128], bv[sl][:, cc, :])
                    sprev, scur = S32[sj[0] % 2], S32[(sj[0] + 1) % 2]
                    sj[0] += 1
                    P.stt(scur, sprev, cdec[:, 4 + h:5 + h], pkv, ALU.mult, ALU.add)
                    P.copy(SbK[c - 1], scur, eng="act")
            else:
                if g == 0:
                    P.memset(S32[sj[0] % 2], 0.0)
                    P.memset(Sf[ci[0] % NSF], 0.0)
                for cc in range(GC):
                    c = c0 + cc
                    k2 = ci[0] % 2
                    sfc, sfn = Sf[ci[0] % NSF], Sf[(ci[0] + 1) % NSF]
                    ci[0] += 1
                    if c < NT - 1:
                        pkv = pk[kvi[0] % 2]
                        kvi[0] += 1
                        for dc in range(2):
                            P.mm(pkv[:, dc, :], fkf[sl][:, cc, dc * 128:(dc + 1) * 128], fv[sl][:, cc, :])
                    for dc in range(2):
                        P.mm(pss[:, 0:128], fkT[sl][:, cc, dc, :], fq[sl][:, cc, 0, dc, :],
                             start=(dc == 0), stop=(dc == 1))
                    P.tt(sTm[k2], pss[:, 0:128], DT[:, h, :], ALU.mult)
                    if q_gn2:
                        q_gn2.pop(0)()
                    pov = po[k2]
                    P.mm(pov, sTm[k2], fv[sl][:, cc, :], start=True, stop=False)
                    for dc in range(2):
                        P.mm(pov, fq[sl][:, cc, 1, dc, :], sfc[:, dc, :], start=False, stop=False)
                    for dc in range(2):
                        P.mm(pov, fq[sl][:, cc, 2, dc, :], SbK[c][:, dc, :], start=False, stop=(dc == 1))
                    if q_tail:
                        q_tail.pop(0)()
                    if c < NT - 1:
                        sprev, scur = S32[sj[0] % 2], S32[(sj[0] + 1) % 2]
                        sj[0] += 1
                        P.stt(scur, sprev, cdec[:, h:h + 1], pkv, ALU.mult, ALU.add)
                        P.copy(sfn, scur, eng="act")
                    m = mv[k2]
                    P.bn_stats(st6[k2], pov)
                    P.bn_aggr(m[:, 0:2], st6[k2])
                    P.ts(m[:, 2:3], m[:, 1:2], epsv, None, ALU.add)
                    P.tt(m[:, 3:4], m[:, 2:3], mhalf, ALU.pow, eng="pool")
                    if q_gate:
                        q_gate.pop(0)()

                    def tail(k2=k2, sl=sl, cc=cc, s=s, h=h, c0=c0):
                        for ec in range(4):
                            P.tr(pyT[:, ec, :], yb[k2][:, ec * 128:(ec + 1) * 128], ident)
                        P.copy(yTa[sl][:, :, cc * 128:(cc + 1) * 128], pyT[:, 0:4, :], eng="act")
                        if cc == GC - 1:
                            P.dma(C.YT[s, :, h * 4:(h + 1) * 4, c0 * 128:(c0 + GC) * 128], yTa[sl])

                    def gate(k2=k2, sl=sl, cc=cc, tail=tail):
                        P.tt(yb[k2], yn[k2], fsg[sl][:, cc, :], ALU.mult, eng="pool")
                        q_tail.append(tail)

                    def gn2(k2=k2, m=m, pov=pov, gate=gate):
                        P.ts(m[:, 4:5], m[:, 0:1], m[:, 3:4], -1.0, ALU.mult, ALU.mult)
                        P.act(yn[k2], pov, AF.Identity, bias=m[:, 4:5], scale=m[:, 3:4])
                        q_gate.append(gate)
                    q_gn2.append(gn2)
        flush()
        P.emit()


def phase_l0c(C, Xin, Xout):
    import contextlib
    nc = C.nc
    NS, NT = C.NSEQ, C.NT
    TS = 4
    with contextlib.ExitStack() as st0:
        A0 = Alloc(nc, st0)
        W = A0.sb([128, 16, D], BF16, "Wro")
        load_weight(C, W, C.ret_w_out, D, 16)
        with contextlib.ExitStack() as st:
            A = Alloc(nc, st)
            P = Phase(C.K, "l0c")
            _a, _b, G = load_mod(P, A, C, 0, 1, want_pp=False)
            yT = [A.sb([128, 16, TS * 128], BF16, "yT") for _ in range(2)]
            xt = [A.sb([128, D], F32, "xt") for _ in range(2 * TS)]
            tmp = [A.sb([128, 512], F32, "tmp") for _ in range(2)]
            po = [A.ps([128, 512], F32, "po") for _ in range(4)]
            items = [(s, sc) for s in range(NS) for sc in range(NT // TS)]

            def do_load(i):
                s, sc = items[i]
                sl = i % 2
                P.dma(yT[sl], C.YT[s, :, :, sc * TS * 128:(sc + 1) * TS * 128])
                for j in range(TS):
                    c = sc * TS + j
                    P.dma(xt[sl * TS + j], Xin[s, c * 128:(c + 1) * 128, :])

            do_load(0)
            for i, (s, sc) in enumerate(items):
                sl = i % 2
                if i + 1 < len(items):
                    do_load(i + 1)
                for j in range(TS):
                    c = sc * TS + j
                    xv = xt[sl * TS + j]
                    for half in range(2):
                        pv = po[(j * 2 + half) % 4]
                        for ec in range(16):
                            P.mm(pv, yT[sl][:, ec, j * 128:(j + 1) * 128], W[:, ec, half * 512:(half + 1) * 512],
                                 start=(ec == 0), stop=(ec == 15))
                        P.tt(tmp[half], pv, G[s][:, half * 512:(half + 1) * 512], ALU.mult)
                        P.tt(xv[:, half * 512:(half + 1) * 512], tmp[half], xv[:, half * 512:(half + 1) * 512],
                             ALU.add, eng="pool")
                    P.dma(Xout[s, c * 128:(c + 1) * 128, :], xv)
            P.emit()


def phase_l1a(C, Xin):
    import contextlib
    nc = C.nc
    NS, NT = C.NSEQ, C.NT
    GC = 4
    with contextlib.ExitStack() as st0:
        A0 = Alloc(nc, st0)
        W = A0.sb([128, 8, ATT_IN], BF16, "Wai")
        load_weight(C, W, C.att_w_in, ATT_IN, 8)
        with contextlib.ExitStack() as st:
            A = Alloc(nc, st)
            P = Phase(C.K, "l1a")
            ident, mhalf = load_consts(P, A, C)
            a_pp, b_pp, _G = load_mod(P, A, C, 1, 1, want_gate=False)
            gains = A.sb([128, 10, 128], F32, "gains")
            for hh in range(10):
                src = C.att_q_gain if hh < 8 else C.att_k_gain
                P.dma(gains[:, hh, :], src.partition_broadcast(128))
            xt = [A.sb([128, D], F32, "xt") for _ in range(3)]
            cs = [A.sb([128, 2, 2, 32], F32, "cs") for _ in range(4)]
            xn = [A.sb([128, D], BF16, "xn") for _ in range(2)]
            junk = A.sb([128, D], BF16, "junk")
            ss = [A.sb([128, 4], F32, "ss") for _ in range(4)]
            hT = [A.sb([128, 8, 128], BF16, "hT") for _ in range(2)]
            qk32_ = [A.sb([128, 10, 128], F32, "qk32") for _ in range(2)]
            sq_ = [A.sb([128, 10, 128], F32, "sq") for _ in range(2)]
            s10 = [A.sb([128, 32], F32, "s10") for _ in range(2)]
            qkn_ = [A.sb([128, 10, 128], F32, "qkn") for _ in range(2)]
            t1_ = [A.sb([128, 10, 2, 32], F32, "t1") for _ in range(2)]
            t2_ = [A.sb([128, 10, 2, 32], F32, "t2") for _ in range(2)]
            t3_ = [A.sb([128, 10, 2, 32], F32, "t3") for _ in range(2)]
            t4_ = [A.sb([128, 10, 2, 32], F32, "t4") for _ in range(2)]
            qkr = [A.sb([128, 10 * 128], BF16, "qkr") for _ in range(2)]
            vb = [A.sb([128, 256], BF16, "vb") for _ in range(2)]
            acc = [A.sb([128, 10, GC * 128], BF16, "acc") for _ in range(2)]
            pT = [A.ps([128, 8, 128], BF16, "pT") for _ in range(2)]
            pp = [A.ps([128, 512], F32, "pp") for _ in range(3)]
            pq = A.ps([128, 16, 128], BF16, "pq")
            items = [(s, c) for s in range(NS) for c in range(NT)]

            def do_load(i):
                s, c = items[i]
                P.dma(xt[i % 3], Xin[s, c * 128:(c + 1) * 128, :])
                P.dma(cs[i % 4], C.rope_a[c])

            def do_A1(i):
                front1(P, xt[i % 3], ss[i % 4], junk, xn[i % 2], mhalf, xn_eng="act")

            def do_A2(i):
                s, c = items[i]
                front2(P, xn[i % 2], ident, pT[i % 2], hT[i % 2], a_pp[s], b_pp[s], ev_eng="act")

            def do_B(i):
                s, c = items[i]
                sl = i % 2
                g = (i // GC) % 2
                cc = c % GC
                hv = hT[sl]
                qk32, sq, qkn, t1, t2, t3, t4 = qk32_[sl], sq_[sl], qkn_[sl], t1_[sl], t2_[sl], t3_[sl], t4_[sl]
                qkf = qk32.re("p h d -> p (h d)")
                for ct in range(3):
                    pv = pp[ct]
                    for kc in range(8):
                        P.mm(pv, hv[:, kc, :], W[:, kc, ct * 512:(ct + 1) * 512], start=(kc == 0), stop=(kc == 7))
                    if ct < 2:
                        P.copy(qkf[:, ct * 512:(ct + 1) * 512], pv, eng="act")
                    else:
                        P.copy(qkf[:, 1024:1280], pv[:, 0:256], eng="act")
                        P.copy(vb[sl], pv[:, 256:512], eng="act")
                P.dma(C.VA[s, c], vb[sl])

            def do_C(i):
                s, c = items[i]
                sl = i % 2
                qk32, sq, qkn, t1, t2, t3, t4 = qk32_[sl], sq_[sl], qkn_[sl], t1_[sl], t2_[sl], t3_[sl], t4_[sl]
                P.act(sq, qk32, AF.Square)
                sv = s10[sl]
                P.reduce(sv[:, 0:10], sq, ALU.add)
                P.ts(sv[:, 10:20], sv[:, 0:10], 1.0 / ATT_HD, EPS, ALU.mult, ALU.add)
                P.tt(sv[:, 20:30], sv[:, 10:20], V(mhalf.ap.broadcast_to([128, 10]), mhalf.key), ALU.pow, eng="pool")
                P.tt(qkn, qk32, V(sv.ap[:, 20:30].unsqueeze(2).broadcast_to([128, 10, 128]), sv.key), ALU.mult, eng="dve")
                P.tt(qkn, qkn, gains, ALU.mult, eng="dve")
                v5 = qkn.re("p h (f a d) -> p h f a d", f=2, a=2)
                o5 = qkr[sl].re("p (h f a d) -> p h f a d", h=10, f=2, a=2)
                a_, b_ = v5[:, :, :, 0, :], v5[:, :, :, 1, :]
                csv = cs[i % 4]
                cos = V(csv.ap[:, 0].unsqueeze(1).broadcast_to([128, 10, 2, 32]), csv.key)
                sin = V(csv.ap[:, 1].unsqueeze(1).broadcast_to([128, 10, 2, 32]), csv.key)
                P.tt(t1, a_, cos, ALU.mult, eng="dve")
                P.tt(t2, b_, sin, ALU.mult, eng="pool")
                P.tt(t3, a_, sin, ALU.mult, eng="pool")
                P.tt(t4, b_, cos, ALU.mult, eng="dve")
                P.tt(o5[:, :, :, 0, :], t1, t2, ALU.subtract, eng="dve")
                P.tt(o5[:, :, :, 1, :], t3, t4, ALU.add, eng="pool")

            def do_D(i):
                s, c = items[i]
                sl = i % 2
                g = (i // GC) % 2
                cc = c % GC
                for j in range(10):
                    P.tr(pq[:, j, :], qkr[sl][:, j * 128:(j + 1) * 128], ident)
                P.copy(acc[g][:, :, cc * 128:(cc + 1) * 128], pq[:, 0:10, :], eng="act")
                if cc == GC - 1:
                    c0 = c - (GC - 1)
                    P.dma(C.QTA[s, :, :, c0 * 128:(c0 + GC) * 128], acc[g][:, 0:8, :])
                    P.dma(C.KTA[s, :, :, c0 * 128:(c0 + GC) * 128], acc[g][:, 8:10, :])

            n = len(items)
            do_load(0)
            if n > 1:
                do_load(1)
            do_A1(0)
            do_A2(0)
            for i in range(n + 2):
                if i + 2 < n:
                    do_load(i + 2)
                if i + 1 < n:
                    do_A1(i + 1)
                if 0 <= i - 1 < n:
                    do_C(i - 1)
                if 0 <= i - 2 < n:
                    do_D(i - 2)
                if i + 1 < n:
                    do_A2(i + 1)
                if i < n:
                    do_B(i)
            P.emit()


def phase_l1b(C, Xin, Xout):
    import contextlib
    nc = C.nc
    NS, NT, N = C.NSEQ, C.NT, C.N
    QB = 512
    NQB = N // QB
    NP = NT // 2
    with contextlib.ExitStack() as st0:
        A0 = Alloc(nc, st0)
        W = A0.sb([128, 8, D], BF16, "Wao")
        load_weight(C, W, C.att_w_out, D, 8)
        with contextlib.ExitStack() as st:
            A = Alloc(nc, st)
            P = Phase(C.K, "l1b")
            _a, _b, G = load_mod(P, A, C, 1, 1, want_pp=False)
            gq = A.sb([128, 128], F32, "gq")
            gk = A.sb([128, 128], F32, "gk")
            mm_ = A.sb([128, 4], F32, "mm")
            P.dma(gq, C.att_q_gain.partition_broadcast(128))
            P.dma(gk, C.att_k_gain.partition_broadcast(128))
            P.reduce(mm_[:, 0:1], gq, ALU.max, absval=True)
            P.reduce(mm_[:, 1:2], gk, ALU.max, absval=True)
            P.tt(mm_[:, 2:3], mm_[:, 0:1], mm_[:, 1:2], ALU.mult)
            P.ts(mm_[:, 3:4], mm_[:, 2:3], -math.sqrt(ATT_HD), None, ALU.mult)
            negm = mm_[:, 3:4]
            ones32 = A.sb([128, 128], F32, "ones32")
            P.memset(ones32, 1.0)
            onesb = A.sb([128, 128], BF16, "onesb")
            P.memset(onesb, 1.0)
            pe_us = [u for u in range(NP) if u % 4 == 3] if NP >= 8 else []
            kt = [A.sb([128, 2, N], BF16, "kt") for _ in range(2)]
            va = [A.sb([128, NT, 256], BF16, "va") for _ in range(2)]
            qt = [A.sb([128, 8, QB], BF16, "qt") for _ in range(2)]
            PT = [A.sb([128, 2, QB], BF16, "PT") for _ in range(3)]
            acc = [A.sb([128, 2, QB], F32, "acc") for _ in range(2)]
            accp = [A.sb([128, 2, QB], F32, "accp") for _ in range(2)]
            POOL_SHARE = False
            pool_us = [u for u in range(NP) if u % 4 == 3] if (NP >= 8 and POOL_SHARE) else []
            OT = [A.sb([128, 8, QB], BF16, "OT") for _ in range(2)]
            rinv = [A.sb([128, QB], F32, "rinv") for _ in range(2)]
            xt = [A.sb([128, D], F32, "xt") for _ in range(8)]
            tmp = [A.sb([128, 512], F32, "tmp") for _ in range(2)]
            ps_s = [A.ps([128, 2, 512], F32, "ps_s") for _ in range(2)]
            po = [A.ps([128, 512], F32, "po") for _ in range(2)]
            prs = A.ps([128, 512], F32, "prs")
            pw = A.ps([128, 512], F32, "pw")
            scale = float(ATT_HD) ** -0.5
            blocks = [(s, qb) for s in range(NS) for qb in range(NQB)]
            units = [(bi, h, u) for bi in range(len(blocks)) for h in range(ATT_QH) for u in range(NP)]

            def load_seq(s):
                P.dma(kt[s % 2], C.KTA[s])
                P.dma(va[s % 2], C.VA[s].rearrange("c p d -> p c d"))

            def load_block(bi):
                s, qb = blocks[bi]
                sl = bi % 2
                P.dma(qt[sl], C.QTA[s, :, :, qb * QB:(qb + 1) * QB])
                for j in range(4):
                    c = qb * 4 + j
                    P.dma(xt[sl * 4 + j], Xin[s, c * 128:(c + 1) * 128, :])

            def qk(i):
                bi, h, u = units[i]
                s, qb = blocks[bi]
                for t in range(2):
                    kc = 2 * u + t
                    P.mm(ps_s[i % 2][:, t, :], kt[s % 2][:, h // 4, kc * 128:(kc + 1) * 128], qt[bi % 2][:, h, :])

            deferred = []
            hfq = []

            def head_final(bi, h):
                def f():
                    a = acc[h % 2]
                    if pool_us:
                        P.tt(a, a, accp[h % 2], ALU.add)
                    for t in range(2):
                        P.mm(prs, ones32, a[:, t, :], start=(t == 0 and not pe_us), stop=(t == 1))
                    P.recip(rinv[h % 2], prs)
                    P.tt(OT[bi % 2][:, h, :], po[h % 2], rinv[h % 2], ALU.mult)
                return f

            def out_group(bi, j, half):
                def f():
                    s, qb = blocks[bi]
                    c = qb * 4 + j
                    xv = xt[(bi % 2) * 4 + j]
                    for h in range(8):
                        P.mm(pw, OT[bi % 2][:, h, j * 128:(j + 1) * 128], W[:, h, half * 512:(half + 1) * 512],
                             start=(h == 0), stop=(h == 7))
                    P.tt(tmp[half], pw, G[s][:, half * 512:(half + 1) * 512], ALU.mult)
                    P.tt(xv[:, half * 512:(half + 1) * 512], tmp[half], xv[:, half * 512:(half + 1) * 512],
                         ALU.add, eng="pool")
                    if half == 1:
                        P.dma(Xout[s, c * 128:(c + 1) * 128, :], xv)
                return f

            def pv_(i):
                bi, h, u = units[i]
                s, qb = blocks[bi]
                pt = PT[i % 3]
                import os
                if os.environ.get("DBG_EXP1"):
                    for t in range(2):
                        P.act(pt[:, t, :], ps_s[i % 2][:, t, :], AF.Exp, bias=negm, scale=scale)
                else:
                    P.act(pt, ps_s[i % 2], AF.Exp, bias=negm, scale=scale)
                for t in range(2):
                    kc = 2 * u + t
                    P.mm(po[h % 2], va[s % 2][:, kc, (h // 4) * 128:(h // 4 + 1) * 128], pt[:, t, :],
                         start=(kc == 0), stop=(kc == NT - 1))
                if u in pe_us:
                    for t in range(2):
                        P.mm(prs, onesb, pt[:, t, :], start=(u == pe_us[0] and t == 0), stop=False)
                elif u in pool_us:
                    if u == pool_us[0]:
                        P.copy(accp[h % 2], pt, eng="pool")
                    else:
                        P.tt(accp[h % 2], accp[h % 2], pt, ALU.add, eng="pool")
                elif u == 0:
                    P.copy(acc[h % 2], pt, eng="dve")
                else:
                    P.tt(acc[h % 2], acc[h % 2], pt, ALU.add)
                if u == NP - 1:
                    hfq.append(head_final(bi, h))
                    if h == ATT_QH - 1:
                        for j in range(4):
                            for half in range(2):
                                deferred.append(out_group(bi, j, half))

            nu = len(units)
            load_seq(0)
            load_block(0)
            loaded = 0
            qk(0)
            if nu > 1:
                qk(1)
            for i in range(nu):
                bi, h, u = units[i]
                if h == 0 and u == 0:
                    s, qb = blocks[bi]
                    if qb == 0 and s + 1 < NS:
                        load_seq(s + 1)
                pv_(i)
                if i + 2 < nu:
                    nb = units[i + 2][0]
                    if nb > loaded:
                        while hfq:
                            hfq.pop(0)()
                        while deferred:
                            deferred.pop(0)()
                        load_block(nb)
                        loaded = nb
                    qk(i + 2)
                if hfq and u == 0:
                    hfq.pop(0)()
                elif deferred and (u % 2 == 1) and not hfq:
                    deferred.pop(0)()
                if not deferred and loaded < bi + 1 and bi + 1 < len(blocks) and not (h == ATT_QH - 1 and u == NP - 1):
                    load_block(bi + 1)
                    loaded = bi + 1
            while hfq:
                hfq.pop(0)()
            while deferred:
                deferred.pop(0)()
            P.emit()


def rope_table(n_tokens, nf):
    t = np.arange(n_tokens)
    row = (t // GRID_W).astype(np.float32)
    col = (t % GRID_W).astype(np.float32)
    inv = (np.float32(ROPE_THETA) ** (-(np.arange(nf, dtype=np.float32)) / np.float32(nf))).astype(np.float32)
    ar = (row[:, None] * inv[None, :]).astype(np.float32)
    ac = (col[:, None] * inv[None, :]).astype(np.float32)
    tab = np.stack([np.stack([np.cos(ar), np.cos(ac)], 1), np.stack([np.sin(ar), np.sin(ac)], 1)], 1)
    return np.ascontiguousarray(tab.reshape(n_tokens // 128, 128, 2, 2, nf).astype(np.float32))


def build(NSEQ, N, phases=None):
    nc = bass.Bass("TRN2", target_bir_lowering=False)
    C = Ctx()
    C.nc = nc
    C.NSEQ = NSEQ
    C.N = N
    C.NT = N // 128
    NT = C.NT

    def inp(name, shape, dt=F32):
        return nc.dram_tensor(name, list(shape), dt, kind="ExternalInput").ap()

    C.x = inp("x", [NSEQ, N, D])
    C.cT = inp("cT", [128, 8, NSEQ])
    C.mod_w = inp("mod_w", [2, D, 6 * D])
    C.mod_b = inp("mod_b", [2, 6 * D])
    C.norm1_g = inp("norm1_g", [2, D])
    C.norm2_g = inp("norm2_g", [2, D])
    C.ret_w_in = inp("ret_w_in", [D, RET_IN])
    C.ret_decay = inp("ret_decay", [8])
    C.ret_w_out = inp("ret_w_out", [2 * D, D])
    C.att_w_in = inp("att_w_in", [D, ATT_IN])
    C.att_q_gain = inp("att_q_gain", [ATT_HD])
    C.att_k_gain = inp("att_k_gain", [ATT_HD])
    C.att_w_out = inp("att_w_out", [D, D])
    C.mlp_w1 = inp("mlp_w1", [2, D, DFF])
    C.mlp_w2 = inp("mlp_w2", [2, DFF, D])
    C.final_g = inp("final_g", [D])
    C.ident = inp("ident", [128, 128])
    C.rope_r = inp("rope_r", [NT, 128, 2, 2, 64])
    C.rope_a = inp("rope_a", [NT, 128, 2, 2, 32])
    C.y = nc.dram_tensor("y", [NSEQ, N, D], F32, kind="ExternalOutput").ap()

    def scr(name, shape, dt):
        return nc.dram_tensor(name, list(shape), dt).ap()

    C.MOD = scr("MOD", [2, NSEQ, 6 * D], F32)
    C.XA = scr("XA", [NSEQ, N, D], F32)
    C.XB = scr("XB", [NSEQ, N, D], F32)
    C.QT = scr("QT", [NSEQ, NT, 3, 128, 4, 2, 128], BF16)
    C.KT = scr("KT", [NSEQ, NT, 128, 4, 2, 128], BF16)
    C.KF = scr("KF", [NSEQ, NT, 128, 4, 256], BF16)
    C.KB = scr("KB", [NSEQ, NT, 128, 4, 256], BF16)
    C.VS = scr("VS", [NSEQ, NT, 128, 4, 512], BF16)
    C.SG = scr("SG", [NSEQ, NT, 128, 4, 512], BF16)
    C.YT = scr("YT", [NSEQ, 128, 16, N], BF16)
    C.QTA = scr("QTA", [NSEQ, 128, 8, N], BF16)
    C.KTA = scr("KTA", [NSEQ, 128, 2, N], BF16)
    C.VA = scr("VA", [NSEQ, NT, 128, 256], BF16)
    C.K = Kern(nc)
    if phases is None:
        phases = ["mod", "l0a", "l0b", "l0c", "mlp0", "l1a", "l1b", "mlp1"]
    C.phases = phases
    for ph in phases:
        if ph == "mod":
            phase_mod(C)
        elif ph == "l0a":
            phase_l0a(C)
        elif ph == "l0b":
            phase_l0b(C)
        elif ph == "l0c":
            phase_l0c(C, C.x, C.XA)
        elif ph == "mlp0":
            phase_mlp(C, 0, C.XA, C.XB, False)
        elif ph == "l1a":
            phase_l1a(C, C.XB)
        elif ph == "l1b":
            phase_l1b(C, C.XB, C.XA)
        elif ph == "mlp1":
            phase_mlp(C, 1, C.XA, C.y, True)
        elif ph == "l1a_x":
            phase_l1a(C, C.x)
        elif ph == "l1b_xy":
            phase_l1b(C, C.x, C.y)
        elif ph == "l0c_y":
            phase_l0c(C, C.x, C.y)
        elif ph == "mlp_only":
            phase_mlp(C, 0, C.x, C.y, True)
        else:
            raise ValueError(ph)
    return nc, C


def make_in_maps(inputs, NSEQ_P=2, NSEQ_S=1, ncores=8):
    f = lambda a: np.ascontiguousarray(np.asarray(a, dtype=np.float32))
    xp, xs_ = f(inputs["x_prompt"]), f(inputs["x_sample"])
    cp, cs = f(inputs["c_prompt"]), f(inputs["c_sample"])
    N = xp.shape[1]
    shared = {
        "mod_w": f(inputs["mod_w"]), "mod_b": f(inputs["mod_b"]),
        "norm1_g": f(inputs["norm1_g"]), "norm2_g": f(inputs["norm2_g"]),
        "ret_w_in": f(inputs["ret_w_in"])[0], "ret_decay": f(inputs["ret_decay"])[0].reshape(8),
        "ret_w_out": f(inputs["ret_w_out"])[0], "att_w_in": f(inputs["att_w_in"])[0],
        "att_q_gain": f(inputs["att_q_gain"])[0], "att_k_gain": f(inputs["att_k_gain"])[0],
        "att_w_out": f(inputs["att_w_out"])[0], "mlp_w1": f(inputs["mlp_w1"]), "mlp_w2": f(inputs["mlp_w2"]),
        "final_g": f(inputs["final_g"]),
        "ident": np.eye(128, dtype=np.float32),
        "rope_r": rope_table(N, 64), "rope_a": rope_table(N, 32),
    }
    maps = []
    for i in range(ncores):
        x = np.concatenate([xp[NSEQ_P * i:NSEQ_P * (i + 1)], xs_[NSEQ_S * i:NSEQ_S * (i + 1)]], 0)
        c = np.concatenate([cp[NSEQ_P * i:NSEQ_P * (i + 1)], cs[NSEQ_S * i:NSEQ_S * (i + 1)]], 0)
        cT = np.ascontiguousarray(c.reshape(c.shape[0], 8, 128).transpose(2, 1, 0))
        m = dict(shared)
        m["x"] = np.ascontiguousarray(x)
        m["cT"] = cT
        maps.append(m)
    return maps


def kernel(**inputs):
    maps = make_in_maps(inputs)
    N = maps[0]["x"].shape[1]
    nc, C = build(3, N)
    res = run_bass_kernel_spmd(nc, maps, core_ids=list(range(8)))
    ys = [np.asarray(r["y"]) for r in res.results]
    y_prompt = np.concatenate([y[0:2] for y in ys], 0).astype(np.float32)
    y_sample = np.concatenate([y[2:3] for y in ys], 0).astype(np.float32)
    return (y_prompt, y_sample)
```

```python
import math
import numpy as np
import concourse.bass as bass
import concourse.mybir as mybir
from concourse.bass_utils import run_bass_kernel_spmd
from concourse.alu_op_type import AluOpType as ALU

AF = mybir.ActivationFunctionType
F32 = mybir.dt.float32
BF16 = mybir.dt.bfloat16
AX = mybir.AxisListType

D = 1024
DFF = 4096
EPS = 1e-6
RET_H = 4
RET_DK = 256
RET_DV = 512
RET_IN = 6144
ATT_HD = 128
ATT_QH = 8
ATT_KVH = 2
ATT_IN = 1536
GRID_W = 64
ROPE_THETA = 10000.0


class V:
    __slots__ = ("ap", "key")

    def __init__(self, ap, key):
        self.ap = ap
        self.key = key

    def __getitem__(self, idx):
        return V(self.ap[idx], self.key)

    def sub(self, s):
        return V(self.ap, (self.key, s))

    def re(self, pat, **kw):
        return V(self.ap.rearrange(pat, **kw), self.key)

    def bc(self, shape):
        return V(self.ap.broadcast_to(shape), self.key)


def _keys(*xs):
    out = []
    for x in xs:
        if isinstance(x, V) and x.key is not None:
            out.append(x.key)
    return out


def _a(x):
    return x.ap if isinstance(x, V) else x


class Op:
    __slots__ = ("eng", "fn", "reads", "writes", "dma", "signal", "sem", "count", "deps", "pre")

    def __init__(self, eng, fn, reads, writes, dma):
        self.eng = eng
        self.fn = fn
        self.reads = reads
        self.writes = writes
        self.dma = dma
        self.signal = dma
        self.sem = None
        self.count = 0
        self.deps = ()
        self.pre = None


ENGS = ("pe", "act", "dve", "pool", "sp")


class Kern:
    def __init__(self, nc, ndma=32):
        self.nc = nc
        self.ndma = ndma
        self.dma_sems = [nc.alloc_semaphore(name=f"dq{i}") for i in range(ndma)]
        self.dma_n = 0
        self.dma_cnt = [0] * ndma
        self.phase_i = 0
        self.waited = {e: {} for e in ENGS}
        self.n_inst = 0

    def eng(self, name):
        nc = self.nc
        return {"pe": nc.tensor, "act": nc.scalar, "dve": nc.vector, "pool": nc.gpsimd, "sp": nc.sync}[name]


class Phase:
    def __init__(self, K, name):
        self.K = K
        self.name = name
        self.ops = []
        self._uid = 0

    def uid(self):
        self._uid += 1
        return self._uid

    def add(self, eng, fn, reads=(), writes=(), dma=False):
        self.ops.append(Op(eng, fn, tuple(reads), tuple(writes), dma))

    def mm(self, out, lhsT, rhs, start=True, stop=True):
        self.add("pe", lambda e: e.matmul(out.ap, lhsT.ap, rhs.ap, start=start, stop=stop),
                 _keys(lhsT, rhs), _keys(out))

    def tr(self, out, in_, ident):
        self.add("pe", lambda e: e.transpose(out.ap, in_.ap, ident.ap), _keys(in_, ident), _keys(out))

    def act(self, out, in_, func, bias=None, scale=None, accum=None, eng="act"):
        kw = {}
        if bias is not None:
            kw["bias"] = _a(bias)
        if scale is not None:
            kw["scale"] = _a(scale)
        if accum is not None:
            kw["accum_out"] = accum.ap
        self.add(eng, lambda e: e.activation(out.ap, in_.ap, func, **kw),
                 _keys(in_, bias, scale), _keys(out, accum))

    def ts(self, out, in0, s1, s2, op0, op1=None, eng="dve", accum=None):
        kw = {}
        if op1 is not None:
            kw["op1"] = op1
        if accum is not None:
            kw["accum_out"] = accum.ap
        self.add(eng, lambda e: e.tensor_scalar(out.ap, in0.ap, _a(s1), _a(s2), op0, **kw),
                 _keys(in0, s1, s2), _keys(out, accum))

    def tt(self, out, in0, in1, op, eng="dve"):
        self.add(eng, lambda e: e.tensor_tensor(out.ap, in0.ap, in1.ap, op), _keys(in0, in1), _keys(out))

    def stt(self, out, in0, scalar, in1, op0, op1, eng="dve"):
        self.add(eng, lambda e: e.scalar_tensor_tensor(out.ap, in0.ap, _a(scalar), in1.ap, op0, op1),
                 _keys(in0, scalar, in1), _keys(out))

    def copy(self, out, in_, eng="dve"):
        if eng == "act":
            self.add(eng, lambda e: e.copy(out.ap, in_.ap), _keys(in_), _keys(out))
        else:
            self.add(eng, lambda e: e.tensor_copy(out.ap, in_.ap), _keys(in_), _keys(out))

    def memset(self, out, val, eng="pool"):
        self.add(eng, lambda e: e.memset(out.ap, val), (), _keys(out))

    def recip(self, out, in_, eng="dve"):
        self.add(eng, lambda e: e.reciprocal(out.ap, in_.ap), _keys(in_), _keys(out))

    def reduce(self, out, in_, op, axis=AX.X, eng="dve", absval=None):
        self.add(eng, lambda e: e.tensor_reduce(out.ap, in_.ap, axis, op, apply_absolute_value=absval),
                 _keys(in_), _keys(out))

    def bn_stats(self, out, in_):
        self.add("dve", lambda e: e.bn_stats(out.ap, in_.ap), _keys(in_), _keys(out))

    def bn_aggr(self, out, in_):
        self.add("dve", lambda e: e.bn_aggr(out.ap, in_.ap), _keys(in_), _keys(out))

    def iota(self, out, pattern, base, cm):
        self.add("pool", lambda e: e.iota(out.ap, pattern, base=base, channel_multiplier=cm,
                                          allow_small_or_imprecise_dtypes=True), (), _keys(out))

    def dma(self, out, in_, slow=False):
        kw = {"allow_slow_non_contiguous": True} if slow else {}
        self.add("sp", lambda e: e.dma_start(_a(out), _a(in_), **kw), _keys(in_), _keys(out), dma=True)

    def emit(self):
        K = self.K
        nc = K.nc
        ops = self.ops
        last_w = {}
        rd = {}
        for i, op in enumerate(ops):
            deps = {}

            def add(j, kind, op=op, deps=deps, i=i):
                if j is None or j == i:
                    return
                Pp = ops[j]
                if Pp.eng == op.eng and not Pp.dma and not op.dma:
                    if op.eng == "pe":
                        return
                deps[j] = True

            for k in op.reads:
                add(last_w.get(k), "RAW")
            for k in op.writes:
                add(last_w.get(k), "WAW")
                r = rd.get(k)
                if r:
                    for j in r.values():
                        add(j, "WAR")
            op.deps = tuple(deps)
            for j in deps:
                ops[j].signal = True
            for k in op.reads:
                r = rd.get(k)
                if r is None:
                    r = rd[k] = {}
                r[("d", i) if op.dma else op.eng] = i
            for k in op.writes:
                last_w[k] = i
                rd[k] = {}
        if self.name == "w":
            if not hasattr(K, "w_sems"):
                K.w_sems = {e: nc.alloc_semaphore(name=f"phw_{e}") for e in ("pe", "act", "dve", "pool")}
                K.w_cnt = {e: 0 for e in K.w_sems}
            esem, ecnt = K.w_sems, K.w_cnt
        else:
            esem = {e: nc.alloc_semaphore(name=f"ph{K.phase_i}_{e}") for e in ("pe", "act", "dve", "pool")}
            K.phase_i += 1
            ecnt = {e: 0 for e in esem}
        for op in ops:
            if op.dma:
                n = K.dma_n
                K.dma_n += 1
                s = n % K.ndma
                K.dma_cnt[s] += 16
                op.sem = K.dma_sems[s]
                op.count = K.dma_cnt[s]
                if op.count > 16:
                    op.pre = (op.sem, op.count - 16)
            elif op.signal:
                ecnt[op.eng] += 1
                op.sem = esem[op.eng]
                op.count = ecnt[op.eng]
        by_eng = {e: [] for e in ENGS}
        for op in ops:
            by_eng[op.eng].append(op)
        K.n_inst += len(ops)

        def run(e, name):
            waited = K.waited[name]
            for op in by_eng[name]:
                need = {}
                if op.pre is not None:
                    need[id(op.pre[0])] = op.pre
                for j in op.deps:
                    Pp = ops[j]
                    cur = need.get(id(Pp.sem))
                    if cur is None or cur[1] < Pp.count:
                        need[id(Pp.sem)] = (Pp.sem, Pp.count)
                for sid, (sem, val) in need.items():
                    if waited.get(sid, 0) < val:
                        e.wait_ge(sem, val)
                        waited[sid] = val
                inst = op.fn(e)
                if op.dma:
                    inst.then_inc(op.sem, 16)
                elif op.signal:
                    inst.then_inc(op.sem, 1)
            if name == "sp":
                for s in range(K.ndma):
                    if K.dma_cnt[s] > waited.get(id(K.dma_sems[s]), 0):
                        e.wait_ge(K.dma_sems[s], K.dma_cnt[s])
                        waited[id(K.dma_sems[s])] = K.dma_cnt[s]

        with nc.Block() as block:
            @block.tensor
            def _(e):
                run(e, "pe")

            @block.scalar
            def _(e):
                run(e, "act")

            @block.vector
            def _(e):
                run(e, "dve")

            @block.gpsimd
            def _(e):
                run(e, "pool")

            @block.sync
            def _(e):
                run(e, "sp")
        self.ops = []


class Alloc:
    _n = [0]

    def __init__(self, nc, stack):
        self.nc = nc
        self.stack = stack

    def sb(self, shape, dt, name=None):
        Alloc._n[0] += 1
        nm = f"{name or 't'}_{Alloc._n[0]}"
        t = self.stack.enter_context(self.nc.sbuf_tensor(nm, list(shape), dt))
        return V(t[:] if hasattr(t, "__getitem__") else t.ap(), nm)

    def ps(self, shape, dt, name=None):
        Alloc._n[0] += 1
        nm = f"{name or 'p'}_{Alloc._n[0]}"
        t = self.stack.enter_context(self.nc.psum_tensor(nm, list(shape), dt))
        return V(t[:] if hasattr(t, "__getitem__") else t.ap(), nm)


class Ctx:
    pass


def load_weight(C, dst, src, ncols, kcs):
    import contextlib
    nc = C.nc
    with contextlib.ExitStack() as st:
        A = Alloc(nc, st)
        CH = min(ncols, 2048)
        stage = [A.sb([128, CH], F32, "wst") for _ in range(3)]
        P = Phase(C.K, "w")
        i = 0
        for kc in range(kcs):
            for c0 in range(0, ncols, CH):
                sv = stage[i % 3]
                P.dma(sv, src[kc * 128:(kc + 1) * 128, c0:c0 + CH])
                P.copy(dst[:, kc, c0:c0 + CH], sv, eng=("dve" if i % 2 == 0 else "pool"))
                i += 1
        P.emit()


def load_consts(P, A, C):
    ident32 = A.sb([128, 128], F32, "id32")
    ident = A.sb([128, 128], BF16, "id")
    mhalf = A.sb([128, 1], F32, "mh")
    P.dma(ident32, C.ident)
    P.copy(ident, ident32)
    P.memset(mhalf, -0.5)
    return ident, mhalf


def load_mod(P, A, C, layer, which, want_pp=True, want_gate=True):
    off = 0 if which == 1 else 3 * D
    ng = C.norm1_g if which == 1 else C.norm2_g
    gpp = A.sb([128, 8], F32, "gpp")
    P.dma(gpp, ng[layer].rearrange("(kc p) -> p kc", p=128), slow=True)
    a_pp, b_pp, G = [], [], []
    for s in range(C.NSEQ):
        row = C.MOD[layer, s]
        if want_pp:
            sc = A.sb([128, 8], F32, "scpp")
            a = A.sb([128, 8], F32, "app")
            b = A.sb([128, 8], F32, "bpp")
            P.dma(b, row[off:off + D].rearrange("(kc p) -> p kc", p=128), slow=True)
            P.dma(sc, row[off + D:off + 2 * D].rearrange("(kc p) -> p kc", p=128), slow=True)
            P.stt(a, sc, 1.0, gpp, ALU.add, ALU.mult)
            a_pp.append(a)
            b_pp.append(b)
        if want_gate:
            g = A.sb([128, D], F32, "gate")
            P.dma(g, row[off + 2 * D:off + 3 * D].partition_broadcast(128))
            G.append(g)
    return a_pp, b_pp, G


def front1(P, xt, ss, junk, xn, mhalf, xn_eng="dve"):
    P.act(junk, xt, AF.Square, accum=ss[:, 0:1])
    P.ts(ss[:, 1:2], ss[:, 0:1], 1.0 / D, EPS, ALU.mult, ALU.add)
    P.tt(ss[:, 2:3], ss[:, 1:2], mhalf, ALU.pow, eng="pool")
    if xn_eng == "act":
        P.act(xn, xt, AF.Identity, scale=ss[:, 2:3])
    else:
        P.ts(xn, xt, ss[:, 2:3], None, ALU.mult)


def front2(P, xn, ident, pT, hT, a_pp, b_pp, ev_eng="dve"):
    for kc in range(8):
        P.tr(pT[:, kc, :], xn[:, kc * 128:(kc + 1) * 128], ident)
    for kc in range(8):
        if ev_eng == "act":
            P.act(hT[:, kc, :], pT[:, kc, :], AF.Identity, bias=b_pp[:, kc:kc + 1], scale=a_pp[:, kc:kc + 1])
        else:
            P.ts(hT[:, kc, :], pT[:, kc, :], a_pp[:, kc:kc + 1], b_pp[:, kc:kc + 1], ALU.mult, ALU.add)


def front(P, xt, ss, junk, xn, ident, mhalf, pT, hT, a_pp, b_pp, ev_eng="dve"):
    front1(P, xt, ss, junk, xn, mhalf)
    front2(P, xn, ident, pT, hT, a_pp, b_pp, ev_eng)


def phase_mod(C):
    import contextlib
    nc = C.nc
    NS = C.NSEQ
    with contextlib.ExitStack() as st:
        A = Alloc(nc, st)
        P = Phase(C.K, "mod")
        ct32 = A.sb([128, 8, NS], F32)
        sct = A.sb([128, 8, NS], BF16)
        P.dma(ct32, C.cT)
        P.act(sct, ct32, AF.Silu)
        stage = [A.sb([128, 8, 512], F32, "mst") for _ in range(2)]
        wb = [A.sb([128, 8, 512], BF16, "mwb") for _ in range(2)]
        bias = A.sb([NS, 2, 6 * D], F32)
        res = A.sb([NS, 2, 6 * D], F32)
        pp = [A.ps([128, 512], F32) for _ in range(2)]
        for l in range(2):
            for s in range(NS):
                P.dma(bias[s:s + 1, l, :], C.mod_b[l:l + 1, :])
        i = 0
        for l in range(2):
            for nt in range(12):
                sv, wv, pv = stage[i % 2], wb[i % 2], pp[i % 2]
                P.dma(sv, C.mod_w[l, :, nt * 512:(nt + 1) * 512].rearrange("(kc p) n -> p kc n", p=128))
                P.copy(wv, sv, eng=("dve" if i % 2 == 0 else "pool"))
                for kc in range(8):
                    P.mm(pv[0:NS, :], sct[:, kc, :], wv[:, kc, :], start=(kc == 0), stop=(kc == 7))
                P.tt(res[:, l, nt * 512:(nt + 1) * 512], pv[0:NS, :], bias[:, l, nt * 512:(nt + 1) * 512], ALU.add)
                i += 1
        for l in range(2):
            P.dma(C.MOD[l], res[:, l, :])
        P.emit()


def phase_mlp(C, layer, Xin, Xout, final):
    import contextlib
    nc = C.nc
    NS, NT = C.NSEQ, C.NT
    TS = 2
    with contextlib.ExitStack() as st0:
        A0 = Alloc(nc, st0)
        W1 = A0.sb([128, 8, DFF], BF16, "W1")
        W2 = A0.sb([128, 32, D], BF16, "W2")
        load_weight(C, W1, C.mlp_w1[layer], DFF, 8)
        load_weight(C, W2, C.mlp_w2[layer], D, 32)
        with contextlib.ExitStack() as st:
            A = Alloc(nc, st)
            P = Phase(C.K, "mlp")
            ident, mhalf = load_consts(P, A, C)
            a_pp, b_pp, G = load_mod(P, A, C, layer, 2)
            if final:
                FG = A.sb([128, D], F32, "FG")
                P.dma(FG, C.final_g.partition_broadcast(128))
            xt = [A.sb([128, D], F32, "xt") for _ in range(3 * TS)]
            xn = [A.sb([128, D], BF16, "xn") for _ in range(2)]
            junk = A.sb([128, D], BF16, "junk")
            ss = [A.sb([128, 4], F32, "ss") for _ in range(4)]
            hT = [A.sb([128, 8, TS * 128], BF16, "hT") for _ in range(2)]
            uT = A.sb([128, 32, TS * 128], BF16, "uT")
            r32 = [A.sb([128, TS * 128], F32, "r32") for _ in range(2)]
            tmp = [A.sb([128, 512], F32, "tmp") for _ in range(2)]
            pT = [A.ps([128, 8, 128], BF16, "pT") for _ in range(2)]
            pu = [A.ps([128, 512], F32, "pu") for _ in range(3)]
            po = [A.ps([128, 512], F32, "po") for _ in range(3)]
            items = [(s, sc) for s in range(NS) for sc in range(NT // TS)]
            cnt = [0]

            def xs_of(i):
                return [xt[(i % 3) * TS + j] for j in range(TS)]

            def do_load(i):
                s, sc = items[i]
                for j in range(TS):
                    c = sc * TS + j
                    P.dma(xs_of(i)[j], Xin[s, c * 128:(c + 1) * 128, :])

            fk = {}

            def do_front1(i):
                ks = []
                for j in range(TS):
                    k = cnt[0]
                    cnt[0] += 1
                    ks.append(k)
                    front1(P, xs_of(i)[j], ss[k % 4], junk, xn[k % 2], mhalf)
                fk[i] = ks

            def do_front2(i):
                s, sc = items[i]
                for j in range(TS):
                    k = fk[i][j]
                    front2(P, xn[k % 2], ident, pT[k % 2], hT[i % 2][:, :, j * 128:(j + 1) * 128],
                           a_pp[s], b_pp[s], ev_eng=("act" if k % 2 else "dve"))

            def do_w1(i):
                hv = hT[i % 2]
                for fc in range(32):
                    pv = pu[fc % 3]
                    for kc in range(8):
                        P.mm(pv[:, 0:TS * 128], W1[:, kc, fc * 128:(fc + 1) * 128], hv[:, kc, :],
                             start=(kc == 0), stop=(kc == 7))
                    rv = r32[fc % 2]
                    P.act(rv, pv[:, 0:TS * 128], AF.Relu)
                    P.tt(uT[:, fc, :], rv, rv, ALU.mult, eng=("dve" if fc % 2 == 0 else "pool"))

            def do_w2(i):
                s, sc = items[i]
                xs = xs_of(i)
                for j in range(TS):
                    for half in range(2):
                        pv = po[(j * 2 + half) % 3]
                        for fc in range(32):
                            P.mm(pv, uT[:, fc, j * 128:(j + 1) * 128], W2[:, fc, half * 512:(half + 1) * 512],
                                 start=(fc == 0), stop=(fc == 31))
                        tv = tmp[half]
                        P.tt(tv, pv, G[s][:, half * 512:(half + 1) * 512], ALU.mult)
                        P.tt(xs[j][:, half * 512:(half + 1) * 512], tv, xs[j][:, half * 512:(half + 1) * 512],
                             ALU.add, eng="pool")
                    c = sc * TS + j
                    if final:
                        k = cnt[0]
                        cnt[0] += 1
                        sv = ss[k % 4]
                        P.act(junk, xs[j], AF.Square, accum=sv[:, 0:1])
                        P.ts(sv[:, 1:2], sv[:, 0:1], 1.0 / D, EPS, ALU.mult, ALU.add)
                        P.tt(sv[:, 2:3], sv[:, 1:2], mhalf, ALU.pow, eng="pool")
                        P.stt(xs[j], xs[j], sv[:, 2:3], FG, ALU.mult, ALU.mult)
                    P.dma(Xout[s, c * 128:(c + 1) * 128, :], xs[j])

            n = len(items)
            do_load(0)
            if n > 1:
                do_load(1)
            do_front1(0)
            do_front2(0)
            for i in range(n):
                if i + 2 < n:
                    do_load(i + 2)
                if i + 1 < n:
                    do_front1(i + 1)
                do_w1(i)
                if i + 1 < n:
                    do_front2(i + 1)
                do_w2(i)
            P.emit()


def decay_tables(P, A, C, want):
    T = {}
    dl = A.sb([128, 8], F32, "dl")
    lg = A.sb([128, 8], F32, "lg")
    P.dma(dl, C.ret_decay.partition_broadcast(128))
    P.act(lg, dl, AF.Exp, scale=-1.0)
    P.ts(lg, lg, 1.0, None, ALU.add)
    P.act(lg, lg, AF.Ln)
    P.ts(lg, lg, -1.0, None, ALU.mult)
    T["lg"] = lg
    pidx = A.sb([128, 1], F32, "pidx")
    P.iota(pidx, [[0, 1]], 0, 1)
    jidx = A.sb([128, 128], F32, "jidx")
    P.iota(jidx, [[1, 128]], 0, 0)
    scale = float(RET_DK) ** -0.5
    if "pp" in want:
        rp = A.sb([128, 1], F32, "rp")
        P.ts(rp, pidx, -1.0, 127.0, ALU.mult, ALU.add)
        kdf = A.sb([128, 4], F32, "kdf")
        kdb = A.sb([128, 4], F32, "kdb")
        for h in range(4):
            P.act(kdf[:, h:h + 1], rp, AF.Exp, scale=lg[:, h:h + 1])
            P.act(kdb[:, h:h + 1], pidx, AF.Exp, scale=lg[:, 4 + h:5 + h])
        T["kdf"], T["kdb"] = kdf, kdb
        j1 = A.sb([128, 128], F32, "j1")
        jr = A.sb([128, 128], F32, "jr")
        P.ts(j1, jidx, 1.0, None, ALU.add)
        P.ts(jr, jidx, -1.0, 128.0, ALU.mult, ALU.add)
        qdf = A.sb([128, 8, 128], F32, "qdf")
        qdb = A.sb([128, 8, 128], F32, "qdb")
        for h in range(4):
            for dc in range(2):
                P.act(qdf[:, 2 * h + dc, :], j1, AF.Exp, scale=lg[:, h:h + 1])
                P.act(qdb[:, 2 * h + dc, :], jr, AF.Exp, scale=lg[:, 4 + h:5 + h])
        P.ts(qdf, qdf, scale, None, ALU.mult)
        P.ts(qdb, qdb, scale, None, ALU.mult)
        T["qdf"], T["qdb"] = qdf, qdb
    if "mask" in want:
        diff = A.sb([128, 128], F32, "diff")
        P.ts(diff, jidx, pidx, None, ALU.subtract)
        dpos = A.sb([128, 128], F32, "dpos")
        dneg = A.sb([128, 128], F32, "dneg")
        mge = A.sb([128, 128], F32, "mge")
        P.ts(dpos, diff, 0.0, None, ALU.max)
        P.tt(dneg, dpos, diff, ALU.subtract)
        P.ts(mge, diff, 0.0, None, ALU.is_ge)
        DT = A.sb([128, 4, 128], F32, "DT")
        ea = A.sb([128, 128], F32, "ea")
        eb = A.sb([128, 128], F32, "eb")
        for h in range(4):
            P.act(ea, dpos, AF.Exp, scale=lg[:, h:h + 1])
            P.act(eb, dneg, AF.Exp, scale=lg[:, 4 + h:5 + h])
            P.tt(ea, ea, eb, ALU.subtract)
            P.tt(ea, ea, mge, ALU.mult)
            P.tt(ea, ea, eb, ALU.add)
            P.ts(DT[:, h, :], ea, scale, None, ALU.mult)
        T["DT"] = DT
        cd = A.sb([128, 8], F32, "cdec")
        P.act(cd, lg, AF.Exp, scale=128.0)
        T["cdec"] = cd
    return T


def phase_l0a(C):
    import contextlib
    nc = C.nc
    NS, NT = C.NSEQ, C.NT
    with contextlib.ExitStack() as st0:
        A0 = Alloc(nc, st0)
        W = A0.sb([128, 8, RET_IN], BF16, "Win")
        load_weight(C, W, C.ret_w_in, RET_IN, 8)
        with contextlib.ExitStack() as st:
            A = Alloc(nc, st)
            P = Phase(C.K, "l0a")
            ident, mhalf = load_consts(P, A, C)
            a_pp, b_pp, _G = load_mod(P, A, C, 0, 1, want_gate=False)
            T = decay_tables(P, A, C, ("pp",))
            xt = [A.sb([128, D], F32, "xt") for _ in range(3)]
            cs = [A.sb([128, 2, 2, 64], F32, "cs") for _ in range(3)]
            xn = [A.sb([128, D], BF16, "xn") for _ in range(2)]
            junk = A.sb([128, D], BF16, "junk")
            ss = [A.sb([128, 4], F32, "ss") for _ in range(4)]
            hT = [A.sb([128, 8, 128], BF16, "hT") for _ in range(2)]
            qk32 = [A.sb([128, 2048], F32, "qk32")] * 2
            t1 = A.sb([128, 8, 2, 64], F32, "t1")
            t2 = A.sb([128, 8, 2, 64], F32, "t2")
            t3, t4 = t1, t2
            qkr = [A.sb([128, 2048], BF16, "qkr") for _ in range(2)]
            kf = [A.sb([128, 4, 256], BF16, "kf") for _ in range(2)]
            kb = [A.sb([128, 4, 256], BF16, "kb") for _ in range(2)]
            vo = [A.sb([128, 2048], BF16, "vo") for _ in range(2)]
            sg = [A.sb([128, 2048], BF16, "sg") for _ in range(2)]
            qo = [A.sb([128, 3, 8, 128], BF16, "qo") for _ in range(2)]
            ko = [A.sb([128, 8, 128], BF16, "ko") for _ in range(2)]
            pT = [A.ps([128, 8, 128], BF16, "pT") for _ in range(2)]
            pp = [A.ps([128, 512], F32, "pp") for _ in range(4)]
            pq = A.ps([128, 16, 128], BF16, "pq")
            items = [(s, c) for s in range(NS) for c in range(NT)]

            def do_load(i):
                s, c = items[i]
                P.dma(xt[i % 3], C.x[s, c * 128:(c + 1) * 128, :])
                P.dma(cs[i % 3], C.rope_r[c])

            def do_A1(i):
                front1(P, xt[i % 3], ss[i % 4], junk, xn[i % 2], mhalf)

            def do_A2(i):
                s, c = items[i]
                front2(P, xn[i % 2], ident, pT[i % 2], hT[i % 2], a_pp[s], b_pp[s], ev_eng="dve")

            def do_B(i, cts):
                s, c = items[i]
                sl = i % 2
                hv = hT[sl]
                for ct in cts:
                    pv = pp[ct % 4]
                    for kc in range(8):
                        P.mm(pv, hv[:, kc, :], W[:, kc, ct * 512:(ct + 1) * 512], start=(kc == 0), stop=(kc == 7))
                    if ct < 4:
                        P.copy(qk32[sl][:, ct * 512:(ct + 1) * 512], pv, eng="act")
                    elif ct < 8:
                        P.copy(vo[sl][:, (ct - 4) * 512:(ct - 3) * 512], pv, eng="dve")
                    else:
                        P.act(sg[sl][:, (ct - 8) * 512:(ct - 7) * 512], pv, AF.Silu)

            def do_C(i):
                s, c = items[i]
                sl = i % 2
                v5 = qk32[sl].re("p (h f a d) -> p h f a d", h=8, f=2, a=2)
                o5 = qkr[sl].re("p (h f a d) -> p h f a d", h=8, f=2, a=2)
                a_, b_ = v5[:, :, :, 0, :], v5[:, :, :, 1, :]
                csv = cs[i % 3]
                cos = V(csv.ap[:, 0].unsqueeze(1).broadcast_to([128, 8, 2, 64]), csv.key)
                sin = V(csv.ap[:, 1].unsqueeze(1).broadcast_to([128, 8, 2, 64]), csv.key)
                P.tt(t1, a_, cos, ALU.mult, eng="dve")
                P.tt(t2, b_, sin, ALU.mult, eng="pool")
                P.tt(o5[:, :, :, 0, :], t1, t2, ALU.subtract, eng="dve")
                P.tt(t3, a_, sin, ALU.mult, eng="pool")
                P.tt(t4, b_, cos, ALU.mult, eng="dve")
                P.tt(o5[:, :, :, 1, :], t3, t4, ALU.add, eng="pool")
                kv = qkr[sl][:, 1024:2048].re("p (h d) -> p h d", h=4)
                P.tt(kf[sl], kv, V(T["kdf"].ap.unsqueeze(2).broadcast_to([128, 4, 256]), T["kdf"].key), ALU.mult, eng="pool")
                P.tt(kb[sl], kv, V(T["kdb"].ap.unsqueeze(2).broadcast_to([128, 4, 256]), T["kdb"].key), ALU.mult, eng="pool")

            def do_D(i):
                s, c = items[i]
                sl = i % 2
                for j in range(16):
                    P.tr(pq[:, j, :], qkr[sl][:, j * 128:(j + 1) * 128], ident)
                P.copy(qo[sl][:, 0], pq[:, 0:8, :], eng="dve")
                P.tt(qo[sl][:, 1], pq[:, 0:8, :], T["qdf"], ALU.mult, eng="dve")
                P.tt(qo[sl][:, 2], pq[:, 0:8, :], T["qdb"], ALU.mult, eng="dve")
                P.copy(ko[sl], pq[:, 8:16, :], eng="act")
                P.dma(C.QT[s, c].rearrange("v p h d t -> p v (h d) t"), qo[sl])
                P.dma(C.KT[s, c].rearrange("p h d t -> p (h d) t"), ko[sl])
                P.dma(C.KF[s, c], kf[sl])
                P.dma(C.KB[s, c], kb[sl])
                P.dma(C.VS[s, c].rearrange("p h d -> p (h d)"), vo[sl])
                P.dma(C.SG[s, c].rearrange("p h d -> p (h d)"), sg[sl])

            n = len(items)
            do_load(0)
            if n > 1:
                do_load(1)
            do_A1(0)
            do_A2(0)
            for i in range(n):
                if i + 2 < n:
                    do_load(i + 2)
                do_B(i, range(0, 4))
                if i + 1 < n:
                    do_A1(i + 1)
                do_C(i)
                do_B(i, range(4, 8))
                if i + 1 < n:
                    do_A2(i + 1)
                do_B(i, range(8, 12))
                do_D(i)
            P.emit()


def phase_l0b(C):
    import contextlib
    nc = C.nc
    NS, NT = C.NSEQ, C.NT
    GC = 4
    NG = NT // GC
    with contextlib.ExitStack() as st:
        A = Alloc(nc, st)
        P = Phase(C.K, "l0b")
        ident, mhalf = load_consts(P, A, C)
        T = decay_tables(P, A, C, ("mask",))
        DT, cdec = T["DT"], T["cdec"]
        epsv = A.sb([128, 1], F32, "epsv")
        P.memset(epsv, EPS)
        SbA = A.sb([128, NT, 2, 512], BF16, "SbA")
        SbK = [V(SbA.ap[:, c], ("SbA", c)) for c in range(NT)]
        S32 = [A.sb([128, 2, 512], F32, "S32") for _ in range(2)]
        sj = [0]
        NSF = 4
        Sf = [A.sb([128, 2, 512], BF16, "Sf") for _ in range(NSF)]
        bk = [A.sb([128, GC, 256], BF16, "bk") for _ in range(2)]
        bv = [A.sb([128, GC, 512], BF16, "bv") for _ in range(2)]
        fq = [A.sb([128, GC, 3, 2, 128], BF16, "fq") for _ in range(3)]
        fkT = [A.sb([128, GC, 2, 128], BF16, "fkT") for _ in range(3)]
        fkf = [A.sb([128, GC, 256], BF16, "fkf") for _ in range(3)]
        fv = [A.sb([128, GC, 512], BF16, "fv") for _ in range(3)]
        fsg = [A.sb([128, GC, 512], BF16, "fsg") for _ in range(3)]
        sTm = [A.sb([128, 128], BF16, "sTm") for _ in range(2)]
        st6 = [A.sb([128, 6], F32, "st6") for _ in range(2)]
        mv = [A.sb([128, 8], F32, "mv") for _ in range(2)]
        yn = [A.sb([128, 512], F32, "yn") for _ in range(2)]
        yb = [A.sb([128, 512], BF16, "yb") for _ in range(2)]
        yTa = [A.sb([128, 4, GC * 128], BF16, "yTa") for _ in range(3)]
        pss = A.ps([128, 512], F32, "pss")
        po = [A.ps([128, 512], F32, "po") for _ in range(2)]
        pk = [A.ps([128, 2, 512], F32, "pk") for _ in range(2)]
        pyT = A.ps([128, 8, 128], BF16, "pyT")
        groups = []
        for s in range(NS):
            for h in range(RET_H):
                for g in range(NG - 1, -1, -1):
                    groups.append(("b", s, h, g))
                for g in range(NG):
                    groups.append(("f", s, h, g))
        cntb = [0]
        cntf = [0]
        slot_of = {}

        def do_load(k):
            kind, s, h, g = groups[k]
            c0 = g * GC
            if kind == "b":
                sl = cntb[0] % 2
                cntb[0] += 1
                slot_of[k] = sl
                P.dma(bk[sl], C.KB[s, c0:c0 + GC, :, h, :].rearrange("c p d -> p c d"))
                P.dma(bv[sl], C.VS[s, c0:c0 + GC, :, h, :].rearrange("c p d -> p c d"))
            else:
                sl = cntf[0] % 3
                cntf[0] += 1
                slot_of[k] = sl
                P.dma(fq[sl], C.QT[s, c0:c0 + GC, :, :, h, :, :].rearrange("c v p d t -> p c v d t"))
                P.dma(fkT[sl], C.KT[s, c0:c0 + GC, :, h, :, :].rearrange("c p d t -> p c d t"))
                P.dma(fkf[sl], C.KF[s, c0:c0 + GC, :, h, :].rearrange("c p d -> p c d"))
                P.dma(fv[sl], C.VS[s, c0:c0 + GC, :, h, :].rearrange("c p d -> p c d"))
                P.dma(fsg[sl], C.SG[s, c0:c0 + GC, :, h, :].rearrange("c p d -> p c d"))

        kvi = [0]
        ci = [0]
        q_gn2 = []
        q_gate = []
        q_tail = []

        def flush():
            while q_gn2:
                q_gn2.pop(0)()
            while q_gate:
                q_gate.pop(0)()
            while q_tail:
                q_tail.pop(0)()

        do_load(0)
        for k, (kind, s, h, g) in enumerate(groups):
            if k + 1 < len(groups):
                do_load(k + 1)
            sl = slot_of[k]
            c0 = g * GC
            if kind == "b":
                if g == NG - 1:
                    flush()
                    P.memset(S32[sj[0] % 2], 0.0)
                    P.memset(SbK[NT - 1], 0.0)
                for cc in range(GC - 1, -1, -1):
                    c = c0 + cc
                    if c == 0:
                        continue
                    pkv = pk[kvi[0] % 2]
                    kvi[0] += 1
                    for dc in range(2):
                        P.mm(pkv[:, dc, :], bk[sl][:, cc, dc * 128:(dc + 1) * 128], bv[sl][:, cc, :])
                    sprev, scur = S32[sj[0] % 2], S32[(sj[0] + 1) % 2]
                    sj[0] += 1
                    P.stt(scur, sprev, cdec[:, 4 + h:5 + h], pkv, ALU.mult, ALU.add)
                    P.copy(SbK[c - 1], scur, eng="act")
            else:
                if g == 0:
                    P.memset(S32[sj[0] % 2], 0.0)
                    P.memset(Sf[ci[0] % NSF], 0.0)
                for cc in range(GC):
                    c = c0 + cc
                    k2 = ci[0] % 2
                    sfc, sfn = Sf[ci[0] % NSF], Sf[(ci[0] + 1) % NSF]
                    ci[0] += 1
                    if c < NT - 1:
                        pkv = pk[kvi[0] % 2]
                        kvi[0] += 1
                        for dc in range(2):
                            P.mm(pkv[:, dc, :], fkf[sl][:, cc, dc * 128:(dc + 1) * 128], fv[sl][:, cc, :])
                    for dc in range(2):
                        P.mm(pss[:, 0:128], fkT[sl][:, cc, dc, :], fq[sl][:, cc, 0, dc, :],
                             start=(dc == 0), stop=(dc == 1))
                    P.tt(sTm[k2], pss[:, 0:128], DT[:, h, :], ALU.mult)
                    if q_gn2:
                        q_gn2.pop(0)()
                    pov = po[k2]
                    P.mm(pov, sTm[k2], fv[sl][:, cc, :], start=True, stop=False)
                    for dc in range(2):
                        P.mm(pov, fq[sl][:, cc, 1, dc, :], sfc[:, dc, :], start=False, stop=False)
                    for dc in range(2):
                        P.mm(pov, fq[sl][:, cc, 2, dc, :], SbK[c][:, dc, :], start=False, stop=(dc == 1))
                    if q_tail:
                        q_tail.pop(0)()
                    if c < NT - 1:
                        sprev, scur = S32[sj[0] % 2], S32[(sj[0] + 1) % 2]
                        sj[0] += 1
                        P.stt(scur, sprev, cdec[:, h:h + 1], pkv, ALU.mult, ALU.add)
                        P.copy(sfn, scur, eng="act")
                    m = mv[k2]
                    P.bn_stats(st6[k2], pov)
                    P.bn_aggr(m[:, 0:2], st6[k2])
                    P.ts(m[:, 2:3], m[:, 1:2], epsv, None, ALU.add)
                    P.tt(m[:, 3:4], m[:, 2:3], mhalf, ALU.pow, eng="pool")
                    if q_gate:
                        q_gate.pop(0)()

                    def tail(k2=k2, sl=sl, cc=cc, s=s, h=h, c0=c0):
                        for ec in range(4):
                            P.tr(pyT[:, ec, :], yb[k2][:, ec * 128:(ec + 1) * 128], ident)
                        P.copy(yTa[sl][:, :, cc * 128:(cc + 1) * 128], pyT[:, 0:4, :], eng="act")
                        if cc == GC - 1:
                            P.dma(C.YT[s, :, h * 4:(h + 1) * 4, c0 * 128:(c0 + GC) * 128], yTa[sl])

                    def gate(k2=k2, sl=sl, cc=cc, tail=tail):
                        P.tt(yb[k2], yn[k2], fsg[sl][:, cc, :], ALU.mult, eng="pool")
                        q_tail.append(tail)

                    def gn2(k2=k2, m=m, pov=pov, gate=gate):
                        P.ts(m[:, 4:5], m[:, 0:1], m[:, 3:4], -1.0, ALU.mult, ALU.mult)
                        P.act(yn[k2], pov, AF.Identity, bias=m[:, 4:5], scale=m[:, 3:4])
                        q_gate.append(gate)
                    q_gn2.append(gn2)
        flush()
        P.emit()


def phase_l0c(C, Xin, Xout):
    import contextlib
    nc = C.nc
    NS, NT = C.NSEQ, C.NT
    TS = 4
    with contextlib.ExitStack() as st0:
        A0 = Alloc(nc, st0)
        W = A0.sb([128, 16, D], BF16, "Wro")
        load_weight(C, W, C.ret_w_out, D, 16)
        with contextlib.ExitStack() as st:
            A = Alloc(nc, st)
            P = Phase(C.K, "l0c")
            _a, _b, G = load_mod(P, A, C, 0, 1, want_pp=False)
            yT = [A.sb([128, 16, TS * 128], BF16, "yT") for _ in range(2)]
            xt = [A.sb([128, D], F32, "xt") for _ in range(2 * TS)]
            tmp = [A.sb([128, 512], F32, "tmp") for _ in range(2)]
            po = [A.ps([128, 512], F32, "po") for _ in range(4)]
            items = [(s, sc) for s in range(NS) for sc in range(NT // TS)]

            def do_load(i):
                s, sc = items[i]
                sl = i % 2
                P.dma(yT[sl], C.YT[s, :, :, sc * TS * 128:(sc + 1) * TS * 128])
                for j in range(TS):
                    c = sc * TS + j
                    P.dma(xt[sl * TS + j], Xin[s, c * 128:(c + 1) * 128, :])

            do_load(0)
            for i, (s, sc) in enumerate(items):
                sl = i % 2
                if i + 1 < len(items):
                    do_load(i + 1)
                for j in range(TS):
                    c = sc * TS + j
                    xv = xt[sl * TS + j]
                    for half in range(2):
                        pv = po[(j * 2 + half) % 4]
                        for ec in range(16):
                            P.mm(pv, yT[sl][:, ec, j * 128:(j + 1) * 128], W[:, ec, half * 512:(half + 1) * 512],
                                 start=(ec == 0), stop=(ec == 15))
                        P.tt(tmp[half], pv, G[s][:, half * 512:(half + 1) * 512], ALU.mult)
                        P.tt(xv[:, half * 512:(half + 1) * 512], tmp[half], xv[:, half * 512:(half + 1) * 512],
                             ALU.add, eng="pool")
                    P.dma(Xout[s, c * 128:(c + 1) * 128, :], xv)
            P.emit()


def phase_l1a(C, Xin):
    import contextlib
    nc = C.nc
    NS, NT = C.NSEQ, C.NT
    GC = 4
    with contextlib.ExitStack() as st0:
        A0 = Alloc(nc, st0)
        W = A0.sb([128, 8, ATT_IN], BF16, "Wai")
        load_weight(C, W, C.att_w_in, ATT_IN, 8)
        with contextlib.ExitStack() as st:
            A = Alloc(nc, st)
            P = Phase(C.K, "l1a")
            ident, mhalf = load_consts(P, A, C)
            a_pp, b_pp, _G = load_mod(P, A, C, 1, 1, want_gate=False)
            gains = A.sb([128, 10, 128], F32, "gains")
            for hh in range(10):
                src = C.att_q_gain if hh < 8 else C.att_k_gain
                P.dma(gains[:, hh, :], src.partition_broadcast(128))
            xt = [A.sb([128, D], F32, "xt") for _ in range(3)]
            cs = [A.sb([128, 2, 2, 32], F32, "cs") for _ in range(4)]
            xn = [A.sb([128, D], BF16, "xn") for _ in range(2)]
            junk = A.sb([128, D], BF16, "junk")
            ss = [A.sb([128, 4], F32, "ss") for _ in range(4)]
            hT = [A.sb([128, 8, 128], BF16, "hT") for _ in range(2)]
            qk32_ = [A.sb([128, 10, 128], F32, "qk32") for _ in range(2)]
            sq_ = [A.sb([128, 10, 128], F32, "sq") for _ in range(2)]
            s10 = [A.sb([128, 32], F32, "s10") for _ in range(2)]
            qkn_ = [A.sb([128, 10, 128], F32, "qkn") for _ in range(2)]
            t1_ = [A.sb([128, 10, 2, 32], F32, "t1") for _ in range(2)]
            t2_ = [A.sb([128, 10, 2, 32], F32, "t2") for _ in range(2)]
            t3_ = [A.sb([128, 10, 2, 32], F32, "t3") for _ in range(2)]
            t4_ = [A.sb([128, 10, 2, 32], F32, "t4") for _ in range(2)]
            qkr = [A.sb([128, 10 * 128], BF16, "qkr") for _ in range(2)]
            vb = [A.sb([128, 256], BF16, "vb") for _ in range(2)]
            acc = [A.sb([128, 10, GC * 128], BF16, "acc") for _ in range(2)]
            pT = [A.ps([128, 8, 128], BF16, "pT") for _ in range(2)]
            pp = [A.ps([128, 512], F32, "pp") for _ in range(3)]
            pq = A.ps([128, 16, 128], BF16, "pq")
            items = [(s, c) for s in range(NS) for c in range(NT)]

            def do_load(i):
                s, c = items[i]
                P.dma(xt[i % 3], Xin[s, c * 128:(c + 1) * 128, :])
                P.dma(cs[i % 4], C.rope_a[c])

            def do_A1(i):
                front1(P, xt[i % 3], ss[i % 4], junk, xn[i % 2], mhalf, xn_eng="act")

            def do_A2(i):
                s, c = items[i]
                front2(P, xn[i % 2], ident, pT[i % 2], hT[i % 2], a_pp[s], b_pp[s], ev_eng="act")

            def do_B(i):
                s, c = items[i]
                sl = i % 2
                g = (i // GC) % 2
                cc = c % GC
                hv = hT[sl]
                qk32, sq, qkn, t1, t2, t3, t4 = qk32_[sl], sq_[sl], qkn_[sl], t1_[sl], t2_[sl], t3_[sl], t4_[sl]
                qkf = qk32.re("p h d -> p (h d)")
                for ct in range(3):
                    pv = pp[ct]
                    for kc in range(8):
                        P.mm(pv, hv[:, kc, :], W[:, kc, ct * 512:(ct + 1) * 512], start=(kc == 0), stop=(kc == 7))
                    if ct < 2:
                        P.copy(qkf[:, ct * 512:(ct + 1) * 512], pv, eng="act")
                    else:
                        P.copy(qkf[:, 1024:1280], pv[:, 0:256], eng="act")
                        P.copy(vb[sl], pv[:, 256:512], eng="act")
                P.dma(C.VA[s, c], vb[sl])

            def do_C1(i):
                sl = i % 2
                qk32, sq = qk32_[sl], sq_[sl]
                P.act(sq, qk32, AF.Square)
                sv = s10[sl]
                P.reduce(sv[:, 0:10], sq, ALU.add)
                P.ts(sv[:, 10:20], sv[:, 0:10], 1.0 / ATT_HD, EPS, ALU.mult, ALU.add)
                P.tt(sv[:, 20:30], sv[:, 10:20], V(mhalf.ap.broadcast_to([128, 10]), mhalf.key), ALU.pow, eng="pool")

            def do_C(i):
                s, c = items[i]
                sl = i % 2
                qk32, sq, qkn, t1, t2, t3, t4 = qk32_[sl], sq_[sl], qkn_[sl], t1_[sl], t2_[sl], t3_[sl], t4_[sl]
                sv = s10[sl]
                P.tt(qkn, qk32, V(sv.ap[:, 20:30].unsqueeze(2).broadcast_to([128, 10, 128]), sv.key), ALU.mult, eng="dve")
                P.tt(qkn, qkn, gains, ALU.mult, eng="dve")
                v5 = qkn.re("p h (f a d) -> p h f a d", f=2, a=2)
                o5 = qkr[sl].re("p (h f a d) -> p h f a d", h=10, f=2, a=2)
                a_, b_ = v5[:, :, :, 0, :], v5[:, :, :, 1, :]
                csv = cs[i % 4]
                cos = V(csv.ap[:, 0].unsqueeze(1).broadcast_to([128, 10, 2, 32]), csv.key)
                sin = V(csv.ap[:, 1].unsqueeze(1).broadcast_to([128, 10, 2, 32]), csv.key)
                P.tt(t1, a_, cos, ALU.mult, eng="dve")
                P.tt(t2, b_, sin, ALU.mult, eng="pool")
                P.tt(t3, a_, sin, ALU.mult, eng="pool")
                P.tt(t4, b_, cos, ALU.mult, eng="dve")
                P.tt(o5[:, :, :, 0, :], t1, t2, ALU.subtract, eng="dve")
                P.tt(o5[:, :, :, 1, :], t3, t4, ALU.add, eng="pool")

            def do_D(i):
                s, c = items[i]
                sl = i % 2
                g = (i // GC) % 2
                cc = c % GC
                for j in range(10):
                    P.tr(pq[:, j, :], qkr[sl][:, j * 128:(j + 1) * 128], ident)
                P.copy(acc[g][:, :, cc * 128:(cc + 1) * 128], pq[:, 0:10, :], eng="act")
                if cc == GC - 1:
                    c0 = c - (GC - 1)
                    P.dma(C.QTA[s, :, :, c0 * 128:(c0 + GC) * 128], acc[g][:, 0:8, :])
                    P.dma(C.KTA[s, :, :, c0 * 128:(c0 + GC) * 128], acc[g][:, 8:10, :])

            n = len(items)
            do_load(0)
            if n > 1:
                do_load(1)
            do_A1(0)
            do_A2(0)
            for i in range(n + 2):
                if i + 2 < n:
                    do_load(i + 2)
                if i + 1 < n:
                    do_A1(i + 1)
                if 0 <= i - 1 < n:
                    do_C(i - 1)
                if 0 <= i - 2 < n:
                    do_D(i - 2)
                if i + 1 < n:
                    do_A2(i + 1)
                if i < n:
                    do_B(i)
                    do_C1(i)
            P.emit()


def phase_l1b(C, Xin, Xout):
    import contextlib
    nc = C.nc
    NS, NT, N = C.NSEQ, C.NT, C.N
    QB = 512
    NQB = N // QB
    NP = NT // 2
    with contextlib.ExitStack() as st0:
        A0 = Alloc(nc, st0)
        W = A0.sb([128, 8, D], BF16, "Wao")
        load_weight(C, W, C.att_w_out, D, 8)
        with contextlib.ExitStack() as st:
            A = Alloc(nc, st)
            P = Phase(C.K, "l1b")
            _a, _b, G = load_mod(P, A, C, 1, 1, want_pp=False)
            gq = A.sb([128, 128], F32, "gq")
            gk = A.sb([128, 128], F32, "gk")
            mm_ = A.sb([128, 4], F32, "mm")
            P.dma(gq, C.att_q_gain.partition_broadcast(128))
            P.dma(gk, C.att_k_gain.partition_broadcast(128))
            P.reduce(mm_[:, 0:1], gq, ALU.max, absval=True)
            P.reduce(mm_[:, 1:2], gk, ALU.max, absval=True)
            P.tt(mm_[:, 2:3], mm_[:, 0:1], mm_[:, 1:2], ALU.mult)
            P.ts(mm_[:, 3:4], mm_[:, 2:3], -math.sqrt(ATT_HD), None, ALU.mult)
            negm = mm_[:, 3:4]
            ones32 = A.sb([128, 128], F32, "ones32")
            P.memset(ones32, 1.0)
            onesb = A.sb([128, 128], BF16, "onesb")
            P.memset(onesb, 1.0)
            pe_us = [u for u in range(NP) if u % 4 == 3] if NP >= 8 else []
            kt = [A.sb([128, 2, N], BF16, "kt") for _ in range(2)]
            va = [A.sb([128, NT, 256], BF16, "va") for _ in range(2)]
            qt = [A.sb([128, 8, QB], BF16, "qt") for _ in range(2)]
            PT = [A.sb([128, 2, QB], BF16, "PT") for _ in range(3)]
            acc = [A.sb([128, 2, QB], F32, "acc") for _ in range(2)]
            accp = [A.sb([128, 2, QB], F32, "accp") for _ in range(2)]
            POOL_SHARE = False
            pool_us = [u for u in range(NP) if u % 4 == 3] if (NP >= 8 and POOL_SHARE) else []
            OT = [A.sb([128, 8, QB], BF16, "OT") for _ in range(2)]
            rinv = [A.sb([128, QB], F32, "rinv") for _ in range(2)]
            xt = [A.sb([128, D], F32, "xt") for _ in range(8)]
            tmp = [A.sb([128, 512], F32, "tmp") for _ in range(2)]
            ps_s = [A.ps([128, 2, 512], F32, "ps_s") for _ in range(2)]
            po = [A.ps([128, 512], F32, "po") for _ in range(2)]
            prs = A.ps([128, 512], F32, "prs")
            pw = A.ps([128, 512], F32, "pw")
            scale = float(ATT_HD) ** -0.5
            blocks = [(s, qb) for s in range(NS) for qb in range(NQB)]
            units = [(bi, h, u) for bi in range(len(blocks)) for h in range(ATT_QH) for u in range(NP)]

            def load_seq(s):
                P.dma(kt[s % 2], C.KTA[s])
                P.dma(va[s % 2], C.VA[s].rearrange("c p d -> p c d"))

            def load_block(bi):
                s, qb = blocks[bi]
                sl = bi % 2
                P.dma(qt[sl], C.QTA[s, :, :, qb * QB:(qb + 1) * QB])
                for j in range(4):
                    c = qb * 4 + j
                    P.dma(xt[sl * 4 + j], Xin[s, c * 128:(c + 1) * 128, :])

            def qk(i):
                bi, h, u = units[i]
                s, qb = blocks[bi]
                for t in range(2):
                    kc = 2 * u + t
                    P.mm(ps_s[i % 2][:, t, :], kt[s % 2][:, h // 4, kc * 128:(kc + 1) * 128], qt[bi % 2][:, h, :])

            deferred = []
            hfq = []

            def head_final(bi, h):
                def f():
                    a = acc[h % 2]
                    if pool_us:
                        P.tt(a, a, accp[h % 2], ALU.add)
                    for t in range(2):
                        P.mm(prs, ones32, a[:, t, :], start=(t == 0 and not pe_us), stop=(t == 1))
                    P.recip(rinv[h % 2], prs)
                    P.tt(OT[bi % 2][:, h, :], po[h % 2], rinv[h % 2], ALU.mult)
                return f

            def out_group(bi, j, half):
                def f():
                    s, qb = blocks[bi]
                    c = qb * 4 + j
                    xv = xt[(bi % 2) * 4 + j]
                    for h in range(8):
                        P.mm(pw, OT[bi % 2][:, h, j * 128:(j + 1) * 128], W[:, h, half * 512:(half + 1) * 512],
                             start=(h == 0), stop=(h == 7))
                    P.tt(tmp[half], pw, G[s][:, half * 512:(half + 1) * 512], ALU.mult)
                    P.tt(xv[:, half * 512:(half + 1) * 512], tmp[half], xv[:, half * 512:(half + 1) * 512],
                         ALU.add, eng="pool")
                    if half == 1:
                        P.dma(Xout[s, c * 128:(c + 1) * 128, :], xv)
                return f

            def pv_(i):
                bi, h, u = units[i]
                s, qb = blocks[bi]
                pt = PT[i % 3]
                import os
                if os.environ.get("DBG_EXP1"):
                    for t in range(2):
                        P.act(pt[:, t, :], ps_s[i % 2][:, t, :], AF.Exp, bias=negm, scale=scale)
                else:
                    P.act(pt, ps_s[i % 2], AF.Exp, bias=negm, scale=scale)
                for t in range(2):
                    kc = 2 * u + t
                    P.mm(po[h % 2], va[s % 2][:, kc, (h // 4) * 128:(h // 4 + 1) * 128], pt[:, t, :],
                         start=(kc == 0), stop=(kc == NT - 1))
                if u in pe_us:
                    for t in range(2):
                        P.mm(prs, onesb, pt[:, t, :], start=(u == pe_us[0] and t == 0), stop=False)
                elif u in pool_us:
                    if u == pool_us[0]:
                        P.copy(accp[h % 2], pt, eng="pool")
                    else:
                        P.tt(accp[h % 2], accp[h % 2], pt, ALU.add, eng="pool")
                elif u == 0:
                    P.copy(acc[h % 2], pt, eng="dve")
                else:
                    P.tt(acc[h % 2], acc[h % 2], pt, ALU.add)
                if u == NP - 1:
                    hfq.append(head_final(bi, h))
                    if h == ATT_QH - 1:
                        for j in range(4):
                            for half in range(2):
                                deferred.append(out_group(bi, j, half))

            nu = len(units)
            load_seq(0)
            load_block(0)
            loaded = 0
            qk(0)
            if nu > 1:
                qk(1)
            for i in range(nu):
                bi, h, u = units[i]
                if h == 0 and u == 0:
                    s, qb = blocks[bi]
                    if qb == 0 and s + 1 < NS:
                        load_seq(s + 1)
                pv_(i)
                if i + 2 < nu:
                    nb = units[i + 2][0]
                    if nb > loaded:
                        while hfq:
                            hfq.pop(0)()
                        while deferred:
                            deferred.pop(0)()
                        load_block(nb)
                        loaded = nb
                    qk(i + 2)
                if hfq and u == 0:
                    hfq.pop(0)()
                elif deferred and (u % 2 == 1) and not hfq:
                    deferred.pop(0)()
                if not deferred and loaded < bi + 1 and bi + 1 < len(blocks) and not (h == ATT_QH - 1 and u == NP - 1):
                    load_block(bi + 1)
                    loaded = bi + 1
            while hfq:
                hfq.pop(0)()
            while deferred:
                deferred.pop(0)()
            P.emit()


def rope_table(n_tokens, nf):
    t = np.arange(n_tokens)
    row = (t // GRID_W).astype(np.float32)
    col = (t % GRID_W).astype(np.float32)
    inv = (np.float32(ROPE_THETA) ** (-(np.arange(nf, dtype=np.float32)) / np.float32(nf))).astype(np.float32)
    ar = (row[:, None] * inv[None, :]).astype(np.float32)
    ac = (col[:, None] * inv[None, :]).astype(np.float32)
    tab = np.stack([np.stack([np.cos(ar), np.cos(ac)], 1), np.stack([np.sin(ar), np.sin(ac)], 1)], 1)
    return np.ascontiguousarray(tab.reshape(n_tokens // 128, 128, 2, 2, nf).astype(np.float32))


def build(NSEQ, N, phases=None):
    nc = bass.Bass("TRN2", target_bir_lowering=False)
    C = Ctx()
    C.nc = nc
    C.NSEQ = NSEQ
    C.N = N
    C.NT = N // 128
    NT = C.NT

    def inp(name, shape, dt=F32):
        return nc.dram_tensor(name, list(shape), dt, kind="ExternalInput").ap()

    C.x = inp("x", [NSEQ, N, D])
    C.cT = inp("cT", [128, 8, NSEQ])
    C.mod_w = inp("mod_w", [2, D, 6 * D])
    C.mod_b = inp("mod_b", [2, 6 * D])
    C.norm1_g = inp("norm1_g", [2, D])
    C.norm2_g = inp("norm2_g", [2, D])
    C.ret_w_in = inp("ret_w_in", [D, RET_IN])
    C.ret_decay = inp("ret_decay", [8])
    C.ret_w_out = inp("ret_w_out", [2 * D, D])
    C.att_w_in = inp("att_w_in", [D, ATT_IN])
    C.att_q_gain = inp("att_q_gain", [ATT_HD])
    C.att_k_gain = inp("att_k_gain", [ATT_HD])
    C.att_w_out = inp("att_w_out", [D, D])
    C.mlp_w1 = inp("mlp_w1", [2, D, DFF])
    C.mlp_w2 = inp("mlp_w2", [2, DFF, D])
    C.final_g = inp("final_g", [D])
    C.ident = inp("ident", [128, 128])
    C.rope_r = inp("rope_r", [NT, 128, 2, 2, 64])
    C.rope_a = inp("rope_a", [NT, 128, 2, 2, 32])
    C.y = nc.dram_tensor("y", [NSEQ, N, D], F32, kind="ExternalOutput").ap()

    def scr(name, shape, dt):
        return nc.dram_tensor(name, list(shape), dt).ap()

    C.MOD = scr("MOD", [2, NSEQ, 6 * D], F32)
    C.XA = scr("XA", [NSEQ, N, D], F32)
    C.XB = scr("XB", [NSEQ, N, D], F32)
    C.QT = scr("QT", [NSEQ, NT, 3, 128, 4, 2, 128], BF16)
    C.KT = scr("KT", [NSEQ, NT, 128, 4, 2, 128], BF16)
    C.KF = scr("KF", [NSEQ, NT, 128, 4, 256], BF16)
    C.KB = scr("KB", [NSEQ, NT, 128, 4, 256], BF16)
    C.VS = scr("VS", [NSEQ, NT, 128, 4, 512], BF16)
    C.SG = scr("SG", [NSEQ, NT, 128, 4, 512], BF16)
    C.YT = scr("YT", [NSEQ, 128, 16, N], BF16)
    C.QTA = scr("QTA", [NSEQ, 128, 8, N], BF16)
    C.KTA = scr("KTA", [NSEQ, 128, 2, N], BF16)
    C.VA = scr("VA", [NSEQ, NT, 128, 256], BF16)
    C.K = Kern(nc)
    if phases is None:
        phases = ["mod", "l0a", "l0b", "l0c", "mlp0", "l1a", "l1b", "mlp1"]
    C.phases = phases
    for ph in phases:
        if ph == "mod":
            phase_mod(C)
        elif ph == "l0a":
            phase_l0a(C)
        elif ph == "l0b":
            phase_l0b(C)
        elif ph == "l0c":
            phase_l0c(C, C.x, C.XA)
        elif ph == "mlp0":
            phase_mlp(C, 0, C.XA, C.XB, False)
        elif ph == "l1a":
            phase_l1a(C, C.XB)
        elif ph == "l1b":
            phase_l1b(C, C.XB, C.XA)
        elif ph == "mlp1":
            phase_mlp(C, 1, C.XA, C.y, True)
        elif ph == "l1a_x":
            phase_l1a(C, C.x)
        elif ph == "l1b_xy":
            phase_l1b(C, C.x, C.y)
        elif ph == "l0c_y":
            phase_l0c(C, C.x, C.y)
        elif ph == "mlp_only":
            phase_mlp(C, 0, C.x, C.y, True)
        else:
            raise ValueError(ph)
    return nc, C


def make_in_maps(inputs, NSEQ_P=2, NSEQ_S=1, ncores=8):
    f = lambda a: np.ascontiguousarray(np.asarray(a, dtype=np.float32))
    xp, xs_ = f(inputs["x_prompt"]), f(inputs["x_sample"])
    cp, cs = f(inputs["c_prompt"]), f(inputs["c_sample"])
    N = xp.shape[1]
    shared = {
        "mod_w": f(inputs["mod_w"]), "mod_b": f(inputs["mod_b"]),
        "norm1_g": f(inputs["norm1_g"]), "norm2_g": f(inputs["norm2_g"]),
        "ret_w_in": f(inputs["ret_w_in"])[0], "ret_decay": f(inputs["ret_decay"])[0].reshape(8),
        "ret_w_out": f(inputs["ret_w_out"])[0], "att_w_in": f(inputs["att_w_in"])[0],
        "att_q_gain": f(inputs["att_q_gain"])[0], "att_k_gain": f(inputs["att_k_gain"])[0],
        "att_w_out": f(inputs["att_w_out"])[0], "mlp_w1": f(inputs["mlp_w1"]), "mlp_w2": f(inputs["mlp_w2"]),
        "final_g": f(inputs["final_g"]),
        "ident": np.eye(128, dtype=np.float32),
        "rope_r": rope_table(N, 64), "rope_a": rope_table(N, 32),
    }
    maps = []
    for i in range(ncores):
        x = np.concatenate([xp[NSEQ_P * i:NSEQ_P * (i + 1)], xs_[NSEQ_S * i:NSEQ_S * (i + 1)]], 0)
        c = np.concatenate([cp[NSEQ_P * i:NSEQ_P * (i + 1)], cs[NSEQ_S * i:NSEQ_S * (i + 1)]], 0)
        cT = np.ascontiguousarray(c.reshape(c.shape[0], 8, 128).transpose(2, 1, 0))
        m = dict(shared)
        m["x"] = np.ascontiguousarray(x)
        m["cT"] = cT
        maps.append(m)
    return maps


def kernel(**inputs):
    maps = make_in_maps(inputs)
    N = maps[0]["x"].shape[1]
    nc, C = build(3, N)
    res = run_bass_kernel_spmd(nc, maps, core_ids=list(range(8)))
    ys = [np.asarray(r["y"]) for r in res.results]
    y_prompt = np.concatenate([y[0:2] for y in ys], 0).astype(np.float32)
    y_sample = np.concatenate([y[2:3] for y in ys], 0).astype(np.float32)
    return (y_prompt, y_sample)
```

```python
import math
import numpy as np
import concourse.bass as bass
import concourse.mybir as mybir
from concourse.bass_utils import run_bass_kernel_spmd
from concourse.alu_op_type import AluOpType as ALU

AF = mybir.ActivationFunctionType
F32 = mybir.dt.float32
BF16 = mybir.dt.bfloat16
AX = mybir.AxisListType

D = 1024
DFF = 4096
EPS = 1e-6
RET_H = 4
RET_DK = 256
RET_DV = 512
RET_IN = 6144
ATT_HD = 128
ATT_QH = 8
ATT_KVH = 2
ATT_IN = 1536
GRID_W = 64
ROPE_THETA = 10000.0


class V:
    __slots__ = ("ap", "key")

    def __init__(self, ap, key):
        self.ap = ap
        self.key = key

    def __getitem__(self, idx):
        return V(self.ap[idx], self.key)

    def sub(self, s):
        return V(self.ap, (self.key, s))

    def re(self, pat, **kw):
        return V(self.ap.rearrange(pat, **kw), self.key)

    def bc(self, shape):
        return V(self.ap.broadcast_to(shape), self.key)


def _keys(*xs):
    out = []
    for x in xs:
        if isinstance(x, V) and x.key is not None:
            out.append(x.key)
    return out


def _a(x):
    return x.ap if isinstance(x, V) else x


class Op:
    __slots__ = ("eng", "fn", "reads", "writes", "dma", "signal", "sem", "count", "deps", "pre")

    def __init__(self, eng, fn, reads, writes, dma):
        self.eng = eng
        self.fn = fn
        self.reads = reads
        self.writes = writes
        self.dma = dma
        self.signal = dma
        self.sem = None
        self.count = 0
        self.deps = ()
        self.pre = None


ENGS = ("pe", "act", "dve", "pool", "sp")


class Kern:
    def __init__(self, nc, ndma=32):
        self.nc = nc
        self.ndma = ndma
        self.dma_sems = [nc.alloc_semaphore(name=f"dq{i}") for i in range(ndma)]
        self.dma_n = 0
        self.dma_cnt = [0] * ndma
        self.phase_i = 0
        self.waited = {e: {} for e in ENGS}
        self.n_inst = 0

    def eng(self, name):
        nc = self.nc
        return {"pe": nc.tensor, "act": nc.scalar, "dve": nc.vector, "pool": nc.gpsimd, "sp": nc.sync}[name]


class Phase:
    def __init__(self, K, name):
        self.K = K
        self.name = name
        self.ops = []
        self._uid = 0

    def uid(self):
        self._uid += 1
        return self._uid

    def add(self, eng, fn, reads=(), writes=(), dma=False):
        self.ops.append(Op(eng, fn, tuple(reads), tuple(writes), dma))

    def mm(self, out, lhsT, rhs, start=True, stop=True):
        self.add("pe", lambda e: e.matmul(out.ap, lhsT.ap, rhs.ap, start=start, stop=stop),
                 _keys(lhsT, rhs), _keys(out))

    def tr(self, out, in_, ident):
        self.add("pe", lambda e: e.transpose(out.ap, in_.ap, ident.ap), _keys(in_, ident), _keys(out))

    def act(self, out, in_, func, bias=None, scale=None, accum=None, eng="act"):
        kw = {}
        if bias is not None:
            kw["bias"] = _a(bias)
        if scale is not None:
            kw["scale"] = _a(scale)
        if accum is not None:
            kw["accum_out"] = accum.ap
        self.add(eng, lambda e: e.activation(out.ap, in_.ap, func, **kw),
                 _keys(in_, bias, scale), _keys(out, accum))

    def ts(self, out, in0, s1, s2, op0, op1=None, eng="dve", accum=None):
        kw = {}
        if op1 is not None:
            kw["op1"] = op1
        if accum is not None:
            kw["accum_out"] = accum.ap
        self.add(eng, lambda e: e.tensor_scalar(out.ap, in0.ap, _a(s1), _a(s2), op0, **kw),
                 _keys(in0, s1, s2), _keys(out, accum))

    def tt(self, out, in0, in1, op, eng="dve"):
        self.add(eng, lambda e: e.tensor_tensor(out.ap, in0.ap, in1.ap, op), _keys(in0, in1), _keys(out))

    def stt(self, out, in0, scalar, in1, op0, op1, eng="dve"):
        self.add(eng, lambda e: e.scalar_tensor_tensor(out.ap, in0.ap, _a(scalar), in1.ap, op0, op1),
                 _keys(in0, scalar, in1), _keys(out))

    def copy(self, out, in_, eng="dve"):
        if eng == "act":
            self.add(eng, lambda e: e.copy(out.ap, in_.ap), _keys(in_), _keys(out))
        else:
            self.add(eng, lambda e: e.tensor_copy(out.ap, in_.ap), _keys(in_), _keys(out))

    def memset(self, out, val, eng="pool"):
        self.add(eng, lambda e: e.memset(out.ap, val), (), _keys(out))

    def recip(self, out, in_, eng="dve"):
        self.add(eng, lambda e: e.reciprocal(out.ap, in_.ap), _keys(in_), _keys(out))

    def reduce(self, out, in_, op, axis=AX.X, eng="dve", absval=None):
        self.add(eng, lambda e: e.tensor_reduce(out.ap, in_.ap, axis, op, apply_absolute_value=absval),
                 _keys(in_), _keys(out))

    def bn_stats(self, out, in_):
        self.add("dve", lambda e: e.bn_stats(out.ap, in_.ap), _keys(in_), _keys(out))

    def bn_aggr(self, out, in_):
        self.add("dve", lambda e: e.bn_aggr(out.ap, in_.ap), _keys(in_), _keys(out))

    def iota(self, out, pattern, base, cm):
        self.add("pool", lambda e: e.iota(out.ap, pattern, base=base, channel_multiplier=cm,
                                          allow_small_or_imprecise_dtypes=True), (), _keys(out))

    def dma(self, out, in_, slow=False):
        kw = {"allow_slow_non_contiguous": True} if slow else {}
        self.add("sp", lambda e: e.dma_start(_a(out), _a(in_), **kw), _keys(in_), _keys(out), dma=True)

    def emit(self):
        K = self.K
        nc = K.nc
        ops = self.ops
        last_w = {}
        rd = {}
        for i, op in enumerate(ops):
            deps = {}

            def add(j, kind, op=op, deps=deps, i=i):
                if j is None or j == i:
                    return
                Pp = ops[j]
                if Pp.eng == op.eng and not Pp.dma and not op.dma:
                    if op.eng == "pe":
                        return
                deps[j] = True

            for k in op.reads:
                add(last_w.get(k), "RAW")
            for k in op.writes:
                add(last_w.get(k), "WAW")
                r = rd.get(k)
                if r:
                    for j in r.values():
                        add(j, "WAR")
            op.deps = tuple(deps)
            for j in deps:
                ops[j].signal = True
            for k in op.reads:
                r = rd.get(k)
                if r is None:
                    r = rd[k] = {}
                r[("d", i) if op.dma else op.eng] = i
            for k in op.writes:
                last_w[k] = i
                rd[k] = {}
        if self.name == "w":
            if not hasattr(K, "w_sems"):
                K.w_sems = {e: nc.alloc_semaphore(name=f"phw_{e}") for e in ("pe", "act", "dve", "pool")}
                K.w_cnt = {e: 0 for e in K.w_sems}
            esem, ecnt = K.w_sems, K.w_cnt
        else:
            esem = {e: nc.alloc_semaphore(name=f"ph{K.phase_i}_{e}") for e in ("pe", "act", "dve", "pool")}
            K.phase_i += 1
            ecnt = {e: 0 for e in esem}
        for op in ops:
            if op.dma:
                n = K.dma_n
                K.dma_n += 1
                s = n % K.ndma
                K.dma_cnt[s] += 16
                op.sem = K.dma_sems[s]
                op.count = K.dma_cnt[s]
                if op.count > 16:
                    op.pre = (op.sem, op.count - 16)
            elif op.signal:
                ecnt[op.eng] += 1
                op.sem = esem[op.eng]
                op.count = ecnt[op.eng]
        by_eng = {e: [] for e in ENGS}
        for op in ops:
            by_eng[op.eng].append(op)
        K.n_inst += len(ops)

        def run(e, name):
            waited = K.waited[name]
            for op in by_eng[name]:
                need = {}
                if op.pre is not None:
                    need[id(op.pre[0])] = op.pre
                for j in op.deps:
                    Pp = ops[j]
                    cur = need.get(id(Pp.sem))
                    if cur is None or cur[1] < Pp.count:
                        need[id(Pp.sem)] = (Pp.sem, Pp.count)
                for sid, (sem, val) in need.items():
                    if waited.get(sid, 0) < val:
                        e.wait_ge(sem, val)
                        waited[sid] = val
                inst = op.fn(e)
                if op.dma:
                    inst.then_inc(op.sem, 16)
                elif op.signal:
                    inst.then_inc(op.sem, 1)
            if name == "sp":
                for s in range(K.ndma):
                    if K.dma_cnt[s] > waited.get(id(K.dma_sems[s]), 0):
                        e.wait_ge(K.dma_sems[s], K.dma_cnt[s])
                        waited[id(K.dma_sems[s])] = K.dma_cnt[s]

        with nc.Block() as block:
            @block.tensor
            def _(e):
                run(e, "pe")

            @block.scalar
            def _(e):
                run(e, "act")

            @block.vector
            def _(e):
                run(e, "dve")

            @block.gpsimd
            def _(e):
                run(e, "pool")

            @block.sync
            def _(e):
                run(e, "sp")
        self.ops = []


class Alloc:
    _n = [0]

    def __init__(self, nc, stack):
        self.nc = nc
        self.stack = stack

    def sb(self, shape, dt, name=None):
        Alloc._n[0] += 1
        nm = f"{name or 't'}_{Alloc._n[0]}"
        t = self.stack.enter_context(self.nc.sbuf_tensor(nm, list(shape), dt))
        return V(t[:] if hasattr(t, "__getitem__") else t.ap(), nm)

    def ps(self, shape, dt, name=None):
        Alloc._n[0] += 1
        nm = f"{name or 'p'}_{Alloc._n[0]}"
        t = self.stack.enter_context(self.nc.psum_tensor(nm, list(shape), dt))
        return V(t[:] if hasattr(t, "__getitem__") else t.ap(), nm)


class Ctx:
    pass


def load_weight(C, dst, src, ncols, kcs):
    import contextlib
    nc = C.nc
    with contextlib.ExitStack() as st:
        A = Alloc(nc, st)
        CH = min(ncols, 2048)
        stage = [A.sb([128, CH], F32, "wst") for _ in range(3)]
        P = Phase(C.K, "w")
        i = 0
        for kc in range(kcs):
            for c0 in range(0, ncols, CH):
                sv = stage[i % 3]
                P.dma(sv, src[kc * 128:(kc + 1) * 128, c0:c0 + CH])
                P.copy(dst[:, kc, c0:c0 + CH], sv, eng=("dve" if i % 2 == 0 else "pool"))
                i += 1
        P.emit()


def load_consts(P, A, C):
    ident32 = A.sb([128, 128], F32, "id32")
    ident = A.sb([128, 128], BF16, "id")
    mhalf = A.sb([128, 1], F32, "mh")
    P.dma(ident32, C.ident)
    P.copy(ident, ident32)
    P.memset(mhalf, -0.5)
    return ident, mhalf


def load_mod(P, A, C, layer, which, want_pp=True, want_gate=True):
    off = 0 if which == 1 else 3 * D
    ng = C.norm1_g if which == 1 else C.norm2_g
    gpp = A.sb([128, 8], F32, "gpp")
    P.dma(gpp, ng[layer].rearrange("(kc p) -> p kc", p=128), slow=True)
    a_pp, b_pp, G = [], [], []
    for s in range(C.NSEQ):
        row = C.MOD[layer, s]
        if want_pp:
            sc = A.sb([128, 8], F32, "scpp")
            a = A.sb([128, 8], F32, "app")
            b = A.sb([128, 8], F32, "bpp")
            P.dma(b, row[off:off + D].rearrange("(kc p) -> p kc", p=128), slow=True)
            P.dma(sc, row[off + D:off + 2 * D].rearrange("(kc p) -> p kc", p=128), slow=True)
            P.stt(a, sc, 1.0, gpp, ALU.add, ALU.mult)
            a_pp.append(a)
            b_pp.append(b)
        if want_gate:
            g = A.sb([128, D], F32, "gate")
            P.dma(g, row[off + 2 * D:off + 3 * D].partition_broadcast(128))
            G.append(g)
    return a_pp, b_pp, G


def front1(P, xt, ss, junk, xn, mhalf, xn_eng="dve"):
    P.act(junk, xt, AF.Square, accum=ss[:, 0:1])
    P.ts(ss[:, 1:2], ss[:, 0:1], 1.0 / D, EPS, ALU.mult, ALU.add)
    P.tt(ss[:, 2:3], ss[:, 1:2], mhalf, ALU.pow, eng="pool")
    if xn_eng == "act":
        P.act(xn, xt, AF.Identity, scale=ss[:, 2:3])
    else:
        P.ts(xn, xt, ss[:, 2:3], None, ALU.mult)


def front2(P, xn, ident, pT, hT, a_pp, b_pp, ev_eng="dve"):
    for kc in range(8):
        P.tr(pT[:, kc, :], xn[:, kc * 128:(kc + 1) * 128], ident)
    for kc in range(8):
        if ev_eng == "act":
            P.act(hT[:, kc, :], pT[:, kc, :], AF.Identity, bias=b_pp[:, kc:kc + 1], scale=a_pp[:, kc:kc + 1])
        else:
            P.ts(hT[:, kc, :], pT[:, kc, :], a_pp[:, kc:kc + 1], b_pp[:, kc:kc + 1], ALU.mult, ALU.add)


def front(P, xt, ss, junk, xn, ident, mhalf, pT, hT, a_pp, b_pp, ev_eng="dve"):
    front1(P, xt, ss, junk, xn, mhalf)
    front2(P, xn, ident, pT, hT, a_pp, b_pp, ev_eng)


def phase_mod(C):
    import contextlib
    nc = C.nc
    NS = C.NSEQ
    with contextlib.ExitStack() as st:
        A = Alloc(nc, st)
        P = Phase(C.K, "mod")
        ct32 = A.sb([128, 8, NS], F32)
        sct = A.sb([128, 8, NS], BF16)
        P.dma(ct32, C.cT)
        P.act(sct, ct32, AF.Silu)
        stage = [A.sb([128, 8, 512], F32, "mst") for _ in range(2)]
        wb = [A.sb([128, 8, 512], BF16, "mwb") for _ in range(2)]
        bias = A.sb([NS, 2, 6 * D], F32)
        res = A.sb([NS, 2, 6 * D], F32)
        pp = [A.ps([128, 512], F32) for _ in range(2)]
        for l in range(2):
            for s in range(NS):
                P.dma(bias[s:s + 1, l, :], C.mod_b[l:l + 1, :])
        i = 0
        for l in range(2):
            for nt in range(12):
                sv, wv, pv = stage[i % 2], wb[i % 2], pp[i % 2]
                P.dma(sv, C.mod_w[l, :, nt * 512:(nt + 1) * 512].rearrange("(kc p) n -> p kc n", p=128))
                P.copy(wv, sv, eng=("dve" if i % 2 == 0 else "pool"))
                for kc in range(8):
                    P.mm(pv[0:NS, :], sct[:, kc, :], wv[:, kc, :], start=(kc == 0), stop=(kc == 7))
                P.tt(res[:, l, nt * 512:(nt + 1) * 512], pv[0:NS, :], bias[:, l, nt * 512:(nt + 1) * 512], ALU.add)
                i += 1
        for l in range(2):
            P.dma(C.MOD[l], res[:, l, :])
        P.emit()


def phase_mlp(C, layer, Xin, Xout, final):
    import contextlib
    nc = C.nc
    NS, NT = C.NSEQ, C.NT
    TS = 2
    with contextlib.ExitStack() as st0:
        A0 = Alloc(nc, st0)
        W1 = A0.sb([128, 8, DFF], BF16, "W1")
        W2 = A0.sb([128, 32, D], BF16, "W2")
        load_weight(C, W1, C.mlp_w1[layer], DFF, 8)
        load_weight(C, W2, C.mlp_w2[layer], D, 32)
        with contextlib.ExitStack() as st:
            A = Alloc(nc, st)
            P = Phase(C.K, "mlp")
            ident, mhalf = load_consts(P, A, C)
            a_pp, b_pp, G = load_mod(P, A, C, layer, 2)
            if final:
                FG = A.sb([128, D], F32, "FG")
                P.dma(FG, C.final_g.partition_broadcast(128))
            xt = [A.sb([128, D], F32, "xt") for _ in range(3 * TS)]
            xn = [A.sb([128, D], BF16, "xn") for _ in range(2)]
            junk = A.sb([128, D], BF16, "junk")
            ss = [A.sb([128, 4], F32, "ss") for _ in range(4)]
            hT = [A.sb([128, 8, TS * 128], BF16, "hT") for _ in range(2)]
            uT = A.sb([128, 32, TS * 128], BF16, "uT")
            r32 = [A.sb([128, TS * 128], F32, "r32") for _ in range(2)]
            tmp = [A.sb([128, 512], F32, "tmp") for _ in range(2)]
            pT = [A.ps([128, 8, 128], BF16, "pT") for _ in range(2)]
            pu = [A.ps([128, 512], F32, "pu") for _ in range(3)]
            po = [A.ps([128, 512], F32, "po") for _ in range(3)]
            items = [(s, sc) for s in range(NS) for sc in range(NT // TS)]
            cnt = [0]

            def xs_of(i):
                return [xt[(i % 3) * TS + j] for j in range(TS)]

            def do_load(i):
                s, sc = items[i]
                for j in range(TS):
                    c = sc * TS + j
                    P.dma(xs_of(i)[j], Xin[s, c * 128:(c + 1) * 128, :])

            fk = {}

            def do_front1(i):
                ks = []
                for j in range(TS):
                    k = cnt[0]
                    cnt[0] += 1
                    ks.append(k)
                    front1(P, xs_of(i)[j], ss[k % 4], junk, xn[k % 2], mhalf)
                fk[i] = ks

            def do_front2(i):
                s, sc = items[i]
                for j in range(TS):
                    k = fk[i][j]
                    front2(P, xn[k % 2], ident, pT[k % 2], hT[i % 2][:, :, j * 128:(j + 1) * 128],
                           a_pp[s], b_pp[s], ev_eng=("act" if k % 2 else "dve"))

            def do_w1(i):
                hv = hT[i % 2]
                for fc in range(32):
                    pv = pu[fc % 3]
                    for kc in range(8):
                        P.mm(pv[:, 0:TS * 128], W1[:, kc, fc * 128:(fc + 1) * 128], hv[:, kc, :],
                             start=(kc == 0), stop=(kc == 7))
                    rv = r32[fc % 2]
                    P.act(rv, pv[:, 0:TS * 128], AF.Relu)
                    P.tt(uT[:, fc, :], rv, rv, ALU.mult, eng=("dve" if fc % 2 == 0 else "pool"))

            def do_w2(i):
                s, sc = items[i]
                xs = xs_of(i)
                for j in range(TS):
                    for half in range(2):
                        pv = po[(j * 2 + half) % 3]
                        for fc in range(32):
                            P.mm(pv, uT[:, fc, j * 128:(j + 1) * 128], W2[:, fc, half * 512:(half + 1) * 512],
                                 start=(fc == 0), stop=(fc == 31))
                        tv = tmp[half]
                        P.tt(tv, pv, G[s][:, half * 512:(half + 1) * 512], ALU.mult)
                        P.tt(xs[j][:, half * 512:(half + 1) * 512], tv, xs[j][:, half * 512:(half + 1) * 512],
                             ALU.add, eng="pool")
                    c = sc * TS + j
                    if final:
                        k = cnt[0]
                        cnt[0] += 1
                        sv = ss[k % 4]
                        P.act(junk, xs[j], AF.Square, accum=sv[:, 0:1])
                        P.ts(sv[:, 1:2], sv[:, 0:1], 1.0 / D, EPS, ALU.mult, ALU.add)
                        P.tt(sv[:, 2:3], sv[:, 1:2], mhalf, ALU.pow, eng="pool")
                        P.stt(xs[j], xs[j], sv[:, 2:3], FG, ALU.mult, ALU.mult)
                    P.dma(Xout[s, c * 128:(c + 1) * 128, :], xs[j])

            n = len(items)
            do_load(0)
            if n > 1:
                do_load(1)
            do_front1(0)
            do_front2(0)
            for i in range(n):
                if i + 2 < n:
                    do_load(i + 2)
                if i + 1 < n:
                    do_front1(i + 1)
                do_w1(i)
                if i + 1 < n:
                    do_front2(i + 1)
                do_w2(i)
            P.emit()


def decay_tables(P, A, C, want):
    T = {}
    dl = A.sb([128, 8], F32, "dl")
    lg = A.sb([128, 8], F32, "lg")
    P.dma(dl, C.ret_decay.partition_broadcast(128))
    P.act(lg, dl, AF.Exp, scale=-1.0)
    P.ts(lg, lg, 1.0, None, ALU.add)
    P.act(lg, lg, AF.Ln)
    P.ts(lg, lg, -1.0, None, ALU.mult)
    T["lg"] = lg
    pidx = A.sb([128, 1], F32, "pidx")
    P.iota(pidx, [[0, 1]], 0, 1)
    jidx = A.sb([128, 128], F32, "jidx")
    P.iota(jidx, [[1, 128]], 0, 0)
    scale = float(RET_DK) ** -0.5
    if "pp" in want:
        rp = A.sb([128, 1], F32, "rp")
        P.ts(rp, pidx, -1.0, 127.0, ALU.mult, ALU.add)
        kdf = A.sb([128, 4], F32, "kdf")
        kdb = A.sb([128, 4], F32, "kdb")
        for h in range(4):
            P.act(kdf[:, h:h + 1], rp, AF.Exp, scale=lg[:, h:h + 1])
            P.act(kdb[:, h:h + 1], pidx, AF.Exp, scale=lg[:, 4 + h:5 + h])
        T["kdf"], T["kdb"] = kdf, kdb
        j1 = A.sb([128, 128], F32, "j1")
        jr = A.sb([128, 128], F32, "jr")
        P.ts(j1, jidx, 1.0, None, ALU.add)
        P.ts(jr, jidx, -1.0, 128.0, ALU.mult, ALU.add)
        qdf = A.sb([128, 8, 128], F32, "qdf")
        qdb = A.sb([128, 8, 128], F32, "qdb")
        for h in range(4):
            for dc in range(2):
                P.act(qdf[:, 2 * h + dc, :], j1, AF.Exp, scale=lg[:, h:h + 1])
                P.act(qdb[:, 2 * h + dc, :], jr, AF.Exp, scale=lg[:, 4 + h:5 + h])
        P.ts(qdf, qdf, scale, None, ALU.mult)
        P.ts(qdb, qdb, scale, None, ALU.mult)
        T["qdf"], T["qdb"] = qdf, qdb
    if "mask" in want:
        diff = A.sb([128, 128], F32, "diff")
        P.ts(diff, jidx, pidx, None, ALU.subtract)
        dpos = A.sb([128, 128], F32, "dpos")
        dneg = A.sb([128, 128], F32, "dneg")
        mge = A.sb([128, 128], F32, "mge")
        P.ts(dpos, diff, 0.0, None, ALU.max)
        P.tt(dneg, dpos, diff, ALU.subtract)
        P.ts(mge, diff, 0.0, None, ALU.is_ge)
        DT = A.sb([128, 4, 128], F32, "DT")
        ea = A.sb([128, 128], F32, "ea")
        eb = A.sb([128, 128], F32, "eb")
        for h in range(4):
            P.act(ea, dpos, AF.Exp, scale=lg[:, h:h + 1])
            P.act(eb, dneg, AF.Exp, scale=lg[:, 4 + h:5 + h])
            P.tt(ea, ea, eb, ALU.subtract)
            P.tt(ea, ea, mge, ALU.mult)
            P.tt(ea, ea, eb, ALU.add)
            P.ts(DT[:, h, :], ea, scale, None, ALU.mult)
        T["DT"] = DT
        cd = A.sb([128, 8], F32, "cdec")
        P.act(cd, lg, AF.Exp, scale=128.0)
        T["cdec"] = cd
    return T


def phase_l0a(C):
    import contextlib
    nc = C.nc
    NS, NT = C.NSEQ, C.NT
    with contextlib.ExitStack() as st0:
        A0 = Alloc(nc, st0)
        W = A0.sb([128, 8, RET_IN], BF16, "Win")
        load_weight(C, W, C.ret_w_in, RET_IN, 8)
        with contextlib.ExitStack() as st:
            A = Alloc(nc, st)
            P = Phase(C.K, "l0a")
            ident, mhalf = load_consts(P, A, C)
            a_pp, b_pp, _G = load_mod(P, A, C, 0, 1, want_gate=False)
            T = decay_tables(P, A, C, ("pp",))
            xt = [A.sb([128, D], F32, "xt") for _ in range(3)]
            cs = [A.sb([128, 2, 2, 64], F32, "cs") for _ in range(3)]
            xn = [A.sb([128, D], BF16, "xn") for _ in range(2)]
            junk = A.sb([128, D], BF16, "junk")
            ss = [A.sb([128, 4], F32, "ss") for _ in range(4)]
            hT = [A.sb([128, 8, 128], BF16, "hT") for _ in range(2)]
            qk32 = [A.sb([128, 2048], F32, "qk32")] * 2
            t1 = A.sb([128, 8, 2, 64], F32, "t1")
            t2 = A.sb([128, 8, 2, 64], F32, "t2")
            t3, t4 = t1, t2
            qkr = [A.sb([128, 2048], BF16, "qkr") for _ in range(2)]
            kf = [A.sb([128, 4, 256], BF16, "kf") for _ in range(2)]
            kb = [A.sb([128, 4, 256], BF16, "kb") for _ in range(2)]
            vo = [A.sb([128, 2048], BF16, "vo") for _ in range(2)]
            sg = [A.sb([128, 2048], BF16, "sg") for _ in range(2)]
            qo = [A.sb([128, 3, 8, 128], BF16, "qo") for _ in range(2)]
            ko = [A.sb([128, 8, 128], BF16, "ko") for _ in range(2)]
            pT = [A.ps([128, 8, 128], BF16, "pT") for _ in range(2)]
            pp = [A.ps([128, 512], F32, "pp") for _ in range(4)]
            pq = A.ps([128, 16, 128], BF16, "pq")
            items = [(s, c) for s in range(NS) for c in range(NT)]

            def do_load(i):
                s, c = items[i]
                P.dma(xt[i % 3], C.x[s, c * 128:(c + 1) * 128, :])
                P.dma(cs[i % 3], C.rope_r[c])

            def do_A1(i):
                front1(P, xt[i % 3], ss[i % 4], junk, xn[i % 2], mhalf)

            def do_A2(i):
                s, c = items[i]
                front2(P, xn[i % 2], ident, pT[i % 2], hT[i % 2], a_pp[s], b_pp[s], ev_eng="dve")

            def do_B(i, cts):
                s, c = items[i]
                sl = i % 2
                hv = hT[sl]
                for ct in cts:
                    pv = pp[ct % 4]
                    for kc in range(8):
                        P.mm(pv, hv[:, kc, :], W[:, kc, ct * 512:(ct + 1) * 512], start=(kc == 0), stop=(kc == 7))
                    if ct < 4:
                        P.copy(qk32[sl][:, ct * 512:(ct + 1) * 512], pv, eng="act")
                    elif ct < 8:
                        P.copy(vo[sl][:, (ct - 4) * 512:(ct - 3) * 512], pv, eng="dve")
                    else:
                        P.act(sg[sl][:, (ct - 8) * 512:(ct - 7) * 512], pv, AF.Silu)

            def do_C(i):
                s, c = items[i]
                sl = i % 2
                v5 = qk32[sl].re("p (h f a d) -> p h f a d", h=8, f=2, a=2)
                o5 = qkr[sl].re("p (h f a d) -> p h f a d", h=8, f=2, a=2)
                a_, b_ = v5[:, :, :, 0, :], v5[:, :, :, 1, :]
                csv = cs[i % 3]
                cos = V(csv.ap[:, 0].unsqueeze(1).broadcast_to([128, 8, 2, 64]), csv.key)
                sin = V(csv.ap[:, 1].unsqueeze(1).broadcast_to([128, 8, 2, 64]), csv.key)
                P.tt(t1, a_, cos, ALU.mult, eng="dve")
                P.tt(t2, b_, sin, ALU.mult, eng="pool")
                P.tt(o5[:, :, :, 0, :], t1, t2, ALU.subtract, eng="dve")
                P.tt(t3, a_, sin, ALU.mult, eng="pool")
                P.tt(t4, b_, cos, ALU.mult, eng="dve")
                P.tt(o5[:, :, :, 1, :], t3, t4, ALU.add, eng="pool")
                kv = qkr[sl][:, 1024:2048].re("p (h d) -> p h d", h=4)
                P.tt(kf[sl], kv, V(T["kdf"].ap.unsqueeze(2).broadcast_to([128, 4, 256]), T["kdf"].key), ALU.mult, eng="pool")
                P.tt(kb[sl], kv, V(T["kdb"].ap.unsqueeze(2).broadcast_to([128, 4, 256]), T["kdb"].key), ALU.mult, eng="pool")

            def do_D(i):
                s, c = items[i]
                sl = i % 2
                for j in range(16):
                    P.tr(pq[:, j, :], qkr[sl][:, j * 128:(j + 1) * 128], ident)
                P.copy(qo[sl][:, 0], pq[:, 0:8, :], eng="dve")
                P.tt(qo[sl][:, 1], pq[:, 0:8, :], T["qdf"], ALU.mult, eng="dve")
                P.tt(qo[sl][:, 2], pq[:, 0:8, :], T["qdb"], ALU.mult, eng="dve")
                P.copy(ko[sl], pq[:, 8:16, :], eng="act")
                P.dma(C.QT[s, c].rearrange("v p h d t -> p v (h d) t"), qo[sl])
                P.dma(C.KT[s, c].rearrange("p h d t -> p (h d) t"), ko[sl])
                P.dma(C.KF[s, c], kf[sl])
                P.dma(C.KB[s, c], kb[sl])
                P.dma(C.VS[s, c].rearrange("p h d -> p (h d)"), vo[sl])
                P.dma(C.SG[s, c].rearrange("p h d -> p (h d)"), sg[sl])

            n = len(items)
            do_load(0)
            if n > 1:
                do_load(1)
            do_A1(0)
            do_A2(0)
            for i in range(n):
                if i + 2 < n:
                    do_load(i + 2)
                do_B(i, range(0, 4))
                if i + 1 < n:
                    do_A1(i + 1)
                do_C(i)
                do_B(i, range(4, 8))
                if i + 1 < n:
                    do_A2(i + 1)
                do_B(i, range(8, 12))
                do_D(i)
            P.emit()


def phase_l0b(C):
    import contextlib
    nc = C.nc
    NS, NT = C.NSEQ, C.NT
    GC = 4
    NG = NT // GC
    with contextlib.ExitStack() as st:
        A = Alloc(nc, st)
        P = Phase(C.K, "l0b")
        ident, mhalf = load_consts(P, A, C)
        T = decay_tables(P, A, C, ("mask",))
        DT, cdec = T["DT"], T["cdec"]
        epsv = A.sb([128, 1], F32, "epsv")
        P.memset(epsv, EPS)
        SbA = A.sb([128, NT, 2, 512], BF16, "SbA")
        SbK = [V(SbA.ap[:, c], ("SbA", c)) for c in range(NT)]
        S32 = [A.sb([128, 2, 512], F32, "S32") for _ in range(2)]
        sj = [0]
        NSF = 4
        Sf = [A.sb([128, 2, 512], BF16, "Sf") for _ in range(NSF)]
        bk = [A.sb([128, GC, 256], BF16, "bk") for _ in range(2)]
        bv = [A.sb([128, GC, 512], BF16, "bv") for _ in range(2)]
        fq = [A.sb([128, GC, 3, 2, 128], BF16, "fq") for _ in range(3)]
        fkT = [A.sb([128, GC, 2, 128], BF16, "fkT") for _ in range(3)]
        fkf = [A.sb([128, GC, 256], BF16, "fkf") for _ in range(3)]
        fv = [A.sb([128, GC, 512], BF16, "fv") for _ in range(3)]
        fsg = [A.sb([128, GC, 512], BF16, "fsg") for _ in range(3)]
        sTm = [A.sb([128, 128], BF16, "sTm") for _ in range(2)]
        st6 = [A.sb([128, 6], F32, "st6") for _ in range(2)]
        mv = [A.sb([128, 8], F32, "mv") for _ in range(2)]
        yn = [A.sb([128, 512], F32, "yn") for _ in range(2)]
        yb = [A.sb([128, 512], BF16, "yb") for _ in range(2)]
        yTa = [A.sb([128, 4, GC * 128], BF16, "yTa") for _ in range(3)]
        pss = A.ps([128, 512], F32, "pss")
        po = [A.ps([128, 512], F32, "po") for _ in range(2)]
        pk = [A.ps([128, 2, 512], F32, "pk") for _ in range(2)]
        pyT = A.ps([128, 8, 128], BF16, "pyT")
        groups = []
        for s in range(NS):
            for h in range(RET_H):
                for g in range(NG - 1, -1, -1):
                    groups.append(("b", s, h, g))
                for g in range(NG):
                    groups.append(("f", s, h, g))
        cntb = [0]
        cntf = [0]
        slot_of = {}

        def do_load(k):
            kind, s, h, g = groups[k]
            c0 = g * GC
            if kind == "b":
                sl = cntb[0] % 2
                cntb[0] += 1
                slot_of[k] = sl
                P.dma(bk[sl], C.KB[s, c0:c0 + GC, :, h, :].rearrange("c p d -> p c d"))
                P.dma(bv[sl], C.VS[s, c0:c0 + GC, :, h, :].rearrange("c p d -> p c d"))
            else:
                sl = cntf[0] % 3
                cntf[0] += 1
                slot_of[k] = sl
                P.dma(fq[sl], C.QT[s, c0:c0 + GC, :, :, h, :, :].rearrange("c v p d t -> p c v d t"))
                P.dma(fkT[sl], C.KT[s, c0:c0 + GC, :, h, :, :].rearrange("c p d t -> p c d t"))
                P.dma(fkf[sl], C.KF[s, c0:c0 + GC, :, h, :].rearrange("c p d -> p c d"))
                P.dma(fv[sl], C.VS[s, c0:c0 + GC, :, h, :].rearrange("c p d -> p c d"))
                P.dma(fsg[sl], C.SG[s, c0:c0 + GC, :, h, :].rearrange("c p d -> p c d"))

        kvi = [0]
        ci = [0]
        q_gn2 = []
        q_gate = []
        q_tail = []

        def flush():
            while q_gn2:
                q_gn2.pop(0)()
            while q_gate:
                q_gate.pop(0)()
            while q_tail:
                q_tail.pop(0)()

        do_load(0)
        for k, (kind, s, h, g) in enumerate(groups):
            if k + 1 < len(groups):
                do_load(k + 1)
            sl = slot_of[k]
            c0 = g * GC
            if kind == "b":
                if g == NG - 1:
                    flush()
                    P.memset(S32[sj[0] % 2], 0.0)
                    P.memset(SbK[NT - 1], 0.0)
                for cc in range(GC - 1, -1, -1):
                    c = c0 + cc
                    if c == 0:
                        continue
                    pkv = pk[kvi[0] % 2]
                    kvi[0] += 1
                    for dc in range(2):
                        P.mm(pkv[:, dc, :], bk[sl][:, cc, dc * 128:(dc + 1) * 128], bv[sl][:, cc, :])
                    sprev, scur = S32[sj[0] % 2], S32[(sj[0] + 1) % 2]
                    sj[0] += 1
                    P.stt(scur, sprev, cdec[:, 4 + h:5 + h], pkv, ALU.mult, ALU.add)
                    P.copy(SbK[c - 1], scur, eng="act")
            else:
                if g == 0:
                    P.memset(S32[sj[0] % 2], 0.0)
                    P.memset(Sf[ci[0] % NSF], 0.0)
                for cc in range(GC):
                    c = c0 + cc
                    k2 = ci[0] % 2
                    sfc, sfn = Sf[ci[0] % NSF], Sf[(ci[0] + 1) % NSF]
                    ci[0] += 1
                    if c < NT - 1:
                        pkv = pk[kvi[0] % 2]
                        kvi[0] += 1
                        for dc in range(2):
                            P.mm(pkv[:, dc, :], fkf[sl][:, cc, dc * 128:(dc + 1) * 128], fv[sl][:, cc, :])
                    for dc in range(2):
                        P.mm(pss[:, 0:128], fkT[sl][:, cc, dc, :], fq[sl][:, cc, 0, dc, :],
                             start=(dc == 0), stop=(dc == 1))
                    P.tt(sTm[k2], pss[:, 0:128], DT[:, h, :], ALU.mult)
                    if q_gn2:
                        q_gn2.pop(0)()
                    pov = po[k2]
                    P.mm(pov, sTm[k2], fv[sl][:, cc, :], start=True, stop=False)
                    for dc in range(2):
                        P.mm(pov, fq[sl][:, cc, 1, dc, :], sfc[:, dc, :], start=False, stop=False)
                    for dc in range(2):
                        P.mm(pov, fq[sl][:, cc, 2, dc, :], SbK[c][:, dc, :], start=False, stop=(dc == 1))
                    if q_tail:
                        q_tail.pop(0)()
                    if c < NT - 1:
                        sprev, scur = S32[sj[0] % 2], S32[(sj[0] + 1) % 2]
                        sj[0] += 1
                        P.stt(scur, sprev, cdec[:, h:h + 1], pkv, ALU.mult, ALU.add)
                        P.copy(sfn, scur, eng="act")
                    m = mv[k2]
                    P.bn_stats(st6[k2], pov)
                    P.bn_aggr(m[:, 0:2], st6[k2])
                    P.ts(m[:, 2:3], m[:, 1:2], epsv, None, ALU.add)
                    P.tt(m[:, 3:4], m[:, 2:3], mhalf, ALU.pow, eng="pool")
                    if q_gate:
                        q_gate.pop(0)()

                    def tail(k2=k2, sl=sl, cc=cc, s=s, h=h, c0=c0):
                        for ec in range(4):
                            P.tr(pyT[:, ec, :], yb[k2][:, ec * 128:(ec + 1) * 128], ident)
                        P.copy(yTa[sl][:, :, cc * 128:(cc + 1) * 128], pyT[:, 0:4, :], eng="act")
                        if cc == GC - 1:
                            P.dma(C.YT[s, :, h * 4:(h + 1) * 4, c0 * 128:(c0 + GC) * 128], yTa[sl])

                    def gate(k2=k2, sl=sl, cc=cc, tail=tail):
                        P.tt(yb[k2], yn[k2], fsg[sl][:, cc, :], ALU.mult, eng="pool")
                        q_tail.append(tail)

                    def gn2(k2=k2, m=m, pov=pov, gate=gate):
                        P.ts(m[:, 4:5], m[:, 0:1], m[:, 3:4], -1.0, ALU.mult, ALU.mult)
                        P.act(yn[k2], pov, AF.Identity, bias=m[:, 4:5], scale=m[:, 3:4])
                        q_gate.append(gate)
                    q_gn2.append(gn2)
        flush()
        P.emit()


def phase_l0c(C, Xin, Xout):
    import contextlib
    nc = C.nc
    NS, NT = C.NSEQ, C.NT
    TS = 4
    with contextlib.ExitStack() as st0:
        A0 = Alloc(nc, st0)
        W = A0.sb([128, 16, D], BF16, "Wro")
        load_weight(C, W, C.ret_w_out, D, 16)
        with contextlib.ExitStack() as st:
            A = Alloc(nc, st)
            P = Phase(C.K, "l0c")
            _a, _b, G = load_mod(P, A, C, 0, 1, want_pp=False)
            yT = [A.sb([128, 16, TS * 128], BF16, "yT") for _ in range(2)]
            xt = [A.sb([128, D], F32, "xt") for _ in range(2 * TS)]
            tmp = [A.sb([128, 512], F32, "tmp") for _ in range(2)]
            po = [A.ps([128, 512], F32, "po") for _ in range(4)]
            items = [(s, sc) for s in range(NS) for sc in range(NT // TS)]

            def do_load(i):
                s, sc = items[i]
                sl = i % 2
                P.dma(yT[sl], C.YT[s, :, :, sc * TS * 128:(sc + 1) * TS * 128])
                for j in range(TS):
                    c = sc * TS + j
                    P.dma(xt[sl * TS + j], Xin[s, c * 128:(c + 1) * 128, :])

            do_load(0)
            for i, (s, sc) in enumerate(items):
                sl = i % 2
                if i + 1 < len(items):
                    do_load(i + 1)
                for j in range(TS):
                    c = sc * TS + j
                    xv = xt[sl * TS + j]
                    for half in range(2):
                        pv = po[(j * 2 + half) % 4]
                        for ec in range(16):
                            P.mm(pv, yT[sl][:, ec, j * 128:(j + 1) * 128], W[:, ec, half * 512:(half + 1) * 512],
                                 start=(ec == 0), stop=(ec == 15))
                        P.tt(tmp[half], pv, G[s][:, half * 512:(half + 1) * 512], ALU.mult)
                        P.tt(xv[:, half * 512:(half + 1) * 512], tmp[half], xv[:, half * 512:(half + 1) * 512],
                             ALU.add, eng="pool")
                    P.dma(Xout[s, c * 128:(c + 1) * 128, :], xv)
            P.emit()


def phase_l1a(C, Xin):
    import contextlib
    nc = C.nc
    NS, NT = C.NSEQ, C.NT
    GC = 4
    with contextlib.ExitStack() as st0:
        A0 = Alloc(nc, st0)
        W = A0.sb([128, 8, ATT_IN], BF16, "Wai")
        load_weight(C, W, C.att_w_in, ATT_IN, 8)
        with contextlib.ExitStack() as st:
            A = Alloc(nc, st)
            P = Phase(C.K, "l1a")
            ident, mhalf = load_consts(P, A, C)
            a_pp, b_pp, _G = load_mod(P, A, C, 1, 1, want_gate=False)
            gains = A.sb([128, 10, 128], F32, "gains")
            for hh in range(10):
                src = C.att_q_gain if hh < 8 else C.att_k_gain
                P.dma(gains[:, hh, :], src.partition_broadcast(128))
            xt = [A.sb([128, D], F32, "xt") for _ in range(3)]
            cs = [A.sb([128, 2, 2, 32], F32, "cs") for _ in range(4)]
            xn = [A.sb([128, D], BF16, "xn") for _ in range(2)]
            junk = A.sb([128, D], BF16, "junk")
            ss = [A.sb([128, 4], F32, "ss") for _ in range(4)]
            hT = [A.sb([128, 8, 128], BF16, "hT") for _ in range(2)]
            qk32_ = [A.sb([128, 10, 128], F32, "qk32") for _ in range(2)]
            sq_ = [A.sb([128, 10, 128], F32, "sq") for _ in range(2)]
            s10 = [A.sb([128, 32], F32, "s10") for _ in range(2)]
            qkn_ = [A.sb([128, 10, 128], F32, "qkn") for _ in range(2)]
            t1_ = [A.sb([128, 10, 2, 32], F32, "t1") for _ in range(2)]
            t2_ = [A.sb([128, 10, 2, 32], F32, "t2") for _ in range(2)]
            t3_ = [A.sb([128, 10, 2, 32], F32, "t3") for _ in range(2)]
            t4_ = [A.sb([128, 10, 2, 32], F32, "t4") for _ in range(2)]
            qkr = [A.sb([128, 10 * 128], BF16, "qkr") for _ in range(2)]
            vb = [A.sb([128, 256], BF16, "vb") for _ in range(2)]
            acc = [A.sb([128, 10, GC * 128], BF16, "acc") for _ in range(2)]
            pT = [A.ps([128, 8, 128], BF16, "pT") for _ in range(2)]
            pp = [A.ps([128, 512], F32, "pp") for _ in range(3)]
            pq = A.ps([128, 16, 128], BF16, "pq")
            items = [(s, c) for s in range(NS) for c in range(NT)]

            def do_load(i):
                s, c = items[i]
                P.dma(xt[i % 3], Xin[s, c * 128:(c + 1) * 128, :])
                P.dma(cs[i % 4], C.rope_a[c])

            def do_A1(i):
                front1(P, xt[i % 3], ss[i % 4], junk, xn[i % 2], mhalf, xn_eng="act")

            def do_A2(i):
                s, c = items[i]
                front2(P, xn[i % 2], ident, pT[i % 2], hT[i % 2], a_pp[s], b_pp[s], ev_eng="act")

            def do_B(i):
                s, c = items[i]
                sl = i % 2
                g = (i // GC) % 2
                cc = c % GC
                hv = hT[sl]
                qk32, sq, qkn, t1, t2, t3, t4 = qk32_[sl], sq_[sl], qkn_[sl], t1_[sl], t2_[sl], t3_[sl], t4_[sl]
                qkf = qk32.re("p h d -> p (h d)")
                for ct in range(3):
                    pv = pp[ct]
                    for kc in range(8):
                        P.mm(pv, hv[:, kc, :], W[:, kc, ct * 512:(ct + 1) * 512], start=(kc == 0), stop=(kc == 7))
                    if ct < 2:
                        P.copy(qkf[:, ct * 512:(ct + 1) * 512], pv, eng="act")
                    else:
                        P.copy(qkf[:, 1024:1280], pv[:, 0:256], eng="act")
                        P.copy(vb[sl], pv[:, 256:512], eng="act")
                P.dma(C.VA[s, c], vb[sl])

            def do_C1(i):
                sl = i % 2
                qk32, sq = qk32_[sl], sq_[sl]
                P.act(sq, qk32, AF.Square)
                sv = s10[sl]
                P.reduce(sv[:, 0:10], sq, ALU.add)
                P.ts(sv[:, 10:20], sv[:, 0:10], 1.0 / ATT_HD, EPS, ALU.mult, ALU.add)
                P.tt(sv[:, 20:30], sv[:, 10:20], V(mhalf.ap.broadcast_to([128, 10]), mhalf.key), ALU.pow, eng="pool")

            def do_C(i):
                s, c = items[i]
                sl = i % 2
                qk32, sq, qkn, t1, t2, t3, t4 = qk32_[sl], sq_[sl], qkn_[sl], t1_[sl], t2_[sl], t3_[sl], t4_[sl]
                sv = s10[sl]
                P.tt(qkn, qk32, V(sv.ap[:, 20:30].unsqueeze(2).broadcast_to([128, 10, 128]), sv.key), ALU.mult, eng="dve")
                P.tt(qkn, qkn, gains, ALU.mult, eng="dve")
                v5 = qkn.re("p h (f a d) -> p h f a d", f=2, a=2)
                o5 = qkr[sl].re("p (h f a d) -> p h f a d", h=10, f=2, a=2)
                a_, b_ = v5[:, :, :, 0, :], v5[:, :, :, 1, :]
                csv = cs[i % 4]
                cos = V(csv.ap[:, 0].unsqueeze(1).broadcast_to([128, 10, 2, 32]), csv.key)
                sin = V(csv.ap[:, 1].unsqueeze(1).broadcast_to([128, 10, 2, 32]), csv.key)
                P.tt(t1, a_, cos, ALU.mult, eng="dve")
                P.tt(t2, b_, sin, ALU.mult, eng="pool")
                P.tt(t3, a_, sin, ALU.mult, eng="pool")
                P.tt(t4, b_, cos, ALU.mult, eng="dve")
                P.tt(o5[:, :, :, 0, :], t1, t2, ALU.subtract, eng="dve")
                P.tt(o5[:, :, :, 1, :], t3, t4, ALU.add, eng="pool")

            def do_D(i):
                s, c = items[i]
                sl = i % 2
                g = (i // GC) % 2
                cc = c % GC
                for j in range(10):
                    P.tr(pq[:, j, :], qkr[sl][:, j * 128:(j + 1) * 128], ident)
                P.copy(acc[g][:, :, cc * 128:(cc + 1) * 128], pq[:, 0:10, :], eng="act")
                if cc == GC - 1:
                    c0 = c - (GC - 1)
                    P.dma(C.QTA[s, :, :, c0 * 128:(c0 + GC) * 128], acc[g][:, 0:8, :])
                    P.dma(C.KTA[s, :, :, c0 * 128:(c0 + GC) * 128], acc[g][:, 8:10, :])

            n = len(items)
            do_load(0)
            if n > 1:
                do_load(1)
            do_A1(0)
            do_A2(0)
            for i in range(n + 2):
                if i + 2 < n:
                    do_load(i + 2)
                if i + 1 < n:
                    do_A1(i + 1)
                if 0 <= i - 1 < n:
                    do_C(i - 1)
                if 0 <= i - 2 < n:
                    do_D(i - 2)
                if i + 1 < n:
                    do_A2(i + 1)
                if i < n:
                    do_B(i)
                    do_C1(i)
            P.emit()


def phase_l1b(C, Xin, Xout):
    import contextlib
    nc = C.nc
    NS, NT, N = C.NSEQ, C.NT, C.N
    QB = 512
    NQB = N // QB
    NP = NT // 2
    with contextlib.ExitStack() as st0:
        A0 = Alloc(nc, st0)
        W = A0.sb([128, 8, D], BF16, "Wao")
        load_weight(C, W, C.att_w_out, D, 8)
        with contextlib.ExitStack() as st:
            A = Alloc(nc, st)
            P = Phase(C.K, "l1b")
            _a, _b, G = load_mod(P, A, C, 1, 1, want_pp=False)
            gq = A.sb([128, 128], F32, "gq")
            gk = A.sb([128, 128], F32, "gk")
            mm_ = A.sb([128, 4], F32, "mm")
            P.dma(gq, C.att_q_gain.partition_broadcast(128))
            P.dma(gk, C.att_k_gain.partition_broadcast(128))
            P.reduce(mm_[:, 0:1], gq, ALU.max, absval=True)
            P.reduce(mm_[:, 1:2], gk, ALU.max, absval=True)
            P.tt(mm_[:, 2:3], mm_[:, 0:1], mm_[:, 1:2], ALU.mult)
            P.ts(mm_[:, 3:4], mm_[:, 2:3], -math.sqrt(ATT_HD), None, ALU.mult)
            negm = mm_[:, 3:4]
            ones32 = A.sb([128, 128], F32, "ones32")
            P.memset(ones32, 1.0)
            onesb = A.sb([128, 128], BF16, "onesb")
            P.memset(onesb, 1.0)
            pe_us = [u for u in range(NP) if u % 4 == 3] if NP >= 8 else []
            kt = [A.sb([128, 2, N], BF16, "kt") for _ in range(2)]
            va = [A.sb([128, NT, 256], BF16, "va") for _ in range(2)]
            qt = [A.sb([128, 8, QB], BF16, "qt") for _ in range(2)]
            NPT = 4
            PT = [A.sb([128, 2, QB], BF16, "PT") for _ in range(NPT)]
            acc = [A.sb([128, 2, QB], F32, "acc") for _ in range(2)]
            accp = [A.sb([128, 2, QB], F32, "accp") for _ in range(2)]
            POOL_SHARE = False
            pool_us = [u for u in range(NP) if u % 4 == 3] if (NP >= 8 and POOL_SHARE) else []
            OT = [A.sb([128, 8, QB], BF16, "OT") for _ in range(2)]
            rinv = [A.sb([128, QB], F32, "rinv") for _ in range(2)]
            xt = [A.sb([128, D], F32, "xt") for _ in range(8)]
            tmp = [A.sb([128, 512], F32, "tmp") for _ in range(2)]
            ps_s = [A.ps([128, 2, 512], F32, "ps_s") for _ in range(2)]
            po = [A.ps([128, 512], F32, "po") for _ in range(2)]
            prs = A.ps([128, 512], F32, "prs")
            pw = A.ps([128, 512], F32, "pw")
            scale = float(ATT_HD) ** -0.5
            blocks = [(s, qb) for s in range(NS) for qb in range(NQB)]
            units = [(bi, h, u) for bi in range(len(blocks)) for h in range(ATT_QH) for u in range(NP)]

            def load_seq(s):
                P.dma(kt[s % 2], C.KTA[s])
                P.dma(va[s % 2], C.VA[s].rearrange("c p d -> p c d"))

            def load_block(bi):
                s, qb = blocks[bi]
                sl = bi % 2
                P.dma(qt[sl], C.QTA[s, :, :, qb * QB:(qb + 1) * QB])
                for j in range(4):
                    c = qb * 4 + j
                    P.dma(xt[sl * 4 + j], Xin[s, c * 128:(c + 1) * 128, :])

            def qk(i):
                bi, h, u = units[i]
                s, qb = blocks[bi]
                for t in range(2):
                    kc = 2 * u + t
                    P.mm(ps_s[i % 2][:, t, :], kt[s % 2][:, h // 4, kc * 128:(kc + 1) * 128], qt[bi % 2][:, h, :])

            deferred = []
            hfq = []

            def head_final(bi, h):
                def f():
                    a = acc[h % 2]
                    if pool_us:
                        P.tt(a, a, accp[h % 2], ALU.add)
                    for t in range(2):
                        P.mm(prs, ones32, a[:, t, :], start=(t == 0 and not pe_us), stop=(t == 1))
                    P.recip(rinv[h % 2], prs)
                    P.tt(OT[bi % 2][:, h, :], po[h % 2], rinv[h % 2], ALU.mult)
                return f

            def out_group(bi, j, half):
                def f():
                    s, qb = blocks[bi]
                    c = qb * 4 + j
                    xv = xt[(bi % 2) * 4 + j]
                    for h in range(8):
                        P.mm(pw, OT[bi % 2][:, h, j * 128:(j + 1) * 128], W[:, h, half * 512:(half + 1) * 512],
                             start=(h == 0), stop=(h == 7))
                    P.tt(tmp[half], pw, G[s][:, half * 512:(half + 1) * 512], ALU.mult)
                    P.tt(xv[:, half * 512:(half + 1) * 512], tmp[half], xv[:, half * 512:(half + 1) * 512],
                         ALU.add, eng="pool")
                    if half == 1:
                        P.dma(Xout[s, c * 128:(c + 1) * 128, :], xv)
                return f

            def pv_(i):
                bi, h, u = units[i]
                s, qb = blocks[bi]
                pt = PT[i % NPT]
                import os
                if os.environ.get("DBG_EXP1"):
                    for t in range(2):
                        P.act(pt[:, t, :], ps_s[i % 2][:, t, :], AF.Exp, bias=negm, scale=scale)
                else:
                    P.act(pt, ps_s[i % 2], AF.Exp, bias=negm, scale=scale)
                for t in range(2):
                    kc = 2 * u + t
                    P.mm(po[h % 2], va[s % 2][:, kc, (h // 4) * 128:(h // 4 + 1) * 128], pt[:, t, :],
                         start=(kc == 0), stop=(kc == NT - 1))
                if u in pe_us:
                    for t in range(2):
                        P.mm(prs, onesb, pt[:, t, :], start=(u == pe_us[0] and t == 0), stop=False)
                elif u in pool_us:
                    if u == pool_us[0]:
                        P.copy(accp[h % 2], pt, eng="pool")
                    else:
                        P.tt(accp[h % 2], accp[h % 2], pt, ALU.add, eng="pool")
                elif u == 0:
                    P.copy(acc[h % 2], pt, eng="dve")
                else:
                    P.tt(acc[h % 2], acc[h % 2], pt, ALU.add)
                if u == NP - 1:
                    hfq.append(head_final(bi, h))
                    if h == ATT_QH - 1:
                        for j in range(4):
                            for half in range(2):
                                deferred.append(out_group(bi, j, half))

            nu = len(units)
            load_seq(0)
            load_block(0)
            loaded = 0
            qk(0)
            if nu > 1:
                qk(1)
            for i in range(nu):
                bi, h, u = units[i]
                if h == 0 and u == 0:
                    s, qb = blocks[bi]
                    if qb == 0 and s + 1 < NS:
                        load_seq(s + 1)
                pv_(i)
                if i + 2 < nu:
                    nb = units[i + 2][0]
                    if nb > loaded:
                        while hfq:
                            hfq.pop(0)()
                        while deferred:
                            deferred.pop(0)()
                        load_block(nb)
                        loaded = nb
                    qk(i + 2)
                if hfq and u == 0:
                    hfq.pop(0)()
                elif deferred and (u % 2 == 1) and not hfq:
                    deferred.pop(0)()
                if not deferred and loaded < bi + 1 and bi + 1 < len(blocks) and not (h == ATT_QH - 1 and u == NP - 1):
                    load_block(bi + 1)
                    loaded = bi + 1
            while hfq:
                hfq.pop(0)()
            while deferred:
                deferred.pop(0)()
            P.emit()


def rope_table(n_tokens, nf):
    t = np.arange(n_tokens)
    row = (t // GRID_W).astype(np.float32)
    col = (t % GRID_W).astype(np.float32)
    inv = (np.float32(ROPE_THETA) ** (-(np.arange(nf, dtype=np.float32)) / np.float32(nf))).astype(np.float32)
    ar = (row[:, None] * inv[None, :]).astype(np.float32)
    ac = (col[:, None] * inv[None, :]).astype(np.float32)
    tab = np.stack([np.stack([np.cos(ar), np.cos(ac)], 1), np.stack([np.sin(ar), np.sin(ac)], 1)], 1)
    return np.ascontiguousarray(tab.reshape(n_tokens // 128, 128, 2, 2, nf).astype(np.float32))


def build(NSEQ, N, phases=None):
    nc = bass.Bass("TRN2", target_bir_lowering=False)
    C = Ctx()
    C.nc = nc
    C.NSEQ = NSEQ
    C.N = N
    C.NT = N // 128
    NT = C.NT

    def inp(name, shape, dt=F32):
        return nc.dram_tensor(name, list(shape), dt, kind="ExternalInput").ap()

    C.x = inp("x", [NSEQ, N, D])
    C.cT = inp("cT", [128, 8, NSEQ])
    C.mod_w = inp("mod_w", [2, D, 6 * D])
    C.mod_b = inp("mod_b", [2, 6 * D])
    C.norm1_g = inp("norm1_g", [2, D])
    C.norm2_g = inp("norm2_g", [2, D])
    C.ret_w_in = inp("ret_w_in", [D, RET_IN])
    C.ret_decay = inp("ret_decay", [8])
    C.ret_w_out = inp("ret_w_out", [2 * D, D])
    C.att_w_in = inp("att_w_in", [D, ATT_IN])
    C.att_q_gain = inp("att_q_gain", [ATT_HD])
    C.att_k_gain = inp("att_k_gain", [ATT_HD])
    C.att_w_out = inp("att_w_out", [D, D])
    C.mlp_w1 = inp("mlp_w1", [2, D, DFF])
    C.mlp_w2 = inp("mlp_w2", [2, DFF, D])
    C.final_g = inp("final_g", [D])
    C.ident = inp("ident", [128, 128])
    C.rope_r = inp("rope_r", [NT, 128, 2, 2, 64])
    C.rope_a = inp("rope_a", [NT, 128, 2, 2, 32])
    C.y = nc.dram_tensor("y", [NSEQ, N, D], F32, kind="ExternalOutput").ap()

    def scr(name, shape, dt):
        return nc.dram_tensor(name, list(shape), dt).ap()

    C.MOD = scr("MOD", [2, NSEQ, 6 * D], F32)
    C.XA = scr("XA", [NSEQ, N, D], F32)
    C.XB = scr("XB", [NSEQ, N, D], F32)
    C.QT = scr("QT", [NSEQ, NT, 3, 128, 4, 2, 128], BF16)
    C.KT = scr("KT", [NSEQ, NT, 128, 4, 2, 128], BF16)
    C.KF = scr("KF", [NSEQ, NT, 128, 4, 256], BF16)
    C.KB = scr("KB", [NSEQ, NT, 128, 4, 256], BF16)
    C.VS = scr("VS", [NSEQ, NT, 128, 4, 512], BF16)
    C.SG = scr("SG", [NSEQ, NT, 128, 4, 512], BF16)
    C.YT = scr("YT", [NSEQ, 128, 16, N], BF16)
    C.QTA = scr("QTA", [NSEQ, 128, 8, N], BF16)
    C.KTA = scr("KTA", [NSEQ, 128, 2, N], BF16)
    C.VA = scr("VA", [NSEQ, NT, 128, 256], BF16)
    C.K = Kern(nc)
    if phases is None:
        phases = ["mod", "l0a", "l0b", "l0c", "mlp0", "l1a", "l1b", "mlp1"]
    C.phases = phases
    for ph in phases:
        if ph == "mod":
            phase_mod(C)
        elif ph == "l0a":
            phase_l0a(C)
        elif ph == "l0b":
            phase_l0b(C)
        elif ph == "l0c":
            phase_l0c(C, C.x, C.XA)
        elif ph == "mlp0":
            phase_mlp(C, 0, C.XA, C.XB, False)
        elif ph == "l1a":
            phase_l1a(C, C.XB)
        elif ph == "l1b":
            phase_l1b(C, C.XB, C.XA)
        elif ph == "mlp1":
            phase_mlp(C, 1, C.XA, C.y, True)
        elif ph == "l1a_x":
            phase_l1a(C, C.x)
        elif ph == "l1b_xy":
            phase_l1b(C, C.x, C.y)
        elif ph == "l0c_y":
            phase_l0c(C, C.x, C.y)
        elif ph == "mlp_only":
            phase_mlp(C, 0, C.x, C.y, True)
        else:
            raise ValueError(ph)
    return nc, C


def make_in_maps(inputs, NSEQ_P=2, NSEQ_S=1, ncores=8):
    f = lambda a: np.ascontiguousarray(np.asarray(a, dtype=np.float32))
    xp, xs_ = f(inputs["x_prompt"]), f(inputs["x_sample"])
    cp, cs = f(inputs["c_prompt"]), f(inputs["c_sample"])
    N = xp.shape[1]
    shared = {
        "mod_w": f(inputs["mod_w"]), "mod_b": f(inputs["mod_b"]),
        "norm1_g": f(inputs["norm1_g"]), "norm2_g": f(inputs["norm2_g"]),
        "ret_w_in": f(inputs["ret_w_in"])[0], "ret_decay": f(inputs["ret_decay"])[0].reshape(8),
        "ret_w_out": f(inputs["ret_w_out"])[0], "att_w_in": f(inputs["att_w_in"])[0],
        "att_q_gain": f(inputs["att_q_gain"])[0], "att_k_gain": f(inputs["att_k_gain"])[0],
        "att_w_out": f(inputs["att_w_out"])[0], "mlp_w1": f(inputs["mlp_w1"]), "mlp_w2": f(inputs["mlp_w2"]),
        "final_g": f(inputs["final_g"]),
        "ident": np.eye(128, dtype=np.float32),
        "rope_r": rope_table(N, 64), "rope_a": rope_table(N, 32),
    }
    maps = []
    for i in range(ncores):
        x = np.concatenate([xp[NSEQ_P * i:NSEQ_P * (i + 1)], xs_[NSEQ_S * i:NSEQ_S * (i + 1)]], 0)
        c = np.concatenate([cp[NSEQ_P * i:NSEQ_P * (i + 1)], cs[NSEQ_S * i:NSEQ_S * (i + 1)]], 0)
        cT = np.ascontiguousarray(c.reshape(c.shape[0], 8, 128).transpose(2, 1, 0))
        m = dict(shared)
        m["x"] = np.ascontiguousarray(x)
        m["cT"] = cT
        maps.append(m)
    return maps


def kernel(**inputs):
    maps = make_in_maps(inputs)
    N = maps[0]["x"].shape[1]
    nc, C = build(3, N)
    res = run_bass_kernel_spmd(nc, maps, core_ids=list(range(8)))
    ys = [np.asarray(r["y"]) for r in res.results]
    y_prompt = np.concatenate([y[0:2] for y in ys], 0).astype(np.float32)
    y_sample = np.concatenate([y[2:3] for y in ys], 0).astype(np.float32)
    return (y_prompt, y_sample)
```
